# Optimizing a Trainium2 kernel written in Bass

```python
import numpy as np
import jax
import jax.numpy as jnp
from jax import lax

D_MODEL = 2048
BATCH = 4
SEQ = 4096
DEPTH = 1

NSA_HEADS = 16
NSA_KV_HEADS = 4
NSA_GROUP = NSA_HEADS // NSA_KV_HEADS
NSA_HEAD_DIM = 64
NSA_Q_WIDTH = NSA_HEADS * NSA_HEAD_DIM
NSA_KV_WIDTH = NSA_KV_HEADS * NSA_HEAD_DIM
CMP_BLOCK = 32
CMP_STRIDE = 16
CMP_HIDDEN = 256
SLC_BLOCK = 64
SLC_TOPN = 16
WINDOW = 512
Q_BLOCK = 64

M_HEADS = 4
M_HEAD_DIM = 256
M_WIDTH = M_HEADS * M_HEAD_DIM
M_CHUNK = 64
CONV_WIDTH = 4

D_FF = 5632

EPS = 1e-6
NEG_INF = -1e30
FORCE_SCORE = 1e4

SPLIT_SIZES = (NSA_Q_WIDTH, 6 * NSA_KV_WIDTH, 3 * NSA_HEADS, 3 * M_WIDTH, M_HEADS, M_HEADS, M_WIDTH, 2 * D_MODEL)
IN_WIDTH = NSA_Q_WIDTH + 6 * NSA_KV_WIDTH + 3 * NSA_HEADS + 3 * M_WIDTH + 2 * M_HEADS + M_WIDTH + 2 * D_MODEL
F_GATE_START = NSA_Q_WIDTH + 6 * NSA_KV_WIDTH + 3 * NSA_HEADS + 3 * M_WIDTH + M_HEADS

kernel_name = "hybrid_nsa_mlstm_macaron_block"


def rms_norm(x, gain):
    x32 = x.astype(jnp.float32)
    y = x32 * lax.rsqrt(jnp.mean(x32 * x32, axis=-1, keepdims=True) + EPS)
    return (y * gain.astype(jnp.float32)).astype(x.dtype)


def swiglu(x, w_gate, w_up, w_down):
    return (jax.nn.silu(x @ w_gate) * (x @ w_up)) @ w_down


def masked_softmax(s, mask):
    s = jnp.where(mask, s.astype(jnp.float32), NEG_INF)
    return jax.nn.softmax(s, axis=-1) * mask


def split_columns(p):
    out, start = [], 0
    for size in SPLIT_SIZES:
        out.append(p[..., start:start + size])
        start += size
    return out


def causal_depthwise_conv(u, w, bias):
    c = u.shape[-1]
    out = lax.conv_general_dilated(u, w[:, None, :].astype(u.dtype), window_strides=(1,),
                                   padding=[(CONV_WIDTH - 1, 0)],
                                   dimension_numbers=("NWC", "WIO", "NWC"),
                                   feature_group_count=c)
    return out + bias


def compress_blocks(kv, pos_emb, w1, w2):
    b, g, s, d = kv.shape
    n_cmp = (s - CMP_BLOCK) // CMP_STRIDE + 1
    idx = np.arange(n_cmp)[:, None] * CMP_STRIDE + np.arange(CMP_BLOCK)[None, :]
    blocks = kv[:, :, idx, :] + pos_emb
    flat = blocks.reshape(b, g, n_cmp, CMP_BLOCK * d)
    return jax.nn.gelu(flat @ w1) @ w2


def nsa_attention(q, k_cmp, v_cmp, k_slc, v_slc, k_win, v_win, gates,
                  q_gain, kc_gain, ks_gain, kw_gain,
                  cmp_pos_k, cmp_w1_k, cmp_w2_k, cmp_pos_v, cmp_w1_v, cmp_w2_v):
    b, g, r, s, d = q.shape
    q = rms_norm(q, q_gain) * (d ** -0.5)
    t = np.arange(s)

    kc = rms_norm(compress_blocks(k_cmp, cmp_pos_k, cmp_w1_k, cmp_w2_k), kc_gain)
    vc = compress_blocks(v_cmp, cmp_pos_v, cmp_w1_v, cmp_w2_v)
    n_cmp = kc.shape[2]
    cmp_start = np.arange(n_cmp) * CMP_STRIDE
    cmp_mask = (cmp_start + CMP_BLOCK - 1)[None, :] <= t[:, None]
    p_cmp = masked_softmax(jnp.einsum("bgrsd,bgcd->bgrsc", q, kc), cmp_mask)
    o_cmp = jnp.einsum("bgrsc,bgcd->bgrsd", p_cmp.astype(vc.dtype), vc)

    n_slc = s // SLC_BLOCK
    blk = np.arange(n_slc)
    blk_start = blk * SLC_BLOCK
    overlap = ((cmp_start[:, None] < (blk_start + SLC_BLOCK)[None, :]) &
               ((cmp_start + CMP_BLOCK)[:, None] > blk_start[None, :])).astype(np.float32)
    imp = jnp.einsum("bgrsc,cn->bgsn", p_cmp, overlap)
    cur = t // SLC_BLOCK
    forced = (blk[None, :] == 0) | (blk[None, :] == cur[:, None]) | (blk[None, :] == cur[:, None] - 1)
    causal_blk = blk_start[None, :] <= t[:, None]
    imp = jnp.where(forced, FORCE_SCORE, jnp.where(causal_blk, imp, -1.0))
    n_sel = min(SLC_TOPN, n_slc)
    _, sel = lax.top_k(imp, n_sel)

    kb = rms_norm(k_slc, ks_gain).reshape(b, g, n_slc, SLC_BLOCK, d)
    vb = v_slc.reshape(b, g, n_slc, SLC_BLOCK, d)
    pad = ((0, 0), (0, 0), (WINDOW, 0), (0, 0))
    kw = jnp.pad(rms_norm(k_win, kw_gain), pad)
    vw = jnp.pad(v_win, pad)
    bi = jnp.arange(b)[:, None, None, None]
    gi = jnp.arange(g)[None, :, None, None]

    def query_block(i):
        s0 = i * Q_BLOCK
        tq = s0 + jnp.arange(Q_BLOCK)
        qi = lax.dynamic_slice_in_dim(q, s0, Q_BLOCK, axis=3)
        idx = lax.dynamic_slice_in_dim(sel, s0, Q_BLOCK, axis=2)
        ks = kb[bi, gi, idx]
        vs = vb[bi, gi, idx]
        kpos = idx[..., None] * SLC_BLOCK + jnp.arange(SLC_BLOCK)
        smask = (kpos <= tq[:, None, None]).reshape(b, g, 1, Q_BLOCK, n_sel * SLC_BLOCK)
        ss = jnp.einsum("bgrqd,bgqnld->bgrqnl", qi, ks).reshape(b, g, r, Q_BLOCK, n_sel * SLC_BLOCK)
        ps = masked_softmax(ss, smask).reshape(b, g, r, Q_BLOCK, n_sel, SLC_BLOCK)
        o_s = jnp.einsum("bgrqnl,bgqnld->bgrqd", ps.astype(vs.dtype), vs)
        kwi = lax.dynamic_slice_in_dim(kw, s0, Q_BLOCK + WINDOW, axis=2)
        vwi = lax.dynamic_slice_in_dim(vw, s0, Q_BLOCK + WINDOW, axis=2)
        wpos = s0 - WINDOW + jnp.arange(Q_BLOCK + WINDOW)
        wmask = ((wpos[None, :] <= tq[:, None]) & (wpos[None, :] > tq[:, None] - WINDOW)
                 & (wpos[None, :] >= 0))
        pw = masked_softmax(jnp.einsum("bgrqd,bgkd->bgrqk", qi, kwi), wmask)
        o_w = jnp.einsum("bgrqk,bgkd->bgrqd", pw.astype(vwi.dtype), vwi)
        return o_s, o_w

    o_slc, o_win = lax.map(query_block, jnp.arange(s // Q_BLOCK))
    o_slc = jnp.moveaxis(o_slc, 0, 3).reshape(b, g, r, s, d)
    o_win = jnp.moveaxis(o_win, 0, 3).reshape(b, g, r, s, d)

    gt = jax.nn.sigmoid(gates.astype(jnp.float32)).astype(q.dtype)
    return gt[..., 0:1] * o_cmp + gt[..., 1:2] * o_slc + gt[..., 2:3] * o_win


def mlstm_chunkwise(q, k, v, i_pre, f_pre):
    b, h, s, d = q.shape
    L = M_CHUNK
    nc = s // L
    q = q.reshape(b, h, nc, L, d)
    k = k.reshape(b, h, nc, L, d) * (d ** -0.5)
    v = v.reshape(b, h, nc, L, d)
    log_f = jax.nn.log_sigmoid(f_pre).reshape(b, h, nc, L)
    log_i = i_pre.reshape(b, h, nc, L)
    a = jnp.cumsum(log_f, axis=-1)
    g = a[..., -1]

    causal = np.tril(np.ones((L, L), dtype=bool))
    log_w = jnp.where(causal, a[..., :, None] - a[..., None, :] + log_i[..., None, :], NEG_INF)
    m_intra = jnp.max(log_w, axis=-1)
    w_intra = jnp.exp(log_w - m_intra[..., None]) * jnp.einsum("bhcjd,bhcsd->bhcjs", q, k)
    num_intra = jnp.einsum("bhcjs,bhcsd->bhcjd", w_intra, v)
    den_intra = jnp.sum(w_intra, axis=-1)

    log_u = g[..., None] - a + log_i

    def step(carry, xs):
        c_st, n_st, m_st = carry
        q_c, k_c, v_c, g_c, lu_c = xs
        num_inter = jnp.einsum("bhjd,bhde->bhje", q_c, c_st)
        den_inter = jnp.einsum("bhjd,bhd->bhj", q_c, n_st)
        m_new = jnp.maximum(g_c + m_st, jnp.max(lu_c, axis=-1))
        decay = jnp.exp(g_c + m_st - m_new)
        u = jnp.exp(lu_c - m_new[..., None])
        uk = u[..., None] * k_c
        c_new = decay[..., None, None] * c_st + jnp.einsum("bhsd,bhse->bhde", uk, v_c)
        n_new = decay[..., None] * n_st + jnp.sum(uk, axis=2)
        return (c_new, n_new, m_new), (num_inter, den_inter, m_st)

    init = (jnp.zeros((b, h, d, d), jnp.float32), jnp.zeros((b, h, d), jnp.float32),
            jnp.zeros((b, h), jnp.float32))
    xs = (jnp.moveaxis(q, 2, 0), jnp.moveaxis(k, 2, 0), jnp.moveaxis(v, 2, 0),
          jnp.moveaxis(g, 2, 0), jnp.moveaxis(log_u, 2, 0))
    _, (num_inter, den_inter, m_prev) = lax.scan(step, init, xs)
    num_inter = jnp.moveaxis(num_inter, 0, 2)
    den_inter = jnp.moveaxis(den_inter, 0, 2)
    m_prev = jnp.moveaxis(m_prev, 0, 2)

    log_inter = a + m_prev[..., None]
    m_comb = jnp.maximum(log_inter, m_intra)
    s_inter = jnp.exp(log_inter - m_comb)
    s_intra = jnp.exp(m_intra - m_comb)
    num = s_inter[..., None] * num_inter + s_intra[..., None] * num_intra
    den = s_inter * den_inter + s_intra * den_intra
    hcell = num / jnp.maximum(jnp.abs(den), jnp.exp(-m_comb))[..., None]
    return hcell.reshape(b, h, s, d)


def setup_inputs(seed: int = 0) -> dict:
    key = jax.random.key(seed)
    ks = jax.random.split(key, 40)
    L = DEPTH

    def nrm(k, shape, scale):
        return jax.random.normal(k, shape, jnp.float32) * scale

    def gain(k, shape):
        return 1.0 + 0.02 * jax.random.normal(k, shape, jnp.float32)

    b_in = nrm(ks[6], (L, IN_WIDTH), 0.02)
    b_in = b_in.at[:, F_GATE_START:F_GATE_START + M_HEADS].add(jnp.linspace(3.0, 6.0, M_HEADS))
    cmp_in = CMP_BLOCK * NSA_HEAD_DIM
    return {
        "x": nrm(ks[0], (BATCH, SEQ, D_MODEL), 1.0),
        "ffn1_norm": gain(ks[1], (L, D_MODEL)),
        "ffn1_w_gate": nrm(ks[2], (L, D_MODEL, D_FF), D_MODEL ** -0.5),
        "ffn1_w_up": nrm(ks[3], (L, D_MODEL, D_FF), D_MODEL ** -0.5),
        "ffn1_w_down": nrm(ks[4], (L, D_FF, D_MODEL), D_FF ** -0.5),
        "mix_norm": gain(ks[5], (L, D_MODEL)),
        "w_in": nrm(ks[7], (L, D_MODEL, IN_WIDTH), D_MODEL ** -0.5),
        "b_in": b_in,
        "nsa_q_gain": gain(ks[8], (L, NSA_HEAD_DIM)),
        "nsa_kc_gain": gain(ks[9], (L, NSA_HEAD_DIM)),
        "nsa_ks_gain": gain(ks[10], (L, NSA_HEAD_DIM)),
        "nsa_kw_gain": gain(ks[11], (L, NSA_HEAD_DIM)),
        "cmp_pos_k": nrm(ks[12], (L, CMP_BLOCK, NSA_HEAD_DIM), 0.1),
        "cmp_w1_k": nrm(ks[13], (L, cmp_in, CMP_HIDDEN), cmp_in ** -0.5),
        "cmp_w2_k": nrm(ks[14], (L, CMP_HIDDEN, NSA_HEAD_DIM), CMP_HIDDEN ** -0.5),
        "cmp_pos_v": nrm(ks[15], (L, CMP_BLOCK, NSA_HEAD_DIM), 0.1),
        "cmp_w1_v": nrm(ks[16], (L, cmp_in, CMP_HIDDEN), cmp_in ** -0.5),
        "cmp_w2_v": nrm(ks[17], (L, CMP_HIDDEN, NSA_HEAD_DIM), CMP_HIDDEN ** -0.5),
        "m_conv_w": nrm(ks[18], (L, CONV_WIDTH, 2 * M_WIDTH), CONV_WIDTH ** -0.5),
        "m_conv_b": nrm(ks[19], (L, 2 * M_WIDTH), 0.02),
        "m_out_gain": gain(ks[20], (L, M_HEADS, M_HEAD_DIM)),
        "w_branch_nsa": nrm(ks[21], (L, NSA_Q_WIDTH, D_MODEL), NSA_Q_WIDTH ** -0.5),
        "w_branch_mlstm": nrm(ks[22], (L, M_WIDTH, D_MODEL), M_WIDTH ** -0.5),
        "w_out": nrm(ks[23], (L, D_MODEL, D_MODEL), D_MODEL ** -0.5),
        "ffn2_norm": gain(ks[24], (L, D_MODEL)),
        "ffn2_w_gate": nrm(ks[25], (L, D_MODEL, D_FF), D_MODEL ** -0.5),
        "ffn2_w_up": nrm(ks[26], (L, D_MODEL, D_FF), D_MODEL ** -0.5),
        "ffn2_w_down": nrm(ks[27], (L, D_FF, D_MODEL), D_FF ** -0.5),
    }


def reference(x, ffn1_norm, ffn1_w_gate, ffn1_w_up, ffn1_w_down, mix_norm, w_in, b_in,
              nsa_q_gain, nsa_kc_gain, nsa_ks_gain, nsa_kw_gain,
              cmp_pos_k, cmp_w1_k, cmp_w2_k, cmp_pos_v, cmp_w1_v, cmp_w2_v,
              m_conv_w, m_conv_b, m_out_gain, w_branch_nsa, w_branch_mlstm, w_out,
              ffn2_norm, ffn2_w_gate, ffn2_w_up, ffn2_w_down):
    b, s, _ = x.shape
    G, R, dh = NSA_KV_HEADS, NSA_GROUP, NSA_HEAD_DIM
    for l in range(DEPTH):
        x = x + 0.5 * swiglu(rms_norm(x, ffn1_norm[l]), ffn1_w_gate[l], ffn1_w_up[l], ffn1_w_down[l])

        hn = rms_norm(x, mix_norm[l])
        proj = hn @ w_in[l] + b_in[l]
        p_q, p_kv, p_g, p_mqkv, p_mi, p_mf, p_mo, p_merge = split_columns(proj)

        q_n = p_q.reshape(b, s, G, R, dh).transpose(0, 2, 3, 1, 4)
        kv_n = p_kv.reshape(b, s, 6, G, dh).transpose(2, 0, 3, 1, 4)
        g_n = p_g.reshape(b, s, G, R, 3).transpose(0, 2, 3, 1, 4)
        o_nsa = nsa_attention(q_n, kv_n[0], kv_n[1], kv_n[2], kv_n[3], kv_n[4], kv_n[5], g_n,
                              nsa_q_gain[l], nsa_kc_gain[l], nsa_ks_gain[l], nsa_kw_gain[l],
                              cmp_pos_k[l], cmp_w1_k[l], cmp_w2_k[l],
                              cmp_pos_v[l], cmp_w1_v[l], cmp_w2_v[l])
        o_nsa = o_nsa.transpose(0, 3, 1, 2, 4).reshape(b, s, NSA_Q_WIDTH)

        qk_m = jax.nn.silu(causal_depthwise_conv(p_mqkv[..., :2 * M_WIDTH], m_conv_w[l], m_conv_b[l]))
        v_m = p_mqkv[..., 2 * M_WIDTH:]

        def heads(u):
            return u.reshape(b, s, M_HEADS, M_HEAD_DIM).transpose(0, 2, 1, 3).astype(jnp.float32)

        hcell = mlstm_chunkwise(heads(qk_m[..., :M_WIDTH]), heads(qk_m[..., M_WIDTH:]), heads(v_m),
                                p_mi.transpose(0, 2, 1).astype(jnp.float32),
                                p_mf.transpose(0, 2, 1).astype(jnp.float32))
        hcell = rms_norm(hcell.transpose(0, 2, 1, 3), m_out_gain[l]).reshape(b, s, M_WIDTH)
        h_m = (jax.nn.sigmoid(p_mo.astype(jnp.float32)) * hcell).astype(x.dtype)

        gates = jax.nn.sigmoid(p_merge.astype(jnp.float32)).astype(x.dtype)
        merged = gates[..., :D_MODEL] * (o_nsa @ w_branch_nsa[l]) + gates[..., D_MODEL:] * (h_m @ w_branch_mlstm[l])
        x = x + merged @ w_out[l]

        x = x + 0.5 * swiglu(rms_norm(x, ffn2_norm[l]), ffn2_w_gate[l], ffn2_w_up[l], ffn2_w_down[l])
    return x
```

```python
import numpy as np
import ml_dtypes
import concourse.bass as bass
import concourse.mybir as mybir
from concourse.bass_utils import run_bass_kernel_spmd
from contextlib import ExitStack

F32 = mybir.dt.float32
BF16 = mybir.dt.bfloat16
AF = mybir.ActivationFunctionType
ALU = mybir.AluOpType
AX = mybir.AxisListType

D = 2048
DFF = 5632
KC = D // 128
FC = DFF // 128
TT = 512
SEQV = 4096
NOWN = 2048
EPS = 1e-6


class T:
    __slots__ = ("ap", "name", "w", "r")

    def __init__(self, ap, name=""):
        self.ap = ap
        self.name = name
        self.w = None
        self.r = []

    def __getitem__(self, idx):
        return self.ap[idx]


class Prog:
    ENGS = ("pe", "act", "dve", "pool", "sp")

    def __init__(self, nc):
        self.nc = nc
        self.ops = []
        self.last = {}
        self.dmas = []
        self.pending = {e: set() for e in self.ENGS}

    def barrier(self):
        deps = set(self.last.values()) | set(self.dmas)
        self.dmas = []
        for e in self.ENGS:
            self.pending[e] |= deps

    def op(self, eng, fn, reads=(), writes=(), dma=None):
        i = len(self.ops)
        deps = set(self.pending[eng])
        self.pending[eng] = set()
        self.last[eng] = i
        if dma is not None:
            self.dmas.append(i)
        for t in reads:
            if t.w is not None:
                deps.add(t.w)
        for t in writes:
            if t.w is not None:
                deps.add(t.w)
            deps.update(t.r)
        for t in reads:
            t.r.append(i)
        for t in writes:
            t.w = i
            t.r = []
        d2 = set()
        for d in deps:
            o = self.ops[d]
            if o["dma"] is None and o["eng"] == eng and eng == "pe":
                continue
            d2.add(d)
            o["needed"] = True
        self.ops.append(dict(eng=eng, fn=fn, deps=d2, dma=dma, needed=False, ev=None))
        return i

    def emit(self):
        nc = self.nc
        cnt = {}
        for o in self.ops:
            if o["dma"] is not None:
                k = "dma_" + o["dma"]
                cnt[k] = cnt.get(k, 0) + 16
                o["ev"] = (k, cnt[k])
            elif o["needed"]:
                k = "eng_" + o["eng"]
                cnt[k] = cnt.get(k, 0) + 1
                o["ev"] = (k, cnt[k])
        keys = sorted(cnt.keys())
        self.maxvals = dict(cnt)
        sems = {k: nc.alloc_semaphore(name=k) for k in keys}
        per_eng = {e: [] for e in self.ENGS}
        for o in self.ops:
            per_eng[o["eng"]].append(o)
        handles = {"pe": "tensor", "act": "scalar", "dve": "vector", "pool": "gpsimd", "sp": "sync"}
        ops = self.ops
        with nc.Block() as block:
            for e in self.ENGS:
                lst = per_eng[e]

                def body(h, lst=lst, e=e):
                    seen = {}
                    for o in lst:
                        need = {}
                        for d in o["deps"]:
                            k, v = ops[d]["ev"]
                            if seen.get(k, 0) >= v:
                                continue
                            if need.get(k, 0) < v:
                                need[k] = v
                        for k, v in need.items():
                            h.wait_ge(sems[k], v)
                            seen[k] = v
                        ins = o["fn"](h)
                        if o["ev"] is not None:
                            ins.then_inc(sems[o["ev"][0]], 16 if o["dma"] is not None else 1)
                    if e == "sp":
                        for k in keys:
                            if k.startswith("dma_"):
                                h.wait_ge(sems[k], cnt[k])
                getattr(block, handles[e])(body)
        return sems


class Ctx:
    def __init__(self, nc):
        self.nc = nc
        self.P = Prog(nc)
        self.n = 0
        self.psum = [T(nc.alloc_psum_tensor("ps%d" % i, [128, 512], F32).ap(), "ps%d" % i) for i in range(8)]
        self.psi = 0
        self.ps_mod = 8
        self.pools = {}
        self.stack = None

    def begin(self, ps_mod=8):
        self.stack = ExitStack()
        self.pools = {}
        self.ps_mod = ps_mod

    def end(self):
        self.P.barrier()
        self.stack.close()
        self.stack = None
        self.pools = {}

    def gsb(self, shape, dt, name):
        return T(self.nc.alloc_sbuf_tensor(name, list(shape), dt).ap(), name)

    def sb(self, shape, dt, name=None):
        self.n += 1
        name = "%s_%d" % (name or "t", self.n)
        h = self.stack.enter_context(self.nc.sbuf_tensor(name, list(shape), dt))
        return T(h.ap(), name)

    def ps(self):
        t = self.psum[self.psi % self.ps_mod]
        self.psi += 1
        return t

    def pool(self, key, n, shape, dt):
        if key not in self.pools:
            self.pools[key] = [[self.sb(shape, dt, "%s%d" % (key, i)) for i in range(n)], 0]
        p = self.pools[key]
        t = p[0][p[1] % n]
        idx = p[1] % n
        p[1] += 1
        return t, "%s%d" % (key, idx)


NS = 2
TS = NS * TT


class NormJob:
    def __init__(self, cx, src_dram, t0, gain_sb, ones_bf, xn, ss=None):
        self.cx, self.t0, self.gain_sb, self.ones_bf, self.xn = cx, t0, gain_sb, ones_bf, xn
        self.src_v = src_dram.rearrange("(c p) t -> p c t", p=128)
        self.ss = ss if ss is not None else [cx.psum[6], cx.psum[7]]

    def chunk(self, c):
        cx, P, t0, src_v, ss, ones_bf = self.cx, self.cx.P, self.t0, self.src_v, self.ss, self.ones_bf
        xc, kx = cx.pool("xc", 3, [128, TS], F32)
        P.op("sp", lambda h, c=c, xc=xc: h.dma_start(out=xc[:, :], in_=src_v[:, c, t0:t0 + TS]), writes=[xc], dma=kx)
        sq, _ = cx.pool("sq", 2, [128, TS], BF16)
        P.op("act", lambda h, xc=xc, sq=sq: h.activation(out=sq[:, :], in_=xc[:, :], func=AF.Square), reads=[xc], writes=[sq])
        for s_ in range(NS):
            mm(P, ss[s_], ss[s_][:, :], ones_bf, ones_bf[:, :], sq, sq[:, s_ * TT:(s_ + 1) * TT], c == 0, c == KC - 1)

    def finish(self):
        cx, P, t0, src_v, ss, xn, gain_sb = self.cx, self.cx.P, self.t0, self.src_v, self.ss, self.xn, self.gain_sb
        rstd, _ = cx.pool("rstdL", 2, [128, TS], F32)
        for s_ in range(NS):
            P.op("dve", lambda h, s_=s_: h.tensor_scalar(out=rstd[:, s_ * TT:(s_ + 1) * TT], in0=ss[s_][:, :], scalar1=1.0 / D, scalar2=EPS, op0=ALU.mult, op1=ALU.add),
                 reads=[ss[s_]], writes=[rstd])
        P.op("act", lambda h: h.activation(out=rstd[:, :], in_=rstd[:, :], func=AF.Sqrt), reads=[rstd], writes=[rstd])
        P.op("dve", lambda h: h.reciprocal(out=rstd[:, :], in_=rstd[:, :]), reads=[rstd], writes=[rstd])
        for c in range(KC):
            xc, kx = cx.pool("xc", 3, [128, TS], F32)
            P.op("sp", lambda h, c=c, xc=xc: h.dma_start(out=xc[:, :], in_=src_v[:, c, t0:t0 + TS]), writes=[xc], dma=kx)
            P.op("dve", lambda h, c=c, xc=xc: h.scalar_tensor_tensor(out=xn[:, c, :], in0=xc[:, :], scalar=gain_sb[:, c:c + 1], in1=rstd[:, :], op0=ALU.mult, op1=ALU.mult),
                 reads=[xc, rstd, gain_sb], writes=[xn])


def ffn_stage(cx, src_dram, dst_dram, tiles, gain_sb, wg, wu, wd, ones_bf, tag, dst_off=0):
    P = cx.P
    cx.begin(6)
    xn = cx.sb([128, KC, TS], BF16, "xn")
    hbuf = cx.sb([128, FC, TS], BF16, "hbuf")
    src_v = src_dram.rearrange("(c p) t -> p c t", p=128)
    dst_v = dst_dram.rearrange("(c p) t -> p c t", p=128)
    nj = NormJob(cx, src_dram, tiles[0], gain_sb, ones_bf, xn)
    for c in range(KC):
        nj.chunk(c)
    nj.finish()
    for ti_, t0 in enumerate(tiles):
        nxt = NormJob(cx, src_dram, tiles[ti_ + 1], gain_sb, ones_bf, xn) if ti_ + 1 < len(tiles) else None
        for j in range(FC):
            if nxt is not None and 20 <= j < 20 + KC:
                nxt.chunk(j - 20)
            wgs, kg = cx.pool("wA", 4, [128, KC * 128], BF16)
            P.op("pool", lambda h, j=j, wgs=wgs: h.dma_start(out=wgs[:, :], in_=wg[j, :, :], max_dma_last_dim=8192), writes=[wgs], dma=kg)
            wus, ku = cx.pool("wA", 4, [128, KC * 128], BF16)
            P.op("pool", lambda h, j=j, wus=wus: h.dma_start(out=wus[:, :], in_=wu[j, :, :], max_dma_last_dim=8192), writes=[wus], dma=ku)
            pg = [cx.ps() for _ in range(NS)]
            pu = [cx.ps() for _ in range(NS)]
            for c in range(KC):
                for s_ in range(NS):
                    mm(P, pg[s_], pg[s_][:, :], wgs, wgs[:, c * 128:(c + 1) * 128], xn, xn[:, c, s_ * TT:(s_ + 1) * TT], c == 0, c == KC - 1)
            for c in range(KC):
                for s_ in range(NS):
                    mm(P, pu[s_], pu[s_][:, :], wus, wus[:, c * 128:(c + 1) * 128], xn, xn[:, c, s_ * TT:(s_ + 1) * TT], c == 0, c == KC - 1)
            for s_ in range(NS):
                sg, _ = cx.pool("sg", 3, [128, TT], BF16)
                P.op("act", lambda h, pg_=pg[s_], sg=sg: h.activation(out=sg[:, :], in_=pg_[:, :], func=AF.Silu), reads=[pg[s_]], writes=[sg])
                P.op("dve", lambda h, j=j, sg=sg, pu_=pu[s_], s_=s_: h.tensor_tensor(out=hbuf[:, j, s_ * TT:(s_ + 1) * TT], in0=sg[:, :], in1=pu_[:, :], op=ALU.mult),
                     reads=[sg, pu[s_]], writes=[hbuf])
        if nxt is not None:
            nxt.finish()
        for m in range(KC):
            wds, kd = cx.pool("wD", 2, [128, FC * 128], BF16)
            for q in range(4):
                f0 = q * (FC // 4) * 128
                f1 = (q + 1) * (FC // 4) * 128
                P.op("pool", lambda h, m=m, wds=wds, f0=f0, f1=f1: h.dma_start(out=wds[:, f0:f1], in_=wd[m, :, f0:f1], max_dma_last_dim=5632), writes=[wds], dma=kd)
            po = [cx.ps() for _ in range(NS)]
            for j in range(FC):
                for s_ in range(NS):
                    mm(P, po[s_], po[s_][:, :], wds, wds[:, j * 128:(j + 1) * 128], hbuf, hbuf[:, j, s_ * TT:(s_ + 1) * TT], j == 0, j == FC - 1)
            xc, kx = cx.pool("xc", 3, [128, TS], F32)
            P.op("sp", lambda h, m=m, xc=xc, t0=t0: h.dma_start(out=xc[:, :], in_=src_v[:, m, t0:t0 + TS]), writes=[xc], dma=kx)
            ot, ko = cx.pool("ot", 2, [128, TS], F32)
            for s_ in range(NS):
                P.op("dve", lambda h, po_=po[s_], ot=ot, xc=xc, s_=s_: h.scalar_tensor_tensor(out=ot[:, s_ * TT:(s_ + 1) * TT], in0=po_[:, :], scalar=0.5, in1=xc[:, s_ * TT:(s_ + 1) * TT], op0=ALU.mult, op1=ALU.add),
                     reads=[po[s_], xc], writes=[ot])
            P.op("sp", lambda h, m=m, ot=ot, t0=t0: h.dma_start(out=dst_v[:, m, t0 + dst_off:t0 + dst_off + TS], in_=ot[:, :]), reads=[ot], dma=ko)
    cx.end()


def mm(P, out_t, out_ap, lhsT_t, lhsT_ap, rhs_t, rhs_ap, start, stop):
    P.op("pe", lambda h: h.matmul(out_ap, lhsT_ap, rhs_ap, start=start, stop=stop), reads=[lhsT_t, rhs_t], writes=[out_t])


NEGB = -30000.0
TILES_ALL = (["kcmp"] * 2 + ["vcmp"] * 2 + ["kslc"] * 2 + ["kwin"] * 2 + ["vslc"] * 2 + ["vwin"] * 2
             + ["mq"] * 8 + ["mk"] * 8 + ["mv"] * 8 + ["small"])
TILES_OWN = ["q"] * 8 + ["mo"] * 8 + ["merge"] * 32
NT_ALL = len(TILES_ALL)
NT_IN = NT_ALL + len(TILES_OWN)


def inproj_stage(cx, G):
    P = cx.P
    cx.begin(8)
    xn = cx.sb([128, KC, TS], BF16, "xn")
    wi, bi = G["wi"], G["bi_sb"]
    ones_bf, bd64, identf, vcol = G["ones_bf"], G["bd64"], G["identf"], G["vcol"]
    gk = G["gk_sb"]
    first_idx = {}
    for j, k in enumerate(TILES_ALL + TILES_OWN):
        first_idx.setdefault(k, j)
    for ti in range(SEQV // TS):
        t0b = ti * TS
        own = t0b >= NOWN
        nj = NormJob(cx, G["x1T"], t0b, G["gmix_sb"], ones_bf, xn, ss=[cx.ps(), cx.ps()])
        for c in range(KC):
            nj.chunk(c)
        nj.finish()
        nxt = None
        plan = TILES_ALL + (TILES_OWN if own else [])
        deferred = []
        for j, kind in enumerate(plan):
            if nxt is not None and 8 <= j < 8 + KC:
                nxt.chunk(j - 8)
            if nxt is not None and j == len(plan) - 1:
                pass
            k = j - first_idx[kind]
            ws, kw_ = cx.pool("wA", 6, [128, KC * 128], BF16)
            P.op("pool", lambda h, j=j, ws=ws: h.dma_start(out=ws[:, :], in_=wi[j, :, :], max_dma_last_dim=8192), writes=[ws], dma=kw_)
            pss = [cx.ps() for _ in range(NS)]
            for c in range(KC):
                for s_ in range(NS):
                    mm(P, pss[s_], pss[s_][:, :], ws, ws[:, c * 128:(c + 1) * 128], xn, xn[:, c, s_ * TT:(s_ + 1) * TT], c == 0, c == KC - 1)
            while deferred:
                deferred.pop(0)()
            for s_ in range(NS):
                ps = pss[s_]
                t0 = t0b + s_ * TT
                bcol = bi[:, j:j + 1]
                if kind in ("kcmp", "vcmp"):
                    ob, ko = cx.pool("obb", 3, [128, TT], BF16)
                    P.op("act", lambda h, ps=ps, ob=ob, bcol=bcol: h.activation(out=ob[:, :], in_=ps[:, :], func=AF.Identity, bias=bcol), reads=[ps, bi], writes=[ob])
                    r0 = (0 if kind == "kcmp" else 256) + k * 128
                    P.op("sp", lambda h, ob=ob, r0=r0, t0=t0: h.dma_start(out=G["kvc"][r0:r0 + 128, t0:t0 + TT], in_=ob[:, :]), reads=[ob], dma=ko)
                elif kind in ("mq", "mk"):
                    ob, ko = cx.pool("obf", 3, [128, TT], F32)
                    P.op("act", lambda h, ps=ps, ob=ob, bcol=bcol: h.activation(out=ob[:, :], in_=ps[:, :], func=AF.Identity, bias=bcol), reads=[ps, bi], writes=[ob])
                    dst = G["mqr"] if kind == "mq" else G["mkr"]
                    P.op("sp", lambda h, ob=ob, dst=dst, k=k, t0=t0: h.dma_start(out=dst[k * 128:(k + 1) * 128, t0:t0 + TT], in_=ob[:, :]), reads=[ob], dma=ko)
                elif kind in ("mo", "merge"):
                    ob, ko = cx.pool("obf", 3, [128, TT], F32)
                    P.op("act", lambda h, ps=ps, ob=ob, bcol=bcol: h.activation(out=ob[:, :], in_=ps[:, :], func=AF.Sigmoid, bias=bcol), reads=[ps, bi], writes=[ob])
                    dst = G["mo"] if kind == "mo" else G["mg"]
                    P.op("sp", lambda h, ob=ob, dst=dst, k=k, t0=t0: h.dma_start(out=dst[k * 128:(k + 1) * 128, t0 - NOWN:t0 - NOWN + TT], in_=ob[:, :]), reads=[ob], dma=ko)
                elif kind in ("kslc", "kwin", "q"):
                    z, _ = cx.pool("z", 5, [128, TT], F32)
                    P.op("act", lambda h, ps=ps, z=z, bcol=bcol: h.activation(out=z[:, :], in_=ps[:, :], func=AF.Identity, bias=bcol), reads=[ps, bi], writes=[z])
                    sq, _ = cx.pool("sqe", 5, [128, TT], BF16)
                    P.op("act", lambda h, z=z, sq=sq: h.activation(out=sq[:, :], in_=z[:, :], func=AF.Square), reads=[z], writes=[sq])
                    def part_b(ps=ps, z=z, t0=t0, k=k, kind=kind, j=j, own=own, sq=sq):
                        ps2 = cx.ps()
                        mm(P, ps2, ps2[:, :], bd64, bd64[:, :], sq, sq[:, :], True, True)
                        rs, _ = cx.pool("rstd", 2, [128, TT], F32)
                        P.op("dve", lambda h, ps2=ps2, rs=rs: h.tensor_scalar(out=rs[:, :], in0=ps2[:, :], scalar1=1.0 / 64, scalar2=EPS, op0=ALU.mult, op1=ALU.add), reads=[ps2], writes=[rs])
                        sc = 64.0 if kind == "q" else 1.0
                        P.op("act", lambda h, rs=rs, sc=sc: h.activation(out=rs[:, :], in_=rs[:, :], func=AF.Sqrt, scale=sc), reads=[rs], writes=[rs])
                        P.op("dve", lambda h, rs=rs: h.reciprocal(out=rs[:, :], in_=rs[:, :]), reads=[rs], writes=[rs])
                        gi = {"kslc": 0, "kwin": 1, "q": 2}[kind]
                        ob, ko = cx.pool("obb", 3, [128, TT], BF16)
                        P.op("dve", lambda h, z=z, rs=rs, ob=ob, gi=gi: h.scalar_tensor_tensor(out=ob[:, :], in0=z[:, :], scalar=gk[:, gi:gi + 1], in1=rs[:, :], op0=ALU.mult, op1=ALU.mult),
                             reads=[z, rs, gk], writes=[ob])
                        if kind == "q":
                            P.op("sp", lambda h, ob=ob, k=k, t0=t0: h.dma_start(out=G["qn"][k * 128:(k + 1) * 128, t0 - NOWN:t0 - NOWN + TT], in_=ob[:, :]), reads=[ob], dma=ko)
                        else:
                            dst = G["ks"] if kind == "kslc" else G["kw"]
                            P.op("sp", lambda h, ob=ob, dst=dst, k=k, t0=t0: h.dma_start(out=dst[k * 128:(k + 1) * 128, t0:t0 + TT], in_=ob[:, :]), reads=[ob], dma=ko)
                    deferred.append(part_b)
                else:
                    z, _ = cx.pool("z", 5, [128, TT], F32)
                    P.op("act", lambda h, ps=ps, z=z, bcol=bcol: h.activation(out=z[:, :], in_=ps[:, :], func=AF.Identity, bias=bcol), reads=[ps, bi], writes=[z])
                    def part_b(ps=ps, z=z, t0=t0, k=k, kind=kind, j=j, own=own):
                        pst = cx.ps()
                        for s in range(4):
                            P.op("pe", lambda h, pst=pst, z=z, s=s: h.transpose(out=pst[:, s * 128:(s + 1) * 128], in_=z[:, s * 128:(s + 1) * 128], identity=identf[:, :]),
                                 reads=[z, identf], writes=[pst])
                        if kind == "small":
                            ob, ko = cx.pool("otf", 2, [128, TT], F32)
                            P.op("dve", lambda h, pst=pst, ob=ob: h.tensor_copy(out=ob[:, :], in_=pst[:, :]), reads=[pst], writes=[ob])
                            dstv = G["small"].rearrange("(s p) f -> p s f", p=128)
                            P.op("sp", lambda h, ob=ob, dstv=dstv, t0=t0: h.dma_start(out=dstv[:, t0 // 128:t0 // 128 + 4, :], in_=ob[:, :].rearrange("p (s f) -> p s f", s=4)), reads=[ob], dma=ko)
                        else:
                            ob, ko = cx.pool("otb", 2, [128, TT], BF16)
                            if own or kind == "mv":
                                P.op("dve", lambda h, pst=pst, ob=ob: h.tensor_copy(out=ob[:, :], in_=pst[:, :]), reads=[pst], writes=[ob])
                            else:
                                P.op("dve", lambda h, pst=pst, ob=ob: h.tensor_scalar(out=ob[:, :], in0=pst[:, :], scalar1=vcol[:, 0:1], scalar2=None, op0=ALU.mult), reads=[pst, vcol], writes=[ob])
                            if kind == "mv":
                                dstv = G["mv"].rearrange("(s p) f -> p s f", p=128)[:, t0 // 128:t0 // 128 + 4, k * 128:(k + 1) * 128]
                            else:
                                br = 0 if kind == "vslc" else 1
                                dstv = G["v"].rearrange("(s p) b f -> p s b f", p=128)[:, t0 // 128:t0 // 128 + 4, br, k * 128:(k + 1) * 128]
                            P.op("sp", lambda h, ob=ob, dstv=dstv: h.dma_start(out=dstv, in_=ob[:, :].rearrange("p (s f) -> p s f", s=4)), reads=[ob], dma=ko)
                    deferred.append(part_b)
        while deferred:
            deferred.pop(0)()
    cx.end()


def cmp_stage(cx, G):
    P = cx.P
    cx.begin(8)
    ones_bf, validc = G["ones_bf"], G["validc"]
    KcT, Vc = G["KcT"], G["Vc"]
    for kv in range(2):
        w1 = cx.sb([64, 32 * 256], BF16, "w1")
        P.op("pool", lambda h, kv=kv, w1=w1: h.dma_start(out=w1[:, :], in_=G["w1"][kv, :, :], max_dma_last_dim=8192), writes=[w1], dma="w1")
        w2 = cx.sb([128, 2 * 64], BF16, "w2")
        P.op("pool", lambda h, kv=kv, w2=w2: h.dma_start(out=w2[:, :], in_=G["w2"][kv, :, :]), writes=[w2], dma="w2")
        posT = cx.sb([64, 32], BF16, "posT")
        P.op("pool", lambda h, kv=kv, posT=posT: h.dma_start(out=posT[:, :], in_=G["posT"][kv, :, :]), writes=[posT], dma="posT")
        b1 = cx.sb([128, 2], F32, "b1")
        for hc in range(2):
            ps = cx.ps()
            for j in range(32):
                mm(P, ps, ps[:, 0:1], w1, w1[:, j * 256 + hc * 128:j * 256 + hc * 128 + 128], posT, posT[:, j:j + 1], j == 0, j == 31)
            P.op("dve", lambda h, ps=ps, hc=hc, b1=b1: h.tensor_copy(out=b1[:, hc:hc + 1], in_=ps[:, 0:1]), reads=[ps], writes=[b1])
        for g in range(4):
            kvT, kk = cx.pool("kvT", 2, [64, SEQV + 32], BF16)
            P.op("dve", lambda h, kvT=kvT: h.memset(kvT[:, SEQV:SEQV + 32], 0.0), writes=[kvT])
            P.op("sp", lambda h, kvT=kvT, kv=kv, g=g: h.dma_start(out=kvT[:, 0:SEQV], in_=G["kvc"][kv * 256 + g * 64:kv * 256 + g * 64 + 64, :]), writes=[kvT], dma=kk)
            gT, _ = cx.pool("gT", 2, [128, 2, 256], BF16)
            for hc in range(2):
                ps = cx.ps()
                for j in range(32):
                    mm(P, ps, ps[:, 0:256], w1, w1[:, j * 256 + hc * 128:j * 256 + hc * 128 + 128], kvT, kvT[:, j:j + SEQV:16], j == 0, j == 31)
                z, _ = cx.pool("cz", 2, [128, 256], F32)
                t, _ = cx.pool("ct", 2, [128, 256], F32)
                P.op("act", lambda h, ps=ps, z=z, hc=hc, b1=b1: h.activation(out=z[:, :], in_=ps[:, 0:256], func=AF.Identity, bias=b1[:, hc:hc + 1]), reads=[ps, b1], writes=[z])
                P.op("act", lambda h, z=z, t=t: h.activation(out=t[:, :], in_=z[:, :], func=AF.Square), reads=[z], writes=[t])
                P.op("dve", lambda h, t=t: h.tensor_scalar(out=t[:, :], in0=t[:, :], scalar1=0.044715, scalar2=1.0, op0=ALU.mult, op1=ALU.add), reads=[t], writes=[t])
                P.op("dve", lambda h, t=t, z=z: h.tensor_tensor(out=t[:, :], in0=t[:, :], in1=z[:, :], op=ALU.mult), reads=[t, z], writes=[t])
                P.op("act", lambda h, t=t: h.activation(out=t[:, :], in_=t[:, :], func=AF.Sigmoid, scale=1.5957691216057308), reads=[t], writes=[t])
                P.op("dve", lambda h, t=t, z=z, gT=gT, hc=hc: h.tensor_tensor(out=gT[:, hc, :], in0=t[:, :], in1=z[:, :], op=ALU.mult), reads=[t, z], writes=[gT])
            if kv == 0:
                ps = cx.ps()
                for hc in range(2):
                    mm(P, ps, ps[0:64, 0:256], w2, w2[:, hc * 64:(hc + 1) * 64], gT, gT[:, hc, :], hc == 0, hc == 1)
                zk, _ = cx.pool("zk", 2, [64, 256], F32)
                sq, _ = cx.pool("sqk", 2, [64, 256], BF16)
                P.op("act", lambda h, ps=ps, zk=zk: h.activation(out=zk[:, :], in_=ps[0:64, 0:256], func=AF.Copy), reads=[ps], writes=[zk])
                P.op("act", lambda h, zk=zk, sq=sq: h.activation(out=sq[:, :], in_=zk[:, :], func=AF.Square), reads=[zk], writes=[sq])
                ps2 = cx.ps()
                mm(P, ps2, ps2[0:64, 0:256], ones_bf, ones_bf[0:64, 0:64], sq, sq[:, :], True, True)
                rs, _ = cx.pool("rsk", 2, [64, 256], F32)
                P.op("dve", lambda h, ps2=ps2, rs=rs: h.tensor_scalar(out=rs[:, :], in0=ps2[0:64, 0:256], scalar1=1.0 / 64, scalar2=EPS, op0=ALU.mult, op1=ALU.add), reads=[ps2], writes=[rs])
                P.op("act", lambda h, rs=rs: h.activation(out=rs[:, :], in_=rs[:, :], func=AF.Sqrt), reads=[rs], writes=[rs])
                P.op("dve", lambda h, rs=rs: h.reciprocal(out=rs[:, :], in_=rs[:, :]), reads=[rs], writes=[rs])
                P.op("dve", lambda h, zk=zk, rs=rs, g=g: h.scalar_tensor_tensor(out=KcT[:, g, :], in0=zk[:, :], scalar=G["gk_sb"][0:64, 3:4], in1=rs[:, :], op0=ALU.mult, op1=ALU.mult),
                     reads=[zk, rs, G["gk_sb"]], writes=[KcT])
            else:
                for ct in range(2):
                    ps = cx.ps()
                    for hc in range(2):
                        mm(P, ps, ps[:, 0:64], gT, gT[:, hc, ct * 128:(ct + 1) * 128], w2, w2[:, hc * 64:(hc + 1) * 64], hc == 0, hc == 1)
                    P.op("dve", lambda h, ps=ps, g=g, ct=ct: h.tensor_scalar(out=Vc[:, g, ct, 0:64], in0=ps[:, 0:64], scalar1=validc[:, ct:ct + 1], scalar2=None, op0=ALU.mult),
                         reads=[ps, validc], writes=[Vc])
                    P.op("dve", lambda h, g=g, ct=ct: h.tensor_copy(out=Vc[:, g, ct, 64:65], in_=validc[:, ct:ct + 1]), reads=[validc], writes=[Vc])
                    P.op("dve", lambda h, g=g, ct=ct: h.tensor_scalar(out=Vc[:, g, ct, 65:129], in0=G["ov"][:, ct, :], scalar1=validc[:, ct:ct + 1], scalar2=None, op0=ALU.mult),
                         reads=[validc, G["ov"]], writes=[Vc])
    cx.end()


def nsa_stage(cx, G):
    P = cx.P
    cx.begin(4)
    acc = cx.psum[4:8]
    identb, identf, vcol = G["identb"], G["identf"], G["vcol"]
    KcT, Vc = G["KcT"], G["Vc"]
    cpen = cx.sb([128, 4, 512], BF16, "cpen")
    wpen = cx.sb([128, 8, 512], BF16, "wpen")
    cmpp = cx.sb([128, 8, 512], BF16, "cmpp")
    mulm = cx.sb([128, 16, 64], F32, "mulm")
    addm = cx.sb([128, 16, 64], F32, "addm")
    sgate = cx.sb([128, 16, 48], F32, "sgate")
    for t, src in ((cpen, G["c_cpen"]), (wpen, G["c_wpen"]), (cmpp, G["c_cmpp"]), (mulm, G["c_mulm"]), (addm, G["c_addm"])):
        P.op("sp", lambda h, t=t, src=src: h.dma_start(out=t.ap, in_=src), writes=[t], dma="nc")
    P.op("sp", lambda h: h.dma_start(out=sgate[:, :, :], in_=G["small"].rearrange("(s p) f -> p s f", p=128)[:, 16:32, 8:56]), writes=[sgate], dma="nc")
    P.barrier()
    P.op("act", lambda h: h.activation(out=sgate[:, :, :], in_=sgate[:, :, :], func=AF.Sigmoid), reads=[sgate], writes=[sgate])
    vview = G["v"].rearrange("(s p) b (g d) -> p s b g d", p=128, g=4)
    cg = conv_gen(cx, G)
    jobctr = [0]

    def evac(accs, qt, g, r, branch, oacc, impa=None):
        W = 129 if branch == 0 else 65
        stg, _ = cx.pool("stg%d" % W, 3, [128, 4, W], F32)
        if branch == 0:
            for hb in range(2):
                a = accs[hb]
                P.op("dve", lambda h, a=a, stg=stg, hb=hb: h.tensor_copy(out=stg[:, 2 * hb:2 * hb + 2, :], in_=a[:, 0:2 * W].rearrange("p (s w) -> p s w", s=2)), reads=[a], writes=[stg])
        else:
            a = accs
            P.op("dve", lambda h, a=a, stg=stg: h.tensor_copy(out=stg[:, :, :], in_=a[:, 0:4 * W].rearrange("p (s w) -> p s w", s=4)), reads=[a], writes=[stg])
        rl, _ = cx.pool("rl", 4, [128, 4, 2], F32)
        gc = (g * 4 + r) * 3 + branch
        P.op("dve", lambda h, stg=stg, rl=rl: h.tensor_scalar(out=rl[:, :, 0:1], in0=stg[:, :, 64:65], scalar1=1e-30, scalar2=None, op0=ALU.max), reads=[stg], writes=[rl])
        P.op("dve", lambda h, rl=rl: h.reciprocal(out=rl[:, :, 0:1], in_=rl[:, :, 0:1]), reads=[rl], writes=[rl])
        P.op("dve", lambda h, rl=rl, qt=qt, gc=gc: h.tensor_tensor(out=rl[:, :, 1:2], in0=rl[:, :, 0:1], in1=sgate[:, qt * 4:qt * 4 + 4, gc:gc + 1], op=ALU.mult), reads=[rl, sgate], writes=[rl])
        tmp, _ = cx.pool("etmp", 3, [128, 4, 64], F32)
        P.op("dve", lambda h, stg=stg, rl=rl, tmp=tmp: h.tensor_tensor(out=tmp[:, :, :], in0=stg[:, :, 0:64], in1=rl[:, :, 1:2].to_broadcast([128, 4, 64]), op=ALU.mult), reads=[stg, rl], writes=[tmp])
        P.op("dve", lambda h, tmp=tmp, r=r, oacc=oacc: h.tensor_tensor(out=oacc[:, :, r * 64:(r + 1) * 64], in0=oacc[:, :, r * 64:(r + 1) * 64], in1=tmp[:, :, :], op=ALU.add), reads=[tmp, oacc], writes=[oacc])
        if impa is not None:
            if r == 0:
                P.op("dve", lambda h, stg=stg, rl=rl, impa=impa: h.tensor_tensor(out=impa[:, :, :], in0=stg[:, :, 65:129], in1=rl[:, :, 0:1].to_broadcast([128, 4, 64]), op=ALU.mult), reads=[stg, rl], writes=[impa])
            else:
                tm2, _ = cx.pool("etmp", 3, [128, 4, 64], F32)
                P.op("dve", lambda h, stg=stg, rl=rl, tm2=tm2: h.tensor_tensor(out=tm2[:, :, :], in0=stg[:, :, 65:129], in1=rl[:, :, 0:1].to_broadcast([128, 4, 64]), op=ALU.mult), reads=[stg, rl], writes=[tm2])
                P.op("dve", lambda h, tm2=tm2, impa=impa: h.tensor_tensor(out=impa[:, :, :], in0=impa[:, :, :], in1=tm2[:, :, :], op=ALU.add), reads=[tm2, impa], writes=[impa])

    pending_out = []
    KsEs, KsEes = [], []
    for i_ in range(2):
        t_ = cx.sb([128, SEQV], BF16, "KsE%d" % i_)
        e_ = T(t_.ap, "KsEe%d" % i_)
        P.op("sp", lambda h, t_=t_: h.dma_start(out=t_[64:128, :], in_=G["c_Eexp"]), writes=[e_], dma="nc")
        KsEs.append(t_)
        KsEes.append(e_)
    P.barrier()

    def load_group(g):
        KsT, k1 = KsEs[g % 2], "KsT%d" % (g % 2)
        KwT, k2 = cx.pool("KwT", 2, [64, SEQV], BF16)
        Vs, k3 = cx.pool("Vs", 2, [128, 32, 65], BF16)
        Vw, k4 = cx.pool("Vw", 2, [128, 32, 65], BF16)
        P.op("sp", lambda h, g=g, KsT=KsT: h.dma_start(out=KsT[0:64, :], in_=G["ks"][g * 64:(g + 1) * 64, :]), writes=[KsT], dma=k1)
        P.op("sp", lambda h, g=g, KwT=KwT: h.dma_start(out=KwT[:, :], in_=G["kw"][g * 64:(g + 1) * 64, :]), writes=[KwT], dma=k2)
        for br, V, kk in ((0, Vs, k3), (1, Vw, k4)):
            for hf in range(2):
                P.op("sp", lambda h, g=g, V=V, br=br, hf=hf: h.dma_start(out=V[:, hf * 16:(hf + 1) * 16, 0:64], in_=vview[:, hf * 16:(hf + 1) * 16, br, g, :]), writes=[V], dma=kk)
            P.op("dve", lambda h, V=V: h.tensor_copy(out=V[:, 0:16, 64:65], in_=vcol[:, 0:1].unsqueeze(1).to_broadcast([128, 16, 1])), reads=[vcol], writes=[V])
            P.op("dve", lambda h, V=V: h.memset(V[:, 16:32, 64:65], 1.0), writes=[V])
        return KsT, KsEes[g % 2], KwT, Vs, Vw

    def load_q(g, qt):
        QT, kq = cx.pool("QT", 3, [128, 4, TT], BF16)
        for r in range(4):
            P.op("sp", lambda h, QT=QT, r=r, g=g, qt=qt: h.dma_start(out=QT[0:64, r, :], in_=G["qn"][(g * 4 + r) * 64:(g * 4 + r + 1) * 64, qt * TT:(qt + 1) * TT]), writes=[QT], dma=kq)
        return QT

    qt_next = [None]
    nxt_grp = load_group(0)
    for g in range(4):
        KsE, KsEe, KwT, Vs, Vw = nxt_grp
        if g < 3:
            nxt_grp = load_group(g + 1)
        for qt in range(4):
            q0 = NOWN + qt * TT
            nkt = q0 // 128 + 4
            if qt_next[0] is None:
                qt_next[0] = load_q(g, qt)
            QT = qt_next[0]
            nb_ = g * 4 + qt + 1
            qt_next[0] = load_q(nb_ // 4, nb_ % 4) if nb_ < 16 else None
            QTp = T(QT.ap, "QTp")
            oacc, _ = cx.pool("oacc", 2, [128, 4, 256], F32)
            impa, _ = cx.pool("impa", 2, [128, 4, 64], F32)
            P.op("dve", lambda h, oacc=oacc: h.memset(oacc[:, :, :], 0.0), writes=[oacc])
            def run_jobs(jobs, depth=4, hooks=None):
                q = []
                for ji, (sc_fn, pv_fn) in enumerate(jobs):
                    if hooks and ji in hooks:
                        hooks[ji]()
                    q.append((sc_fn(), pv_fn))
                    jobctr[0] += 1
                    if jobctr[0] % 10 == 0:
                        next(cg, None)
                    if len(q) > depth:
                        pc, f_ = q.pop(0)
                        f_(pc)
                for pc, f_ in q:
                    f_(pc)

            def cmp_score(r):
                Pc = []
                for ct in range(2):
                    ps = cx.ps()
                    mm(P, ps, ps[:, :], KcT, KcT[:, g, ct * 128:(ct + 1) * 128], QT, QT[0:64, r, :], True, False)
                    mm(P, ps, ps[:, :], identb, identb[:, :], cmpp, cmpp[:, qt * 2 + ct, :], False, True)
                    pc, _ = cx.pool("Ptc", 4, [128, TT], BF16)
                    P.op("act", lambda h, ps=ps, pc=pc: h.activation(out=pc[:, :], in_=ps[:, :], func=AF.Exp), reads=[ps], writes=[pc])
                    Pc.append(pc)
                return Pc

            def cmp_pv(r, Pc):
                for sub in range(4):
                    a = acc[2 + sub // 2]
                    o0 = (sub % 2) * 129
                    for ct in range(2):
                        mm(P, a, a[:, o0:o0 + 129], Pc[ct], Pc[ct][:, sub * 128:(sub + 1) * 128], Vc, Vc[:, g, ct, :], (ct == 0 and sub % 2 == 0), (ct == 1 and sub % 2 == 1))
                evac(acc[2:4], qt, g, r, 0, oacc, impa)

            run_jobs([((lambda r=r: cmp_score(r)), (lambda Pc, r=r: cmp_pv(r, Pc))) for r in range(4)], depth=1)
            pens = []
            for sub in range(4):
                i2, _ = cx.pool("i2", 2, [128, 64], F32)
                i3, _ = cx.pool("i3", 2, [128, 64], F32)
                m8, _ = cx.pool("m8", 2, [128, 16], F32)
                penp, _ = cx.pool("pen", 4, [128, 128], F32)
                P.op("dve", lambda h, penp=penp: h.memset(penp[:, 0:64], 0.0), writes=[penp])
                pen = T(penp.ap[:, 64:128], "penv")
                pen.w, pen.r = None, []
                P.op("dve", lambda h, i2=i2, sub=sub, impa=impa, qt=qt: h.tensor_tensor(out=i2[:, :], in0=impa[:, sub, :], in1=mulm[:, qt * 4 + sub, :], op=ALU.mult), reads=[impa, mulm], writes=[i2])
                P.op("dve", lambda h, i2=i2, sub=sub, qt=qt: h.tensor_tensor(out=i2[:, :], in0=i2[:, :], in1=addm[:, qt * 4 + sub, :], op=ALU.add), reads=[i2, addm], writes=[i2])
                P.op("dve", lambda h, i2=i2, m8=m8: h.max(out=m8[:, 0:8], in_=i2[:, :]), reads=[i2], writes=[m8])
                P.op("dve", lambda h, i2=i2, i3=i3, m8=m8: h.match_replace(out=i3[:, :], in_to_replace=m8[:, 0:8], in_values=i2[:, :], imm_value=-1e30), reads=[i2, m8], writes=[i3])
                P.op("dve", lambda h, i3=i3, m8=m8: h.max(out=m8[:, 8:16], in_=i3[:, :]), reads=[i3, m8], writes=[m8])
                P.op("dve", lambda h, i2=i2, m8=m8, pen=pen: h.tensor_scalar(out=pen[:, :], in0=i2[:, :], scalar1=m8[:, 15:16], scalar2=None, op0=ALU.is_ge), reads=[i2, m8], writes=[pen])
                P.op("dve", lambda h, pen=pen: h.tensor_scalar(out=pen[:, :], in0=pen[:, :], scalar1=-1.0, scalar2=-NEGB, op0=ALU.add, op1=ALU.mult), reads=[pen], writes=[pen])
                P.op("dve", lambda h, penp=penp: h.tensor_copy(out=penp[:, 64:65], in_=penp[:, 64:65]), reads=[pen], writes=[penp])
                pens.append(penp)

            def win_score(r, rel):
                kt = q0 // 128 - 4 + rel
                ps = cx.ps()
                c0, c1 = 128 * max(0, rel - 4), 128 * (min(3, rel) + 1)
                mm(P, ps, ps[:, c0:c1], KwT, KwT[:, kt * 128:(kt + 1) * 128], QT, QT[0:64, r, c0:c1], True, True)
                pc, _ = cx.pool("Pt", 7, [128, TT], BF16)
                P.op("act", lambda h, ps=ps, pc=pc, c0=c0, c1=c1: h.activation(out=pc[:, c0:c1], in_=ps[:, c0:c1], func=AF.Exp), reads=[ps], writes=[pc])
                P.op("pool", lambda h, pc=pc, rel=rel, c0=c0, c1=c1: h.tensor_tensor(out=pc[:, c0:c1], in0=pc[:, c0:c1], in1=wpen[:, rel, c0:c1], op=ALU.mult), reads=[pc, wpen], writes=[pc])
                return pc

            def win_pv(r, rel, pc):
                kt = q0 // 128 - 4 + rel
                for sub in range(4):
                    if not (sub <= rel <= sub + 4):
                        continue
                    a = acc[r % 2]
                    mm(P, a, a[:, sub * 65:(sub + 1) * 65], pc, pc[:, sub * 128:(sub + 1) * 128], Vw, Vw[:, kt, :], (rel == 0 and sub == 0), (rel == 7 and sub == 3))
                if rel == 7:
                    evac(acc[r % 2], qt, g, r, 2, oacc)

            def emit_pen():
                for sub in range(4):
                    penp = pens[sub]
                    pst = cx.ps()
                    P.op("pe", lambda h, pst=pst, penp=penp: h.transpose(out=pst[:, 0:128], in_=penp[:, :], identity=identf[:, :]), reads=[penp, identf], writes=[pst])
                    for r in range(4):
                        P.op("dve", lambda h, pst=pst, sub=sub, QT=QT, r=r: h.tensor_copy(out=QT[64:128, r, sub * 128:(sub + 1) * 128], in_=pst[64:128, 0:128]), reads=[pst], writes=[QTp])


            def emit_prev_out():
                while pending_out:
                    pending_out.pop(0)()

            run_jobs([((lambda r=r, rel=rel: win_score(r, rel)), (lambda pc, r=r, rel=rel: win_pv(r, rel, pc))) for rp in range(2) for rel in range(8) for r in (2 * rp, 2 * rp + 1)],
                     hooks={6: emit_prev_out, 20: emit_pen})
            def slc_score(r, kt):
                rel = kt - (nkt - 4)
                ps = cx.ps()
                c0 = 128 * max(0, rel)
                P.op("pe", lambda h, ps=ps, kt=kt, r=r, QT=QT, c0=c0, KsE=KsE: h.matmul(ps[:, c0:], KsE[:, kt * 128:(kt + 1) * 128], QT[:, r, c0:], start=True, stop=True),
                     reads=[KsE, KsEe, QT, QTp], writes=[ps])
                pc, _ = cx.pool("Pt", 7, [128, TT], BF16)
                P.op("act", lambda h, ps=ps, pc=pc, c0=c0: h.activation(out=pc[:, c0:], in_=ps[:, c0:], func=AF.Exp), reads=[ps], writes=[pc])
                if rel >= 0:
                    P.op("pool", lambda h, pc=pc, rel=rel, c0=c0: h.tensor_tensor(out=pc[:, c0:], in0=pc[:, c0:], in1=cpen[:, rel, c0:], op=ALU.mult), reads=[pc, cpen], writes=[pc])
                return pc

            def slc_pv(r, kt, pc):
                rel = kt - (nkt - 4)
                for sub in range(4):
                    if rel > sub:
                        continue
                    a = acc[r % 2]
                    mm(P, a, a[:, sub * 65:(sub + 1) * 65], pc, pc[:, sub * 128:(sub + 1) * 128], Vs, Vs[:, kt, :], (kt == 0 and sub == 0), (kt == nkt - 1 and sub == 3))
                if kt == nkt - 1:
                    evac(acc[r % 2], qt, g, r, 1, oacc)

            run_jobs([((lambda r=r, kt=kt: slc_score(r, kt)), (lambda pc, r=r, kt=kt: slc_pv(r, kt, pc))) for rp in range(2) for kt in range(nkt) for r in (2 * rp, 2 * rp + 1)])
            def emit_out(oacc=oacc, g=g, qt=qt):
                for sub in range(4):
                    for hp in range(2):
                        pst = cx.ps()
                        P.op("pe", lambda h, pst=pst, oacc=oacc, sub=sub, hp=hp: h.transpose(out=pst[:, 0:128], in_=oacc[:, sub, hp * 128:(hp + 1) * 128], identity=identf[:, :]),
                             reads=[oacc, identf], writes=[pst])
                        ob, ko = cx.pool("onb", 3, [128, 128], BF16)
                        P.op("act", lambda h, pst=pst, ob=ob: h.activation(out=ob[:, :], in_=pst[:, 0:128], func=AF.Copy), reads=[pst], writes=[ob])
                        r0 = (g * 4 + hp * 2) * 64
                        c0 = qt * TT + sub * 128
                        P.op("sp", lambda h, ob=ob, r0=r0, c0=c0: h.dma_start(out=G["on"][r0:r0 + 128, c0:c0 + 128], in_=ob[:, :]), reads=[ob], dma=ko)
            pending_out.append(emit_out)
    for f_ in pending_out:
        f_()
    del pending_out[:]
    for _ in cg:
        pass
    cx.end()


def conv_gen(cx, G):
    P = cx.P
    identf, vcol = G["identf"], G["vcol"]
    cw, cb = G["cw_sb"], G["cb_sb"]
    HS = NOWN
    for fc in range(16):
        isq = fc < 8
        src = G["mqr"] if isq else G["mkr"]
        f0 = (fc % 8) * 128
        for seg in ([1] if isq else [0, 1]):
            u, ku = cx.pool("cu", 2, [128, 3 + HS], F32)
            if seg == 0:
                P.op("dve", lambda h, u=u: h.memset(u[:, 0:3], 0.0), writes=[u])
                yield
                P.op("sp", lambda h, u=u, src=src, f0=f0: h.dma_start(out=u[:, 3:3 + HS], in_=src[f0:f0 + 128, 0:HS]), writes=[u], dma=ku)
                yield
            else:
                P.op("sp", lambda h, u=u, src=src, f0=f0: h.dma_start(out=u[:, 0:3 + HS], in_=src[f0:f0 + 128, HS - 3:2 * HS]), writes=[u], dma=ku)
                yield
                P.op("dve", lambda h, u=u: h.tensor_scalar(out=u[:, 0:3], in0=u[:, 0:3], scalar1=vcol[:, 0:1], scalar2=None, op0=ALU.mult), reads=[u, vcol], writes=[u])
                yield
            a, _ = cx.pool("ca", 2, [128, HS], F32)
            P.op("dve", lambda h, u=u, a=a, fc=fc: h.tensor_scalar(out=a[:, :], in0=u[:, 0:HS], scalar1=cw[:, fc, 0:1], scalar2=cb[:, fc:fc + 1], op0=ALU.mult, op1=ALU.add),
                 reads=[u, cw, cb], writes=[a])
            yield
            for j in range(1, 4):
                P.op("dve", lambda h, u=u, a=a, fc=fc, j=j: h.scalar_tensor_tensor(out=a[:, :], in0=u[:, j:j + HS], scalar=cw[:, fc, j:j + 1], in1=a[:, :], op0=ALU.mult, op1=ALU.add),
                     reads=[u, a, cw], writes=[a])
                yield
            P.op("act", lambda h, a=a: h.activation(out=a[:, :], in_=a[:, :], func=AF.Silu), reads=[a], writes=[a])
            yield
            ob, ko = cx.pool("cob", 2, [128, HS], BF16)
            if isq:
                P.op("dve", lambda h, a=a, ob=ob: h.tensor_copy(out=ob[:, :], in_=a[:, :]), reads=[a], writes=[ob])
                yield
                P.op("sp", lambda h, ob=ob, f0=f0: h.dma_start(out=G["qc"][f0:f0 + 128, :], in_=ob[:, :]), reads=[ob], dma=ko)
                yield
            else:
                P.op("dve", lambda h, a=a, ob=ob: h.tensor_scalar(out=ob[:, :], in0=a[:, :], scalar1=1.0 / 16, scalar2=None, op0=ALU.mult), reads=[a], writes=[ob])
                yield
                P.op("sp", lambda h, ob=ob, f0=f0, seg=seg: h.dma_start(out=G["kc"][f0:f0 + 128, seg * HS:(seg + 1) * HS], in_=ob[:, :]), reads=[ob], dma=ko)
                yield
                for i4 in range(4):
                    pst = cx.ps()
                    for s in range(4):
                        i = i4 * 4 + s
                        P.op("pe", lambda h, pst=pst, a=a, s=s, i=i: h.transpose(out=pst[:, s * 128:(s + 1) * 128], in_=a[:, i * 128:(i + 1) * 128], identity=identf[:, :]),
                             reads=[a, identf], writes=[pst])
                    kt_, kk = cx.pool("ckt", 2, [128, TT], BF16)
                    P.op("act", lambda h, pst=pst, kt_=kt_: h.activation(out=kt_[:, :], in_=pst[:, :], func=AF.Copy, scale=1.0 / 16), reads=[pst], writes=[kt_])
                    tt0 = (seg * HS) // 128 + i4 * 4
                    dstv = G["kctm"].rearrange("(s p) f -> p s f", p=128)[:, tt0:tt0 + 4, f0:f0 + 128]
                    P.op("sp", lambda h, kt_=kt_, dstv=dstv: h.dma_start(out=dstv, in_=kt_[:, :].rearrange("p (s f) -> p s f", s=4)), reads=[kt_], dma=kk)
                    yield


def mlstm_stage(cx, G):
    P = cx.P
    cx.begin(8)
    ones_bf, identf, vcol, triu, ones_f = G["ones_bf"], G["identf"], G["vcol"], G["triu"], G["ones_f"]
    cw, cb = G["cw_sb"], G["cb_sb"]
    HS = NOWN
    small = cx.sb([128, 32, 8], F32, "small")
    logf = cx.sb([128, 32, 4], F32, "logf")
    P.op("sp", lambda h: h.dma_start(out=small[:, :, :], in_=G["small"].rearrange("(s p) f -> p s f", p=128)[:, :, 0:8]), writes=[small], dma="ms")
    P.op("act", lambda h: h.activation(out=logf[:, :, :], in_=small[:, :, 4:8], func=AF.Exp, scale=-1.0), reads=[small], writes=[logf])
    P.op("dve", lambda h: h.tensor_scalar(out=logf[:, :, :], in0=logf[:, :, :], scalar1=1.0, scalar2=None, op0=ALU.add), reads=[logf], writes=[logf])
    P.op("act", lambda h: h.activation(out=logf[:, :, :], in_=logf[:, :, :], func=AF.Ln), reads=[logf], writes=[logf])
    P.op("dve", lambda h: h.tensor_scalar(out=logf[:, :, :], in0=logf[:, :, :], scalar1=-1.0, scalar2=None, op0=ALU.mult), reads=[logf], writes=[logf])
    C = [cx.sb([128, 2, 256], F32, "C%d" % h_) for h_ in range(4)]
    Cb = [cx.sb([128, 2, 256], BF16, "Cb%d" % h_) for h_ in range(4)]
    nb = [cx.sb([128, 2, 128], F32, "nb%d" % h_) for h_ in range(4)]
    nbb = [cx.sb([128, 2, 128], BF16, "nbb%d" % h_) for h_ in range(4)]
    for t in C + Cb + nb + nbb:
        P.op("dve", lambda h, t=t: h.memset(t[:, :, :], 0.0), writes=[t])
    kcv = G["kc"].rearrange("(c p) t -> p c t", p=128)
    qcv = G["qc"].rearrange("(c p) t -> p c t", p=128)
    hcv = G["hc"].rearrange("(c p) t -> p c t", p=128)
    def prep_tile(tile):
        own = tile >= 16
        Wm = ea = qT = None
        R, _ = cx.pool("R", 2, [128, 4, 128], F32)
        P.op("dve", lambda h, R=R, tile=tile: h.tensor_tensor(out=R[:, :, :], in0=triu[:, :].unsqueeze(1).to_broadcast([128, 4, 128]),
                                                              in1=logf[:, tile, :].unsqueeze(2).to_broadcast([128, 4, 128]), op=ALU.mult), reads=[triu, logf], writes=[R])
        psA = cx.ps()
        mm(P, psA, psA[:, :], ones_f, ones_f[:, :], R, R[:, :, :].rearrange("p a b -> p (a b)"), True, True)
        psB = cx.ps()
        mm(P, psB, psB[:, 0:4], triu, triu[:, :], logf, logf[:, tile, :], True, True)
        mm(P, psB, psB[:, 4:8], ones_f, ones_f[:, :], logf, logf[:, tile, :], True, True)
        bc, _ = cx.pool("bc", 2, [128, 4], F32)
        ucol, _ = cx.pool("ucol", 2, [128, 4], F32)
        eg, _ = cx.pool("eg", 2, [128, 4], F32)
        P.op("dve", lambda h, bc=bc, psB=psB, tile=tile: h.tensor_tensor(out=bc[:, :], in0=small[:, tile, 0:4], in1=psB[:, 0:4], op=ALU.subtract), reads=[small, psB], writes=[bc])
        P.op("dve", lambda h, bc=bc, ucol=ucol, psB=psB: h.tensor_tensor(out=ucol[:, :], in0=bc[:, :], in1=psB[:, 4:8], op=ALU.add), reads=[bc, psB], writes=[ucol])
        P.op("act", lambda h, ucol=ucol: h.activation(out=ucol[:, :], in_=ucol[:, :], func=AF.Exp), reads=[ucol], writes=[ucol])
        P.op("act", lambda h, eg=eg, psB=psB: h.activation(out=eg[:, :], in_=psB[:, 4:8], func=AF.Exp), reads=[psB], writes=[eg])
        kT, k1 = cx.pool("kT", 2, [128, 8, 128], BF16)
        kM, k2 = cx.pool("kM", 2, [128, 1024], BF16)
        vM, k3 = cx.pool("vM", 2, [128, 1024], BF16)
        P.op("sp", lambda h, kM=kM, tile=tile: h.dma_start(out=kM[:, :], in_=G["kctm"][tile * 128:(tile + 1) * 128, :]), writes=[kM], dma=k2)
        P.op("sp", lambda h, vM=vM, tile=tile: h.dma_start(out=vM[:, :], in_=G["mv"][tile * 128:(tile + 1) * 128, :]), writes=[vM], dma=k3)
        if own:
            P.op("sp", lambda h, kT=kT, tile=tile: h.dma_start(out=kT[:, :, :], in_=kcv[:, :, tile * 128:(tile + 1) * 128]), writes=[kT], dma=k1)
            qT, k4 = cx.pool("qT", 2, [128, 8, 128], BF16)
            P.op("sp", lambda h, qT=qT, tile=tile: h.dma_start(out=qT[:, :, :], in_=qcv[:, :, (tile - 16) * 128:(tile - 15) * 128]), writes=[qT], dma=k4)
            Wm, _ = cx.pool("Wm", 2, [128, 4, 128], F32)
            ea, _ = cx.pool("ea", 2, [128, 4, 128], F32)
            P.op("dve", lambda h, Wm=Wm, psA=psA, bc=bc: h.tensor_tensor(out=Wm[:, :, :], in0=psA[:, :].rearrange("p (a b) -> p a b", a=4),
                                                                         in1=bc[:, :].unsqueeze(2).to_broadcast([128, 4, 128]), op=ALU.add), reads=[psA, bc], writes=[Wm])
            P.op("act", lambda h, Wm=Wm: h.activation(out=Wm[:, :, :], in_=Wm[:, :, :], func=AF.Exp), reads=[Wm], writes=[Wm])
            P.op("dve", lambda h, Wm=Wm: h.tensor_tensor(out=Wm[:, :, :], in0=Wm[:, :, :], in1=triu[:, :].unsqueeze(1).to_broadcast([128, 4, 128]), op=ALU.mult), reads=[Wm, triu], writes=[Wm])
            P.op("act", lambda h, ea=ea, psA=psA: h.activation(out=ea[:, :, :], in_=psA[:, :].rearrange("p (a b) -> p a b", a=4), func=AF.Exp), reads=[psA], writes=[ea])
        return dict(own=own, bc=bc, ucol=ucol, eg=eg, kT=kT, kM=kM, vM=vM, qT=qT, Wm=Wm, ea=ea)

    nxt_prep = prep_tile(0)
    for tile in range(32):
        cur_ = nxt_prep
        if tile + 1 < 32:
            nxt_prep = prep_tile(tile + 1)
        own, bc, ucol, eg, kT, kM, vM, qT, Wm, ea = (cur_[k_] for k_ in ("own", "bc", "ucol", "eg", "kT", "kM", "vM", "qT", "Wm", "ea"))
        for hp in range(2):
            hds = (2 * hp, 2 * hp + 1)
            st = {}
            if own:
                for hd in hds:
                    ps = cx.ps()
                    for dc in range(2):
                        mm(P, ps, ps[:, 0:128], kT, kT[:, 2 * hd + dc, :], qT, qT[:, 2 * hd + dc, :], dc == 0, dc == 1)
                    st[hd, "ps"] = ps
            for hd in hds:
                if own:
                    ps = st[hd, "ps"]
                    AT, _ = cx.pool("AT", 4, [128, 128], BF16)
                    P.op("dve", lambda h, AT=AT, ps=ps, Wm=Wm, hd=hd: h.tensor_tensor(out=AT[:, :], in0=ps[:, 0:128], in1=Wm[:, hd, :], op=ALU.mult), reads=[ps, Wm], writes=[AT])
                    qs, _ = cx.pool("qs", 4, [128, 2, 128], BF16)
                    P.op("dve", lambda h, qs=qs, qT=qT, ea=ea, hd=hd: h.tensor_tensor(out=qs[:, :, :], in0=qT[:, 2 * hd:2 * hd + 2, :], in1=ea[:, hd:hd + 1, :].to_broadcast([128, 2, 128]), op=ALU.mult),
                         reads=[qT, ea], writes=[qs])
                    st[hd, "AT"] = AT
                    st[hd, "qs"] = qs
                uk, _ = cx.pool("uk", 4, [128, 256], BF16)
                P.op("dve", lambda h, uk=uk, kM=kM, ucol=ucol, hd=hd: h.tensor_scalar(out=uk[:, :], in0=kM[:, hd * 256:(hd + 1) * 256], scalar1=ucol[:, hd:hd + 1], scalar2=None, op0=ALU.mult),
                     reads=[kM, ucol], writes=[uk])
                st[hd, "uk"] = uk
            for hd in hds:
                if own:
                    AT, qs = st[hd, "AT"], st[hd, "qs"]
                    pn = cx.ps()
                    for dch in range(2):
                        o_ = pn[:, dch * 128:(dch + 1) * 128]
                        mm(P, pn, o_, vM, vM[:, hd * 256 + dch * 128:hd * 256 + (dch + 1) * 128], AT, AT[:, :], True, False)
                        for ec in range(2):
                            mm(P, pn, o_, Cb[hd], Cb[hd][:, ec, dch * 128:(dch + 1) * 128], qs, qs[:, ec, :], False, ec == 1)
                    o_ = pn[:, 256:384]
                    mm(P, pn, o_, ones_bf, ones_bf[:, :], AT, AT[:, :], True, False)
                    for ec in range(2):
                        mm(P, pn, o_, nbb[hd], nbb[hd][:, ec, :], qs, qs[:, ec, :], False, ec == 1)
                    st[hd, "pn"] = pn
                uk = st[hd, "uk"]
                pc = cx.ps()
                pnb = cx.ps()
                for ec in range(2):
                    mm(P, pc, pc[:, ec * 256:(ec + 1) * 256], uk, uk[:, ec * 128:(ec + 1) * 128], vM, vM[:, hd * 256:(hd + 1) * 256], True, True)
                for ec in range(2):
                    mm(P, pnb, pnb[:, ec * 128:(ec + 1) * 128], uk, uk[:, ec * 128:(ec + 1) * 128], ones_bf, ones_bf[:, :], True, True)
                st[hd, "pc"] = pc
                st[hd, "pnb"] = pnb
            for hd in hds:
                if own:
                    pn = st[hd, "pn"]
                    rd, _ = cx.pool("rd", 4, [128, 128], F32)
                    P.op("act", lambda h, rd=rd, pn=pn: h.activation(out=rd[:, :], in_=pn[:, 256:384], func=AF.Abs), reads=[pn], writes=[rd])
                    P.op("dve", lambda h, rd=rd: h.tensor_scalar(out=rd[:, :], in0=rd[:, :], scalar1=1.0, scalar2=None, op0=ALU.max), reads=[rd], writes=[rd])
                    P.op("dve", lambda h, rd=rd: h.reciprocal(out=rd[:, :], in_=rd[:, :]), reads=[rd], writes=[rd])
                    ho, kh = cx.pool("ho", 4, [128, 2, 128], F32)
                    P.op("dve", lambda h, ho=ho, pn=pn, rd=rd: h.tensor_tensor(out=ho[:, :, :], in0=pn[:, 0:256].rearrange("p (a b) -> p a b", a=2),
                                                                             in1=rd[:, :].unsqueeze(1).to_broadcast([128, 2, 128]), op=ALU.mult), reads=[pn, rd], writes=[ho])
                    P.op("sp", lambda h, ho=ho, hd=hd, tile=tile: h.dma_start(out=hcv[:, 2 * hd:2 * hd + 2, (tile - 16) * 128:(tile - 15) * 128], in_=ho[:, :, :]), reads=[ho], dma=kh)
                pc, pnb = st[hd, "pc"], st[hd, "pnb"]
                Cf = C[hd][:, :, :].rearrange("p a b -> p (a b)")
                nf = nb[hd][:, :, :].rearrange("p a b -> p (a b)")
                P.op("dve", lambda h, Cf=Cf, pc=pc, eg=eg, hd=hd: h.scalar_tensor_tensor(out=Cf, in0=Cf, scalar=eg[:, hd:hd + 1], in1=pc[:, :], op0=ALU.mult, op1=ALU.add), reads=[C[hd], eg, pc], writes=[C[hd]])
                P.op("dve", lambda h, nf=nf, pnb=pnb, eg=eg, hd=hd: h.scalar_tensor_tensor(out=nf, in0=nf, scalar=eg[:, hd:hd + 1], in1=pnb[:, 0:256], op0=ALU.mult, op1=ALU.add), reads=[nb[hd], eg, pnb], writes=[nb[hd]])
                if tile == 15:
                    P.op("dve", lambda h, Cf=Cf, hd=hd: h.tensor_scalar(out=Cf, in0=Cf, scalar1=vcol[:, 0:1], scalar2=None, op0=ALU.mult), reads=[C[hd], vcol], writes=[C[hd]])
                    P.op("dve", lambda h, nf=nf, hd=hd: h.tensor_scalar(out=nf, in0=nf, scalar1=vcol[:, 0:1], scalar2=None, op0=ALU.mult), reads=[nb[hd], vcol], writes=[nb[hd]])
                P.op("act", lambda h, hd=hd: h.activation(out=Cb[hd][:, :, :], in_=C[hd][:, :, :], func=AF.Copy), reads=[C[hd]], writes=[Cb[hd]])
                P.op("act", lambda h, hd=hd: h.activation(out=nbb[hd][:, :, :], in_=nb[hd][:, :, :], func=AF.Copy), reads=[nb[hd]], writes=[nbb[hd]])
    P.barrier()
    mov = G["mo"].rearrange("(c p) t -> p c t", p=128)
    hmv = G["hm"].rearrange("(c p) t -> p c t", p=128)
    mg = G["mgain_sb"]
    for t4 in range(NOWN // TT):
        for hd in range(4):
            hc_, k1 = cx.pool("hcl", 2, [128, 2, TT], F32)
            mo_, k2 = cx.pool("mol", 2, [128, 2, TT], F32)
            P.op("sp", lambda h, hc_=hc_, hd=hd, t4=t4: h.dma_start(out=hc_[:, :, :], in_=hcv[:, 2 * hd:2 * hd + 2, t4 * TT:(t4 + 1) * TT]), writes=[hc_], dma=k1)
            P.op("sp", lambda h, mo_=mo_, hd=hd, t4=t4: h.dma_start(out=mo_[:, :, :], in_=mov[:, 2 * hd:2 * hd + 2, t4 * TT:(t4 + 1) * TT]), writes=[mo_], dma=k2)
            ss = cx.ps()
            for dc in range(2):
                sq, _ = cx.pool("sq", 3, [128, TT], BF16)
                P.op("act", lambda h, sq=sq, hc_=hc_, dc=dc: h.activation(out=sq[:, :], in_=hc_[:, dc, :], func=AF.Square), reads=[hc_], writes=[sq])
                mm(P, ss, ss[:, :], ones_bf, ones_bf[:, :], sq, sq[:, :], dc == 0, dc == 1)
            rs, _ = cx.pool("rstd", 2, [128, TT], F32)
            P.op("dve", lambda h, ss=ss, rs=rs: h.tensor_scalar(out=rs[:, :], in0=ss[:, :], scalar1=1.0 / 256, scalar2=EPS, op0=ALU.mult, op1=ALU.add), reads=[ss], writes=[rs])
            P.op("act", lambda h, rs=rs: h.activation(out=rs[:, :], in_=rs[:, :], func=AF.Sqrt), reads=[rs], writes=[rs])
            P.op("dve", lambda h, rs=rs: h.reciprocal(out=rs[:, :], in_=rs[:, :]), reads=[rs], writes=[rs])
            ob, ko = cx.pool("hmo", 2, [128, 2, TT], BF16)
            for dc in range(2):
                P.op("dve", lambda h, hc_=hc_, rs=rs, dc=dc, hd=hd: h.scalar_tensor_tensor(out=hc_[:, dc, :], in0=hc_[:, dc, :], scalar=mg[:, 2 * hd + dc:2 * hd + dc + 1], in1=rs[:, :], op0=ALU.mult, op1=ALU.mult),
                     reads=[hc_, rs, mg], writes=[hc_])
            P.op("dve", lambda h, hc_=hc_, mo_=mo_, ob=ob: h.tensor_tensor(out=ob[:, :, :], in0=hc_[:, :, :], in1=mo_[:, :, :], op=ALU.mult), reads=[hc_, mo_], writes=[ob])
            P.op("sp", lambda h, ob=ob, hd=hd, t4=t4: h.dma_start(out=hmv[:, 2 * hd:2 * hd + 2, t4 * TT:(t4 + 1) * TT], in_=ob[:, :, :]), reads=[ob], dma=ko)
    cx.end()


def merge_stage(cx, G):
    P = cx.P
    cx.begin(8)
    onv = G["on"].rearrange("(c p) t -> p c t", p=128)
    hmv = G["hm"].rearrange("(c p) t -> p c t", p=128)
    mgv = G["mg"].rearrange("(c p) t -> p c t", p=128)
    x1v = G["x1T"].rearrange("(c p) t -> p c t", p=128)
    x2v = G["x2T"].rearrange("(c p) t -> p c t", p=128)
    for t4 in range(NOWN // TS):
        c0 = t4 * TS
        a_, k1 = cx.pool("mon", 1, [128, 8, TS], BF16)
        b_, k2 = cx.pool("mhm", 1, [128, 8, TS], BF16)
        for hf in range(2):
            P.op("sp", lambda h, a_=a_, c0=c0, hf=hf: h.dma_start(out=a_[:, 4 * hf:4 * hf + 4, :], in_=onv[:, 4 * hf:4 * hf + 4, c0:c0 + TS]), writes=[a_], dma=k1)
            P.op("sp", lambda h, b_=b_, c0=c0, hf=hf: h.dma_start(out=b_[:, 4 * hf:4 * hf + 4, :], in_=hmv[:, 4 * hf:4 * hf + 4, c0:c0 + TS]), writes=[b_], dma=k2)
        mT, _ = cx.pool("mT", 1, [128, KC, TS], BF16)
        for oc in range(KC):
            wa, ka = cx.pool("wB", 4, [128, 8 * 128], BF16)
            P.op("pool", lambda h, wa=wa, oc=oc: h.dma_start(out=wa[:, :], in_=G["wbn"][oc, :, :], max_dma_last_dim=4096), writes=[wa], dma=ka)
            wb, kb = cx.pool("wB", 4, [128, 8 * 128], BF16)
            P.op("pool", lambda h, wb=wb, oc=oc: h.dma_start(out=wb[:, :], in_=G["wbm"][oc, :, :], max_dma_last_dim=4096), writes=[wb], dma=kb)
            gA, kga = cx.pool("gA", 3, [128, TS], F32)
            gB, kgb = cx.pool("gB", 3, [128, TS], F32)
            P.op("sp", lambda h, gA=gA, oc=oc, c0=c0: h.dma_start(out=gA[:, :], in_=mgv[:, oc, c0:c0 + TS]), writes=[gA], dma=kga)
            P.op("sp", lambda h, gB=gB, oc=oc, c0=c0: h.dma_start(out=gB[:, :], in_=mgv[:, KC + oc, c0:c0 + TS]), writes=[gB], dma=kgb)
            pa = [cx.ps() for _ in range(NS)]
            pb = [cx.ps() for _ in range(NS)]
            for c in range(8):
                for s_ in range(NS):
                    mm(P, pa[s_], pa[s_][:, :], wa, wa[:, c * 128:(c + 1) * 128], a_, a_[:, c, s_ * TT:(s_ + 1) * TT], c == 0, c == 7)
            for c in range(8):
                for s_ in range(NS):
                    mm(P, pb[s_], pb[s_][:, :], wb, wb[:, c * 128:(c + 1) * 128], b_, b_[:, c, s_ * TT:(s_ + 1) * TT], c == 0, c == 7)
            for s_ in range(NS):
                sl = slice(s_ * TT, (s_ + 1) * TT)
                P.op("dve", lambda h, gA=gA, pa_=pa[s_], sl=sl: h.tensor_tensor(out=gA[:, sl], in0=gA[:, sl], in1=pa_[:, :], op=ALU.mult), reads=[gA, pa[s_]], writes=[gA])
                P.op("dve", lambda h, gB=gB, pb_=pb[s_], sl=sl: h.tensor_tensor(out=gB[:, sl], in0=gB[:, sl], in1=pb_[:, :], op=ALU.mult), reads=[gB, pb[s_]], writes=[gB])
            P.op("dve", lambda h, gA=gA, gB=gB, mT=mT, oc=oc: h.tensor_tensor(out=mT[:, oc, :], in0=gA[:, :], in1=gB[:, :], op=ALU.add), reads=[gA, gB], writes=[mT])
        for oc in range(KC):
            wo, kw_ = cx.pool("wA", 4, [128, KC * 128], BF16)
            P.op("pool", lambda h, wo=wo, oc=oc: h.dma_start(out=wo[:, :], in_=G["wo"][oc, :, :], max_dma_last_dim=8192), writes=[wo], dma=kw_)
            xt, kx = cx.pool("xt", 4, [128, TS], F32)
            P.op("sp", lambda h, xt=xt, oc=oc, c0=c0: h.dma_start(out=xt[:, :], in_=x1v[:, oc, NOWN + c0:NOWN + c0 + TS]), writes=[xt], dma=kx + "l")
            po = [cx.ps() for _ in range(NS)]
            for c in range(KC):
                for s_ in range(NS):
                    mm(P, po[s_], po[s_][:, :], wo, wo[:, c * 128:(c + 1) * 128], mT, mT[:, c, s_ * TT:(s_ + 1) * TT], c == 0, c == KC - 1)
            for s_ in range(NS):
                sl = slice(s_ * TT, (s_ + 1) * TT)
                P.op("dve", lambda h, xt=xt, po_=po[s_], sl=sl: h.tensor_tensor(out=xt[:, sl], in0=xt[:, sl], in1=po_[:, :], op=ALU.add), reads=[xt, po[s_]], writes=[xt])
            P.op("sp", lambda h, xt=xt, oc=oc, c0=c0: h.dma_start(out=x2v[:, oc, c0:c0 + TS], in_=xt[:, :]), reads=[xt], dma=kx + "s")
    cx.end()


def build(stage="full"):
    nc = bass.Bass("TRN2", target_bir_lowering=False)

    def din(name, shape, dt=F32):
        return nc.dram_tensor(name, list(shape), dt, kind="ExternalInput").ap()

    def scr(name, shape, dt=F32):
        return nc.dram_tensor(name, list(shape), dt, kind="Internal").ap()

    G = {}
    xT = din("xT", [D, SEQV])
    g1 = din("g1", [128, KC]); g2 = din("g2", [128, KC]); gmix = din("gmix", [128, KC])
    wg1 = din("wg1", [FC, 128, D]); wu1 = din("wu1", [FC, 128, D]); wd1 = din("wd1", [KC, 128, DFF])
    wg2 = din("wg2", [FC, 128, D]); wu2 = din("wu2", [FC, 128, D]); wd2 = din("wd2", [KC, 128, DFF])
    G["wi"] = din("wi", [NT_IN, 128, D]); bi = din("bi", [128, NT_IN])
    G["w1"] = din("w1", [2, 64, 32 * 256]); G["w2"] = din("w2", [2, 128, 128]); G["posT"] = din("posT", [2, 64, 32])
    G["wbn"] = din("wbn", [KC, 128, 1024]); G["wbm"] = din("wbm", [KC, 128, 1024]); G["wo"] = din("wo", [KC, 128, D])
    cbf = din("cbf", [128, 3 * 128], BF16)
    cf = din("cf", [128, 3 * 128 + 64 + 2 * 64])
    gk = din("gk", [128, 4]); mgain = din("mgain", [128, 8]); cw = din("cw", [128, 16 * 4]); cb = din("cb", [128, 16])
    pc = din("pc", [128, 3])
    G["c_cpen"] = din("c_cpen", [128, 4, 512], BF16); G["c_wpen"] = din("c_wpen", [128, 8, 512], BF16)
    G["c_cmpp"] = din("c_cmpp", [128, 8, 512], BF16); G["c_Eexp"] = din("c_Eexp", [64, SEQV], BF16)
    G["c_mulm"] = din("c_mulm", [128, 16, 64]); G["c_addm"] = din("c_addm", [128, 16, 64])
    out = nc.dram_tensor("out", [D, NOWN], F32, kind="ExternalOutput").ap()

    G["x1T"] = scr("x1T", [D, SEQV]); G["x2T"] = scr("x2T", [D, NOWN])
    G["kvc"] = scr("kvc", [512, SEQV], BF16); G["ks"] = scr("ks", [256, SEQV], BF16); G["kw"] = scr("kw", [256, SEQV], BF16)
    G["v"] = scr("vv", [SEQV, 2, 256], BF16); G["mqr"] = scr("mqr", [1024, SEQV]); G["mkr"] = scr("mkr", [1024, SEQV])
    G["mv"] = scr("mv", [SEQV, 1024], BF16); G["small"] = scr("small", [SEQV, 128])
    G["qn"] = scr("qn", [1024, NOWN], BF16); G["mo"] = scr("mo", [1024, NOWN]); G["mg"] = scr("mg", [4096, NOWN])
    G["on"] = scr("on", [1024, NOWN], BF16); G["hm"] = scr("hm", [1024, NOWN], BF16); G["hc"] = scr("hc", [1024, NOWN])
    G["qc"] = scr("qc", [1024, NOWN], BF16); G["kc"] = scr("kc", [1024, SEQV], BF16); G["kctm"] = scr("kctm", [SEQV, 1024], BF16)

    cx = Ctx(nc)
    P = cx.P
    cbf_sb = cx.gsb([128, 3 * 128], BF16, "cbf_sb")
    cf_sb = cx.gsb([128, 3 * 128 + 64 + 128], F32, "cf_sb")
    small_sb = {}
    for nm, src, shp in (("g1", g1, [128, KC]), ("g2", g2, [128, KC]), ("gmix", gmix, [128, KC]), ("bi", bi, [128, NT_IN]), ("gk", gk, [128, 4]),
                         ("mgain", mgain, [128, 8]), ("cw", cw, [128, 64]), ("cb", cb, [128, 16]), ("pc", pc, [128, 3])):
        t = cx.gsb(shp, F32, nm + "_sb")
        P.op("sp", lambda h, t=t, src=src: h.dma_start(out=t.ap, in_=src), writes=[t], dma="c0")
        small_sb[nm] = t
    P.op("sp", lambda h: h.dma_start(out=cbf_sb[:, :], in_=cbf[:, :]), writes=[cbf_sb], dma="c0")
    P.op("sp", lambda h: h.dma_start(out=cf_sb[:, :], in_=cf[:, :]), writes=[cf_sb], dma="c0")
    P.barrier()

    class V(T):
        __slots__ = ("parent",)

        def __init__(self, parent, ap):
            self.parent = parent
            self.ap = ap
            self.name = parent.name

        w = property(lambda s: s.parent.w, lambda s, v: setattr(s.parent, "w", v))
        r = property(lambda s: s.parent.r, lambda s, v: setattr(s.parent, "r", v))

    G["ones_bf"] = V(cbf_sb, cbf_sb[:, 0:128]); G["identb"] = V(cbf_sb, cbf_sb[:, 128:256]); G["bd64"] = V(cbf_sb, cbf_sb[:, 256:384])
    G["identf"] = V(cf_sb, cf_sb[:, 0:128]); G["triu"] = V(cf_sb, cf_sb[:, 128:256]); G["ones_f"] = V(cf_sb, cf_sb[:, 256:384])
    G["ov"] = V(cf_sb, cf_sb[:, 448:576].rearrange("p (a b) -> p a b", a=2))
    G["bi_sb"] = small_sb["bi"]; G["gk_sb"] = small_sb["gk"]; G["gmix_sb"] = small_sb["gmix"]; G["mgain_sb"] = small_sb["mgain"]
    G["cw_sb"] = V(small_sb["cw"], small_sb["cw"][:, :].rearrange("p (a b) -> p a b", b=4)); G["cb_sb"] = small_sb["cb"]
    G["vcol"] = V(small_sb["pc"], small_sb["pc"][:, 0:1]); G["validc"] = V(small_sb["pc"], small_sb["pc"][:, 1:3])
    G["KcT"] = cx.gsb([64, 4, 256], BF16, "KcT"); G["Vc"] = cx.gsb([128, 4, 2, 129], BF16, "Vc")

    all_tiles = [i * TS for i in range(SEQV // TS)]
    ffn_stage(cx, xT, G["x1T"], all_tiles, small_sb["g1"], wg1, wu1, wd1, G["ones_bf"], "f1")
    inproj_stage(cx, G)
    cmp_stage(cx, G)
    nsa_stage(cx, G)
    mlstm_stage(cx, G)
    merge_stage(cx, G)
    ffn_stage(cx, G["x2T"], out, [i * TS for i in range(NOWN // TS)], small_sb["g2"], wg2, wu2, wd2, G["ones_bf"], "f2")
    P.emit()
    return nc


def tile_w(w, kc):
    K, N = w.shape
    return np.ascontiguousarray(w.reshape(kc, 128, N // 128, 128).transpose(2, 1, 0, 3).reshape(N // 128, 128, kc * 128))


def colvec(v, n):
    return np.ascontiguousarray(np.asarray(v, np.float32).reshape(n, 128).T)


_cache = {}


def static_consts():
    bf = ml_dtypes.bfloat16
    p = np.arange(128)
    col = np.arange(512)
    ones = np.ones((128, 128), np.float32)
    ident = np.eye(128, dtype=np.float32)
    bd = (p[:, None] // 64 == p[None, :] // 64).astype(np.float32)
    triu = (p[:, None] <= p[None, :]).astype(np.float32)
    c = {}
    c["cbf"] = np.concatenate([ones, ident, bd], axis=1).astype(bf)
    ov = np.zeros((128, 2, 64), np.float32)
    for ct in range(2):
        cc = ct * 128 + p
        n = np.arange(64)
        ov[:, ct, :] = ((16 * cc[:, None] < 64 * n[None, :] + 64) & (16 * cc[:, None] + 32 > 64 * n[None, :])).astype(np.float32)
    c["cf"] = np.concatenate([ident, triu, ones, np.zeros((128, 64), np.float32), ov.reshape(128, 128)], axis=1).astype(np.float32)
    cpen = np.zeros((128, 4, 512), np.float32)
    for rel in range(4):
        cpen[:, rel, :] = np.where(128 * rel + p[:, None] > col[None, :], 0.0, 1.0)
    c["c_cpen"] = cpen.astype(bf)
    wpen = np.zeros((128, 8, 512), np.float32)
    for rel in range(8):
        kp = -512 + 128 * rel + p[:, None]
        ok = (kp <= col[None, :]) & (kp > col[None, :] - 512)
        wpen[:, rel, :] = np.where(ok, 1.0, 0.0)
    c["c_wpen"] = wpen.astype(bf)
    cmpp = np.zeros((128, 8, 512), np.float32)
    for qt in range(4):
        for ct in range(2):
            cc = ct * 128 + p[:, None]
            qv = NOWN + qt * 512 + col[None, :]
            cmpp[:, qt * 2 + ct, :] = np.where(16 * cc + 31 <= qv, 0.0, NEGB)
    c["c_cmpp"] = cmpp.astype(bf)
    kk = np.arange(SEQV)
    c["c_Eexp"] = (kk[None, :] // 64 == np.arange(64)[:, None]).astype(np.float32).astype(bf)
    return c


def percore_consts(hh):
    v = float(hh)
    pc = np.zeros((128, 3), np.float32)
    pc[:, 0] = v
    pc[:, 1] = v
    pc[:, 2] = 1.0
    mulm = np.zeros((128, 16, 64), np.float32)
    addm = np.zeros((128, 16, 64), np.float32)
    p = np.arange(128)
    n = np.arange(64)
    for qs in range(16):
        tv = NOWN + qs * 128 + p
        if hh == 1:
            tr = tv
            nr = n
        else:
            tr = tv - NOWN
            nr = n - 32
        cur = tr // 64
        real = nr[None, :] >= 0
        forced = real & ((nr[None, :] == 0) | (nr[None, :] == cur[:, None]) | (nr[None, :] == cur[:, None] - 1))
        causal = real & (nr[None, :] * 64 <= tr[:, None])
        mulm[:, qs, :] = (causal & ~forced).astype(np.float32)
        addm[:, qs, :] = np.where(forced, 1e4, np.where(causal, 0.0, -1.0))
    return {"pc": pc, "c_mulm": mulm, "c_addm": addm}


def prep_inputs(inp):
    f = lambda k: np.asarray(inp[k], np.float32)[0]
    x = np.asarray(inp["x"], np.float32)
    w_in = f("w_in")
    b_in = f("b_in")
    KVB = 1024
    cols = []
    for kvidx in (0, 1, 2, 4, 3, 5):
        cols.append(np.arange(KVB + kvidx * 256, KVB + (kvidx + 1) * 256))
    MQ = 2608
    cols.append(np.arange(MQ, MQ + 1024)); cols.append(np.arange(MQ + 1024, MQ + 2048)); cols.append(np.arange(MQ + 2048, MQ + 3072))
    small_cols = np.concatenate([np.arange(5680, 5684), np.arange(5684, 5688), np.arange(2560, 2608)])
    cols_all = np.concatenate(cols)
    cols_own = np.concatenate([np.arange(0, 1024), np.arange(5688, 6712), np.arange(6712, 10808)])
    w_small = np.zeros((D, 128), np.float32); w_small[:, :56] = w_in[:, small_cols]
    b_small = np.zeros((128,), np.float32); b_small[:56] = b_in[small_cols]
    w_perm = np.concatenate([w_in[:, cols_all], w_small, w_in[:, cols_own]], axis=1)
    b_perm = np.concatenate([b_in[cols_all], b_small, b_in[cols_own]])
    assert w_perm.shape[1] == NT_IN * 128
    gk = np.stack([np.tile(f("nsa_ks_gain"), 2), np.tile(f("nsa_kw_gain"), 2), np.tile(f("nsa_q_gain"), 2), np.tile(f("nsa_kc_gain"), 2)], axis=1)
    w1 = np.stack([f("cmp_w1_k").reshape(32, 64, 256).transpose(1, 0, 2).reshape(64, 32 * 256),
                   f("cmp_w1_v").reshape(32, 64, 256).transpose(1, 0, 2).reshape(64, 32 * 256)])
    w2 = np.stack([f("cmp_w2_k").reshape(2, 128, 64).transpose(1, 0, 2).reshape(128, 128),
                   f("cmp_w2_v").reshape(2, 128, 64).transpose(1, 0, 2).reshape(128, 128)])
    posT = np.stack([f("cmp_pos_k").T, f("cmp_pos_v").T])
    cwv = f("m_conv_w")
    cw = np.ascontiguousarray(cwv.reshape(4, 16, 128).transpose(2, 1, 0).reshape(128, 64))
    common = {
        "g1": colvec(f("ffn1_norm"), KC), "g2": colvec(f("ffn2_norm"), KC), "gmix": colvec(f("mix_norm"), KC),
        "wg1": tile_w(f("ffn1_w_gate"), KC), "wu1": tile_w(f("ffn1_w_up"), KC), "wd1": tile_w(f("ffn1_w_down"), FC),
        "wg2": tile_w(f("ffn2_w_gate"), KC), "wu2": tile_w(f("ffn2_w_up"), KC), "wd2": tile_w(f("ffn2_w_down"), FC),
        "wi": tile_w(w_perm, KC), "bi": colvec(b_perm, NT_IN),
        "w1": np.ascontiguousarray(w1), "w2": np.ascontiguousarray(w2), "posT": np.ascontiguousarray(posT),
        "wbn": tile_w(f("w_branch_nsa"), 8), "wbm": tile_w(f("w_branch_mlstm"), 8), "wo": tile_w(f("w_out"), KC),
        "gk": np.ascontiguousarray(gk.astype(np.float32)), "mgain": colvec(f("m_out_gain").reshape(-1), 8),
        "cw": cw, "cb": colvec(f("m_conv_b"), 16),
    }
    common.update(static_consts())
    pcs = [percore_consts(0), percore_consts(1)]
    maps = []
    for c in range(8):
        b, hh = c // 2, c % 2
        m = dict(common)
        m.update(pcs[hh])
        m["xT"] = np.ascontiguousarray(np.concatenate([x[b, 0:NOWN].T, x[b, NOWN * hh:NOWN * hh + NOWN].T], axis=1))
        maps.append(m)
    return maps


def kernel(**inp):
    if "nc" not in _cache:
        _cache["nc"] = build()
    nc = _cache["nc"]
    maps = prep_inputs(inp)
    res = run_bass_kernel_spmd(nc, maps, core_ids=list(range(8)))
    outp = np.empty((4, 4096, D), np.float32)
    for c in range(8):
        b, hh = c // 2, c % 2
        outp[b, NOWN * hh:NOWN * hh + NOWN, :] = res.results[c]["out"].T
    return outp
```

```python
import numpy as np
import ml_dtypes
import concourse.bass as bass
import concourse.mybir as mybir
from concourse.bass_utils import run_bass_kernel_spmd
from contextlib import ExitStack

F32 = mybir.dt.float32
BF16 = mybir.dt.bfloat16
AF = mybir.ActivationFunctionType
ALU = mybir.AluOpType
AX = mybir.AxisListType

D = 2048
DFF = 5632
KC = D // 128
FC = DFF // 128
TT = 512
SEQV = 4096
NOWN = 2048
EPS = 1e-6


class T:
    __slots__ = ("ap", "name", "w", "r")

    def __init__(self, ap, name=""):
        self.ap = ap
        self.name = name
        self.w = None
        self.r = []

    def __getitem__(self, idx):
        return self.ap[idx]


class Prog:
    ENGS = ("pe", "act", "dve", "pool", "sp")

    def __init__(self, nc):
        self.nc = nc
        self.ops = []
        self.last = {}
        self.dmas = []
        self.pending = {e: set() for e in self.ENGS}

    def barrier(self):
        deps = set(self.last.values()) | set(self.dmas)
        self.dmas = []
        for e in self.ENGS:
            self.pending[e] |= deps

    def op(self, eng, fn, reads=(), writes=(), dma=None):
        i = len(self.ops)
        deps = set(self.pending[eng])
        self.pending[eng] = set()
        self.last[eng] = i
        if dma is not None:
            self.dmas.append(i)
        for t in reads:
            if t.w is not None:
                deps.add(t.w)
        for t in writes:
            if t.w is not None:
                deps.add(t.w)
            deps.update(t.r)
        for t in reads:
            t.r.append(i)
        for t in writes:
            t.w = i
            t.r = []
        d2 = set()
        for d in deps:
            o = self.ops[d]
            if o["dma"] is None and o["eng"] == eng and eng == "pe":
                continue
            d2.add(d)
            o["needed"] = True
        self.ops.append(dict(eng=eng, fn=fn, deps=d2, dma=dma, needed=False, ev=None))
        return i

    def emit(self):
        nc = self.nc
        cnt = {}
        for o in self.ops:
            if o["dma"] is not None:
                k = "dma_" + o["dma"]
                cnt[k] = cnt.get(k, 0) + 16
                o["ev"] = (k, cnt[k])
            elif o["needed"]:
                k = "eng_" + o["eng"]
                cnt[k] = cnt.get(k, 0) + 1
                o["ev"] = (k, cnt[k])
        keys = sorted(cnt.keys())
        self.maxvals = dict(cnt)
        sems = {k: nc.alloc_semaphore(name=k) for k in keys}
        per_eng = {e: [] for e in self.ENGS}
        for o in self.ops:
            per_eng[o["eng"]].append(o)
        handles = {"pe": "tensor", "act": "scalar", "dve": "vector", "pool": "gpsimd", "sp": "sync"}
        ops = self.ops
        with nc.Block() as block:
            for e in self.ENGS:
                lst = per_eng[e]

                def body(h, lst=lst, e=e):
                    seen = {}
                    for o in lst:
                        need = {}
                        for d in o["deps"]:
                            k, v = ops[d]["ev"]
                            if seen.get(k, 0) >= v:
                                continue
                            if need.get(k, 0) < v:
                                need[k] = v
                        for k, v in need.items():
                            h.wait_ge(sems[k], v)
                            seen[k] = v
                        ins = o["fn"](h)
                        if o["ev"] is not None:
                            ins.then_inc(sems[o["ev"][0]], 16 if o["dma"] is not None else 1)
                    if e == "sp":
                        for k in keys:
                            if k.startswith("dma_"):
                                h.wait_ge(sems[k], cnt[k])
                getattr(block, handles[e])(body)
        return sems


class Ctx:
    def __init__(self, nc):
        self.nc = nc
        self.P = Prog(nc)
        self.n = 0
        self.psum = [T(nc.alloc_psum_tensor("ps%d" % i, [128, 512], F32).ap(), "ps%d" % i) for i in range(8)]
        self.psi = 0
        self.ps_mod = 8
        self.pools = {}
        self.stack = None

    def begin(self, ps_mod=8):
        self.stack = ExitStack()
        self.pools = {}
        self.ps_mod = ps_mod

    def end(self):
        self.P.barrier()
        self.stack.close()
        self.stack = None
        self.pools = {}

    def gsb(self, shape, dt, name):
        return T(self.nc.alloc_sbuf_tensor(name, list(shape), dt).ap(), name)

    def sb(self, shape, dt, name=None):
        self.n += 1
        name = "%s_%d" % (name or "t", self.n)
        h = self.stack.enter_context(self.nc.sbuf_tensor(name, list(shape), dt))
        return T(h.ap(), name)

    def ps(self):
        t = self.psum[self.psi % self.ps_mod]
        self.psi += 1
        return t

    def pool(self, key, n, shape, dt):
        if key not in self.pools:
            self.pools[key] = [[self.sb(shape, dt, "%s%d" % (key, i)) for i in range(n)], 0]
        p = self.pools[key]
        t = p[0][p[1] % n]
        idx = p[1] % n
        p[1] += 1
        return t, "%s%d" % (key, idx)


NS = 2
TS = NS * TT


class NormJob:
    def __init__(self, cx, src_dram, t0, gain_sb, ones_bf, xn, ss=None):
        self.cx, self.t0, self.gain_sb, self.ones_bf, self.xn = cx, t0, gain_sb, ones_bf, xn
        self.src_v = src_dram.rearrange("(c p) t -> p c t", p=128)
        self.ss = ss if ss is not None else [cx.psum[6], cx.psum[7]]

    def chunk(self, c):
        cx, P, t0, src_v, ss, ones_bf = self.cx, self.cx.P, self.t0, self.src_v, self.ss, self.ones_bf
        xc, kx = cx.pool("xc", 3, [128, TS], F32)
        P.op("sp", lambda h, c=c, xc=xc: h.dma_start(out=xc[:, :], in_=src_v[:, c, t0:t0 + TS]), writes=[xc], dma=kx)
        sq, _ = cx.pool("sq", 2, [128, TS], BF16)
        P.op("act", lambda h, xc=xc, sq=sq: h.activation(out=sq[:, :], in_=xc[:, :], func=AF.Square), reads=[xc], writes=[sq])
        for s_ in range(NS):
            mm(P, ss[s_], ss[s_][:, :], ones_bf, ones_bf[:, :], sq, sq[:, s_ * TT:(s_ + 1) * TT], c == 0, c == KC - 1)

    def finish(self):
        cx, P, t0, src_v, ss, xn, gain_sb = self.cx, self.cx.P, self.t0, self.src_v, self.ss, self.xn, self.gain_sb
        rstd, _ = cx.pool("rstdL", 2, [128, TS], F32)
        for s_ in range(NS):
            P.op("dve", lambda h, s_=s_: h.tensor_scalar(out=rstd[:, s_ * TT:(s_ + 1) * TT], in0=ss[s_][:, :], scalar1=1.0 / D, scalar2=EPS, op0=ALU.mult, op1=ALU.add),
                 reads=[ss[s_]], writes=[rstd])
        P.op("act", lambda h: h.activation(out=rstd[:, :], in_=rstd[:, :], func=AF.Sqrt), reads=[rstd], writes=[rstd])
        P.op("dve", lambda h: h.reciprocal(out=rstd[:, :], in_=rstd[:, :]), reads=[rstd], writes=[rstd])
        for c in range(KC):
            xc, kx = cx.pool("xc", 3, [128, TS], F32)
            P.op("sp", lambda h, c=c, xc=xc: h.dma_start(out=xc[:, :], in_=src_v[:, c, t0:t0 + TS]), writes=[xc], dma=kx)
            P.op("dve", lambda h, c=c, xc=xc: h.scalar_tensor_tensor(out=xn[:, c, :], in0=xc[:, :], scalar=gain_sb[:, c:c + 1], in1=rstd[:, :], op0=ALU.mult, op1=ALU.mult),
                 reads=[xc, rstd, gain_sb], writes=[xn])


def ffn_stage(cx, src_dram, dst_dram, tiles, gain_sb, wg, wu, wd, ones_bf, tag, dst_off=0):
    P = cx.P
    cx.begin(6)
    xn = cx.sb([128, KC, TS], BF16, "xn")
    hbuf = cx.sb([128, FC, TS], BF16, "hbuf")
    src_v = src_dram.rearrange("(c p) t -> p c t", p=128)
    dst_v = dst_dram.rearrange("(c p) t -> p c t", p=128)
    nj = NormJob(cx, src_dram, tiles[0], gain_sb, ones_bf, xn)
    for c in range(KC):
        nj.chunk(c)
    nj.finish()
    for ti_, t0 in enumerate(tiles):
        nxt = NormJob(cx, src_dram, tiles[ti_ + 1], gain_sb, ones_bf, xn) if ti_ + 1 < len(tiles) else None
        for j in range(FC):
            if nxt is not None and 20 <= j < 20 + KC:
                nxt.chunk(j - 20)
            wgs, kg = cx.pool("wA", 4, [128, KC * 128], BF16)
            P.op("pool", lambda h, j=j, wgs=wgs: h.dma_start(out=wgs[:, :], in_=wg[j, :, :], max_dma_last_dim=8192), writes=[wgs], dma=kg)
            wus, ku = cx.pool("wA", 4, [128, KC * 128], BF16)
            P.op("pool", lambda h, j=j, wus=wus: h.dma_start(out=wus[:, :], in_=wu[j, :, :], max_dma_last_dim=8192), writes=[wus], dma=ku)
            pg = [cx.ps() for _ in range(NS)]
            pu = [cx.ps() for _ in range(NS)]
            for c in range(KC):
                for s_ in range(NS):
                    mm(P, pg[s_], pg[s_][:, :], wgs, wgs[:, c * 128:(c + 1) * 128], xn, xn[:, c, s_ * TT:(s_ + 1) * TT], c == 0, c == KC - 1)
            for c in range(KC):
                for s_ in range(NS):
                    mm(P, pu[s_], pu[s_][:, :], wus, wus[:, c * 128:(c + 1) * 128], xn, xn[:, c, s_ * TT:(s_ + 1) * TT], c == 0, c == KC - 1)
            for s_ in range(NS):
                sg, _ = cx.pool("sg", 3, [128, TT], BF16)
                P.op("act", lambda h, pg_=pg[s_], sg=sg: h.activation(out=sg[:, :], in_=pg_[:, :], func=AF.Silu), reads=[pg[s_]], writes=[sg])
                P.op("dve", lambda h, j=j, sg=sg, pu_=pu[s_], s_=s_: h.tensor_tensor(out=hbuf[:, j, s_ * TT:(s_ + 1) * TT], in0=sg[:, :], in1=pu_[:, :], op=ALU.mult),
                     reads=[sg, pu[s_]], writes=[hbuf])
        if nxt is not None:
            nxt.finish()
        for m in range(KC):
            wds, kd = cx.pool("wD", 2, [128, FC * 128], BF16)
            for q in range(4):
                f0 = q * (FC // 4) * 128
                f1 = (q + 1) * (FC // 4) * 128
                P.op("pool", lambda h, m=m, wds=wds, f0=f0, f1=f1: h.dma_start(out=wds[:, f0:f1], in_=wd[m, :, f0:f1], max_dma_last_dim=5632), writes=[wds], dma=kd)
            po = [cx.ps() for _ in range(NS)]
            for j in range(FC):
                for s_ in range(NS):
                    mm(P, po[s_], po[s_][:, :], wds, wds[:, j * 128:(j + 1) * 128], hbuf, hbuf[:, j, s_ * TT:(s_ + 1) * TT], j == 0, j == FC - 1)
            xc, kx = cx.pool("xc", 3, [128, TS], F32)
            P.op("sp", lambda h, m=m, xc=xc, t0=t0: h.dma_start(out=xc[:, :], in_=src_v[:, m, t0:t0 + TS]), writes=[xc], dma=kx)
            ot, ko = cx.pool("ot", 2, [128, TS], F32)
            for s_ in range(NS):
                P.op("dve", lambda h, po_=po[s_], ot=ot, xc=xc, s_=s_: h.scalar_tensor_tensor(out=ot[:, s_ * TT:(s_ + 1) * TT], in0=po_[:, :], scalar=0.5, in1=xc[:, s_ * TT:(s_ + 1) * TT], op0=ALU.mult, op1=ALU.add),
                     reads=[po[s_], xc], writes=[ot])
            P.op("sp", lambda h, m=m, ot=ot, t0=t0: h.dma_start(out=dst_v[:, m, t0 + dst_off:t0 + dst_off + TS], in_=ot[:, :]), reads=[ot], dma=ko)
    cx.end()


def mm(P, out_t, out_ap, lhsT_t, lhsT_ap, rhs_t, rhs_ap, start, stop):
    P.op("pe", lambda h: h.matmul(out_ap, lhsT_ap, rhs_ap, start=start, stop=stop), reads=[lhsT_t, rhs_t], writes=[out_t])


NEGB = -30000.0
TILES_ALL = (["kcmp"] * 2 + ["vcmp"] * 2 + ["kslc"] * 2 + ["kwin"] * 2 + ["vslc"] * 2 + ["vwin"] * 2
             + ["mq"] * 8 + ["mk"] * 8 + ["mv"] * 8 + ["small"])
TILES_OWN = ["q"] * 8 + ["mo"] * 8 + ["merge"] * 32
NT_ALL = len(TILES_ALL)
NT_IN = NT_ALL + len(TILES_OWN)


def inproj_stage(cx, G):
    P = cx.P
    cx.begin(8)
    xn = cx.sb([128, KC, TS], BF16, "xn")
    wi, bi = G["wi"], G["bi_sb"]
    ones_bf, bd64, identf, vcol = G["ones_bf"], G["bd64"], G["identf"], G["vcol"]
    gk = G["gk_sb"]
    first_idx = {}
    for j, k in enumerate(TILES_ALL + TILES_OWN):
        first_idx.setdefault(k, j)
    for ti in range(SEQV // TS):
        t0b = ti * TS
        own = t0b >= NOWN
        nj = NormJob(cx, G["x1T"], t0b, G["gmix_sb"], ones_bf, xn, ss=[cx.ps(), cx.ps()])
        for c in range(KC):
            nj.chunk(c)
        nj.finish()
        nxt = None
        plan = TILES_ALL + (TILES_OWN if own else [])
        deferred = []
        for j, kind in enumerate(plan):
            if nxt is not None and 8 <= j < 8 + KC:
                nxt.chunk(j - 8)
            if nxt is not None and j == len(plan) - 1:
                pass
            k = j - first_idx[kind]
            ws, kw_ = cx.pool("wA", 6, [128, KC * 128], BF16)
            P.op("pool", lambda h, j=j, ws=ws: h.dma_start(out=ws[:, :], in_=wi[j, :, :], max_dma_last_dim=8192), writes=[ws], dma=kw_)
            pss = [cx.ps() for _ in range(NS)]
            for c in range(KC):
                for s_ in range(NS):
                    mm(P, pss[s_], pss[s_][:, :], ws, ws[:, c * 128:(c + 1) * 128], xn, xn[:, c, s_ * TT:(s_ + 1) * TT], c == 0, c == KC - 1)
            while deferred:
                deferred.pop(0)()
            for s_ in range(NS):
                ps = pss[s_]
                t0 = t0b + s_ * TT
                bcol = bi[:, j:j + 1]
                if kind in ("kcmp", "vcmp"):
                    ob, ko = cx.pool("obb", 3, [128, TT], BF16)
                    P.op("act", lambda h, ps=ps, ob=ob, bcol=bcol: h.activation(out=ob[:, :], in_=ps[:, :], func=AF.Identity, bias=bcol), reads=[ps, bi], writes=[ob])
                    r0 = (0 if kind == "kcmp" else 256) + k * 128
                    P.op("sp", lambda h, ob=ob, r0=r0, t0=t0: h.dma_start(out=G["kvc"][r0:r0 + 128, t0:t0 + TT], in_=ob[:, :]), reads=[ob], dma=ko)
                elif kind in ("mq", "mk"):
                    ob, ko = cx.pool("obf", 3, [128, TT], F32)
                    P.op("act", lambda h, ps=ps, ob=ob, bcol=bcol: h.activation(out=ob[:, :], in_=ps[:, :], func=AF.Identity, bias=bcol), reads=[ps, bi], writes=[ob])
                    dst = G["mqr"] if kind == "mq" else G["mkr"]
                    P.op("sp", lambda h, ob=ob, dst=dst, k=k, t0=t0: h.dma_start(out=dst[k * 128:(k + 1) * 128, t0:t0 + TT], in_=ob[:, :]), reads=[ob], dma=ko)
                elif kind in ("mo", "merge"):
                    ob, ko = cx.pool("obf", 3, [128, TT], F32)
                    P.op("act", lambda h, ps=ps, ob=ob, bcol=bcol: h.activation(out=ob[:, :], in_=ps[:, :], func=AF.Sigmoid, bias=bcol), reads=[ps, bi], writes=[ob])
                    dst = G["mo"] if kind == "mo" else G["mg"]
                    P.op("sp", lambda h, ob=ob, dst=dst, k=k, t0=t0: h.dma_start(out=dst[k * 128:(k + 1) * 128, t0 - NOWN:t0 - NOWN + TT], in_=ob[:, :]), reads=[ob], dma=ko)
                elif kind in ("kslc", "kwin", "q"):
                    z, _ = cx.pool("z", 5, [128, TT], F32)
                    P.op("act", lambda h, ps=ps, z=z, bcol=bcol: h.activation(out=z[:, :], in_=ps[:, :], func=AF.Identity, bias=bcol), reads=[ps, bi], writes=[z])
                    sq, _ = cx.pool("sqe", 5, [128, TT], BF16)
                    P.op("act", lambda h, z=z, sq=sq: h.activation(out=sq[:, :], in_=z[:, :], func=AF.Square), reads=[z], writes=[sq])
                    def part_b(ps=ps, z=z, t0=t0, k=k, kind=kind, j=j, own=own, sq=sq):
                        ps2 = cx.ps()
                        mm(P, ps2, ps2[:, :], bd64, bd64[:, :], sq, sq[:, :], True, True)
                        rs, _ = cx.pool("rstd", 2, [128, TT], F32)
                        P.op("dve", lambda h, ps2=ps2, rs=rs: h.tensor_scalar(out=rs[:, :], in0=ps2[:, :], scalar1=1.0 / 64, scalar2=EPS, op0=ALU.mult, op1=ALU.add), reads=[ps2], writes=[rs])
                        sc = 64.0 if kind == "q" else 1.0
                        P.op("act", lambda h, rs=rs, sc=sc: h.activation(out=rs[:, :], in_=rs[:, :], func=AF.Sqrt, scale=sc), reads=[rs], writes=[rs])
                        P.op("dve", lambda h, rs=rs: h.reciprocal(out=rs[:, :], in_=rs[:, :]), reads=[rs], writes=[rs])
                        gi = {"kslc": 0, "kwin": 1, "q": 2}[kind]
                        ob, ko = cx.pool("obb", 3, [128, TT], BF16)
                        P.op("dve", lambda h, z=z, rs=rs, ob=ob, gi=gi: h.scalar_tensor_tensor(out=ob[:, :], in0=z[:, :], scalar=gk[:, gi:gi + 1], in1=rs[:, :], op0=ALU.mult, op1=ALU.mult),
                             reads=[z, rs, gk], writes=[ob])
                        if kind == "q":
                            P.op("sp", lambda h, ob=ob, k=k, t0=t0: h.dma_start(out=G["qn"][k * 128:(k + 1) * 128, t0 - NOWN:t0 - NOWN + TT], in_=ob[:, :]), reads=[ob], dma=ko)
                        else:
                            dst = G["ks"] if kind == "kslc" else G["kw"]
                            P.op("sp", lambda h, ob=ob, dst=dst, k=k, t0=t0: h.dma_start(out=dst[k * 128:(k + 1) * 128, t0:t0 + TT], in_=ob[:, :]), reads=[ob], dma=ko)
                    deferred.append(part_b)
                else:
                    z, _ = cx.pool("z", 5, [128, TT], F32)
                    P.op("act", lambda h, ps=ps, z=z, bcol=bcol: h.activation(out=z[:, :], in_=ps[:, :], func=AF.Identity, bias=bcol), reads=[ps, bi], writes=[z])
                    def part_b(ps=ps, z=z, t0=t0, k=k, kind=kind, j=j, own=own):
                        pst = cx.ps()
                        for s in range(4):
                            P.op("pe", lambda h, pst=pst, z=z, s=s: h.transpose(out=pst[:, s * 128:(s + 1) * 128], in_=z[:, s * 128:(s + 1) * 128], identity=identf[:, :]),
                                 reads=[z, identf], writes=[pst])
                        if kind == "small":
                            ob, ko = cx.pool("otf", 2, [128, TT], F32)
                            P.op("dve", lambda h, pst=pst, ob=ob: h.tensor_copy(out=ob[:, :], in_=pst[:, :]), reads=[pst], writes=[ob])
                            dstv = G["small"].rearrange("(s p) f -> p s f", p=128)
                            P.op("sp", lambda h, ob=ob, dstv=dstv, t0=t0: h.dma_start(out=dstv[:, t0 // 128:t0 // 128 + 4, :], in_=ob[:, :].rearrange("p (s f) -> p s f", s=4)), reads=[ob], dma=ko)
                        else:
                            ob, ko = cx.pool("otb", 2, [128, TT], BF16)
                            if own or kind == "mv":
                                P.op("dve", lambda h, pst=pst, ob=ob: h.tensor_copy(out=ob[:, :], in_=pst[:, :]), reads=[pst], writes=[ob])
                            else:
                                P.op("dve", lambda h, pst=pst, ob=ob: h.tensor_scalar(out=ob[:, :], in0=pst[:, :], scalar1=vcol[:, 0:1], scalar2=None, op0=ALU.mult), reads=[pst, vcol], writes=[ob])
                            if kind == "mv":
                                dstv = G["mv"].rearrange("(s p) f -> p s f", p=128)[:, t0 // 128:t0 // 128 + 4, k * 128:(k + 1) * 128]
                            else:
                                br = 0 if kind == "vslc" else 1
                                dstv = G["v"].rearrange("(s p) b f -> p s b f", p=128)[:, t0 // 128:t0 // 128 + 4, br, k * 128:(k + 1) * 128]
                            P.op("sp", lambda h, ob=ob, dstv=dstv: h.dma_start(out=dstv, in_=ob[:, :].rearrange("p (s f) -> p s f", s=4)), reads=[ob], dma=ko)
                    deferred.append(part_b)
        while deferred:
            deferred.pop(0)()
    cx.end()


def cmp_stage(cx, G):
    P = cx.P
    cx.begin(8)
    ones_bf, validc = G["ones_bf"], G["validc"]
    KcT, Vc = G["KcT"], G["Vc"]
    for kv in range(2):
        w1 = cx.sb([64, 32 * 256], BF16, "w1")
        P.op("pool", lambda h, kv=kv, w1=w1: h.dma_start(out=w1[:, :], in_=G["w1"][kv, :, :], max_dma_last_dim=8192), writes=[w1], dma="w1")
        w2 = cx.sb([128, 2 * 64], BF16, "w2")
        P.op("pool", lambda h, kv=kv, w2=w2: h.dma_start(out=w2[:, :], in_=G["w2"][kv, :, :]), writes=[w2], dma="w2")
        posT = cx.sb([64, 32], BF16, "posT")
        P.op("pool", lambda h, kv=kv, posT=posT: h.dma_start(out=posT[:, :], in_=G["posT"][kv, :, :]), writes=[posT], dma="posT")
        b1 = cx.sb([128, 2], F32, "b1")
        for hc in range(2):
            ps = cx.ps()
            for j in range(32):
                mm(P, ps, ps[:, 0:1], w1, w1[:, j * 256 + hc * 128:j * 256 + hc * 128 + 128], posT, posT[:, j:j + 1], j == 0, j == 31)
            P.op("dve", lambda h, ps=ps, hc=hc, b1=b1: h.tensor_copy(out=b1[:, hc:hc + 1], in_=ps[:, 0:1]), reads=[ps], writes=[b1])
        for g in range(4):
            kvT, kk = cx.pool("kvT", 2, [64, SEQV + 32], BF16)
            P.op("dve", lambda h, kvT=kvT: h.memset(kvT[:, SEQV:SEQV + 32], 0.0), writes=[kvT])
            P.op("sp", lambda h, kvT=kvT, kv=kv, g=g: h.dma_start(out=kvT[:, 0:SEQV], in_=G["kvc"][kv * 256 + g * 64:kv * 256 + g * 64 + 64, :]), writes=[kvT], dma=kk)
            gT, _ = cx.pool("gT", 2, [128, 2, 256], BF16)
            for hc in range(2):
                ps = cx.ps()
                for j in range(32):
                    mm(P, ps, ps[:, 0:256], w1, w1[:, j * 256 + hc * 128:j * 256 + hc * 128 + 128], kvT, kvT[:, j:j + SEQV:16], j == 0, j == 31)
                z, _ = cx.pool("cz", 2, [128, 256], F32)
                t, _ = cx.pool("ct", 2, [128, 256], F32)
                P.op("act", lambda h, ps=ps, z=z, hc=hc, b1=b1: h.activation(out=z[:, :], in_=ps[:, 0:256], func=AF.Identity, bias=b1[:, hc:hc + 1]), reads=[ps, b1], writes=[z])
                P.op("act", lambda h, z=z, t=t: h.activation(out=t[:, :], in_=z[:, :], func=AF.Square), reads=[z], writes=[t])
                P.op("dve", lambda h, t=t: h.tensor_scalar(out=t[:, :], in0=t[:, :], scalar1=0.044715, scalar2=1.0, op0=ALU.mult, op1=ALU.add), reads=[t], writes=[t])
                P.op("dve", lambda h, t=t, z=z: h.tensor_tensor(out=t[:, :], in0=t[:, :], in1=z[:, :], op=ALU.mult), reads=[t, z], writes=[t])
                P.op("act", lambda h, t=t: h.activation(out=t[:, :], in_=t[:, :], func=AF.Sigmoid, scale=1.5957691216057308), reads=[t], writes=[t])
                P.op("dve", lambda h, t=t, z=z, gT=gT, hc=hc: h.tensor_tensor(out=gT[:, hc, :], in0=t[:, :], in1=z[:, :], op=ALU.mult), reads=[t, z], writes=[gT])
            if kv == 0:
                ps = cx.ps()
                for hc in range(2):
                    mm(P, ps, ps[0:64, 0:256], w2, w2[:, hc * 64:(hc + 1) * 64], gT, gT[:, hc, :], hc == 0, hc == 1)
                zk, _ = cx.pool("zk", 2, [64, 256], F32)
                sq, _ = cx.pool("sqk", 2, [64, 256], BF16)
                P.op("act", lambda h, ps=ps, zk=zk: h.activation(out=zk[:, :], in_=ps[0:64, 0:256], func=AF.Copy), reads=[ps], writes=[zk])
                P.op("act", lambda h, zk=zk, sq=sq: h.activation(out=sq[:, :], in_=zk[:, :], func=AF.Square), reads=[zk], writes=[sq])
                ps2 = cx.ps()
                mm(P, ps2, ps2[0:64, 0:256], ones_bf, ones_bf[0:64, 0:64], sq, sq[:, :], True, True)
                rs, _ = cx.pool("rsk", 2, [64, 256], F32)
                P.op("dve", lambda h, ps2=ps2, rs=rs: h.tensor_scalar(out=rs[:, :], in0=ps2[0:64, 0:256], scalar1=1.0 / 64, scalar2=EPS, op0=ALU.mult, op1=ALU.add), reads=[ps2], writes=[rs])
                P.op("act", lambda h, rs=rs: h.activation(out=rs[:, :], in_=rs[:, :], func=AF.Sqrt), reads=[rs], writes=[rs])
                P.op("dve", lambda h, rs=rs: h.reciprocal(out=rs[:, :], in_=rs[:, :]), reads=[rs], writes=[rs])
                P.op("dve", lambda h, zk=zk, rs=rs, g=g: h.scalar_tensor_tensor(out=KcT[:, g, :], in0=zk[:, :], scalar=G["gk_sb"][0:64, 3:4], in1=rs[:, :], op0=ALU.mult, op1=ALU.mult),
                     reads=[zk, rs, G["gk_sb"]], writes=[KcT])
            else:
                for ct in range(2):
                    ps = cx.ps()
                    for hc in range(2):
                        mm(P, ps, ps[:, 0:64], gT, gT[:, hc, ct * 128:(ct + 1) * 128], w2, w2[:, hc * 64:(hc + 1) * 64], hc == 0, hc == 1)
                    P.op("dve", lambda h, ps=ps, g=g, ct=ct: h.tensor_scalar(out=Vc[:, g, ct, 0:64], in0=ps[:, 0:64], scalar1=validc[:, ct:ct + 1], scalar2=None, op0=ALU.mult),
                         reads=[ps, validc], writes=[Vc])
                    P.op("dve", lambda h, g=g, ct=ct: h.tensor_copy(out=Vc[:, g, ct, 64:65], in_=validc[:, ct:ct + 1]), reads=[validc], writes=[Vc])
                    P.op("dve", lambda h, g=g, ct=ct: h.tensor_scalar(out=Vc[:, g, ct, 65:129], in0=G["ov"][:, ct, :], scalar1=validc[:, ct:ct + 1], scalar2=None, op0=ALU.mult),
                         reads=[validc, G["ov"]], writes=[Vc])
    cx.end()


def nsa_stage(cx, G):
    P = cx.P
    cx.begin(4)
    acc = cx.psum[4:8]
    identb, identf, vcol = G["identb"], G["identf"], G["vcol"]
    KcT, Vc = G["KcT"], G["Vc"]
    cpen = cx.sb([128, 4, 512], BF16, "cpen")
    wpen = cx.sb([128, 8, 512], BF16, "wpen")
    cmpp = cx.sb([128, 8, 512], BF16, "cmpp")
    mulm = cx.sb([128, 16, 64], F32, "mulm")
    addm = cx.sb([128, 16, 64], F32, "addm")
    sgate = cx.sb([128, 16, 48], F32, "sgate")
    for t, src in ((cpen, G["c_cpen"]), (wpen, G["c_wpen"]), (cmpp, G["c_cmpp"]), (mulm, G["c_mulm"]), (addm, G["c_addm"])):
        P.op("sp", lambda h, t=t, src=src: h.dma_start(out=t.ap, in_=src), writes=[t], dma="nc")
    P.op("sp", lambda h: h.dma_start(out=sgate[:, :, :], in_=G["small"].rearrange("(s p) f -> p s f", p=128)[:, 16:32, 8:56]), writes=[sgate], dma="nc")
    P.barrier()
    P.op("act", lambda h: h.activation(out=sgate[:, :, :], in_=sgate[:, :, :], func=AF.Sigmoid), reads=[sgate], writes=[sgate])
    vview = G["v"].rearrange("(s p) b (g d) -> p s b g d", p=128, g=4)
    cg = conv_gen(cx, G)
    jobctr = [0]

    def evac(accs, qt, g, r, branch, oacc, impa=None):
        W = 129 if branch == 0 else 65
        stg, _ = cx.pool("stg%d" % W, 3, [128, 4, W], F32)
        if branch == 0:
            for hb in range(2):
                a = accs[hb]
                P.op("dve", lambda h, a=a, stg=stg, hb=hb: h.tensor_copy(out=stg[:, 2 * hb:2 * hb + 2, :], in_=a[:, 0:2 * W].rearrange("p (s w) -> p s w", s=2)), reads=[a], writes=[stg])
        else:
            a = accs
            P.op("dve", lambda h, a=a, stg=stg: h.tensor_copy(out=stg[:, :, :], in_=a[:, 0:4 * W].rearrange("p (s w) -> p s w", s=4)), reads=[a], writes=[stg])
        rl, _ = cx.pool("rl", 4, [128, 4, 2], F32)
        gc = (g * 4 + r) * 3 + branch
        P.op("dve", lambda h, stg=stg, rl=rl: h.tensor_scalar(out=rl[:, :, 0:1], in0=stg[:, :, 64:65], scalar1=1e-30, scalar2=None, op0=ALU.max), reads=[stg], writes=[rl])
        P.op("dve", lambda h, rl=rl: h.reciprocal(out=rl[:, :, 0:1], in_=rl[:, :, 0:1]), reads=[rl], writes=[rl])
        P.op("dve", lambda h, rl=rl, qt=qt, gc=gc: h.tensor_tensor(out=rl[:, :, 1:2], in0=rl[:, :, 0:1], in1=sgate[:, qt * 4:qt * 4 + 4, gc:gc + 1], op=ALU.mult), reads=[rl, sgate], writes=[rl])
        tmp, _ = cx.pool("etmp", 3, [128, 4, 64], F32)
        P.op("dve", lambda h, stg=stg, rl=rl, tmp=tmp: h.tensor_tensor(out=tmp[:, :, :], in0=stg[:, :, 0:64], in1=rl[:, :, 1:2].to_broadcast([128, 4, 64]), op=ALU.mult), reads=[stg, rl], writes=[tmp])
        P.op("dve", lambda h, tmp=tmp, r=r, oacc=oacc: h.tensor_tensor(out=oacc[:, :, r * 64:(r + 1) * 64], in0=oacc[:, :, r * 64:(r + 1) * 64], in1=tmp[:, :, :], op=ALU.add), reads=[tmp, oacc], writes=[oacc])
        if impa is not None:
            if r == 0:
                P.op("dve", lambda h, stg=stg, rl=rl, impa=impa: h.tensor_tensor(out=impa[:, :, :], in0=stg[:, :, 65:129], in1=rl[:, :, 0:1].to_broadcast([128, 4, 64]), op=ALU.mult), reads=[stg, rl], writes=[impa])
            else:
                tm2, _ = cx.pool("etmp", 3, [128, 4, 64], F32)
                P.op("dve", lambda h, stg=stg, rl=rl, tm2=tm2: h.tensor_tensor(out=tm2[:, :, :], in0=stg[:, :, 65:129], in1=rl[:, :, 0:1].to_broadcast([128, 4, 64]), op=ALU.mult), reads=[stg, rl], writes=[tm2])
                P.op("dve", lambda h, tm2=tm2, impa=impa: h.tensor_tensor(out=impa[:, :, :], in0=impa[:, :, :], in1=tm2[:, :, :], op=ALU.add), reads=[tm2, impa], writes=[impa])

    pending_out = []
    KsEs, KsEes = [], []
    for i_ in range(2):
        t_ = cx.sb([128, SEQV], BF16, "KsE%d" % i_)
        e_ = T(t_.ap, "KsEe%d" % i_)
        P.op("sp", lambda h, t_=t_: h.dma_start(out=t_[64:128, :], in_=G["c_Eexp"]), writes=[e_], dma="nc")
        KsEs.append(t_)
        KsEes.append(e_)
    P.barrier()

    def load_group(g):
        KsT, k1 = KsEs[g % 2], "KsT%d" % (g % 2)
        KwT, k2 = cx.pool("KwT", 2, [64, SEQV], BF16)
        Vs, k3 = cx.pool("Vs", 2, [128, 32, 65], BF16)
        Vw, k4 = cx.pool("Vw", 2, [128, 32, 65], BF16)
        P.op("sp", lambda h, g=g, KsT=KsT: h.dma_start(out=KsT[0:64, :], in_=G["ks"][g * 64:(g + 1) * 64, :]), writes=[KsT], dma=k1)
        P.op("sp", lambda h, g=g, KwT=KwT: h.dma_start(out=KwT[:, :], in_=G["kw"][g * 64:(g + 1) * 64, :]), writes=[KwT], dma=k2)
        for br, V, kk in ((0, Vs, k3), (1, Vw, k4)):
            for hf in range(2):
                P.op("sp", lambda h, g=g, V=V, br=br, hf=hf: h.dma_start(out=V[:, hf * 16:(hf + 1) * 16, 0:64], in_=vview[:, hf * 16:(hf + 1) * 16, br, g, :]), writes=[V], dma=kk)
            P.op("dve", lambda h, V=V: h.tensor_copy(out=V[:, 0:16, 64:65], in_=vcol[:, 0:1].unsqueeze(1).to_broadcast([128, 16, 1])), reads=[vcol], writes=[V])
            P.op("dve", lambda h, V=V: h.memset(V[:, 16:32, 64:65], 1.0), writes=[V])
        return KsT, KsEes[g % 2], KwT, Vs, Vw

    def load_q(g, qt):
        QT, kq = cx.pool("QT", 3, [128, 4, TT], BF16)
        for r in range(4):
            P.op("sp", lambda h, QT=QT, r=r, g=g, qt=qt: h.dma_start(out=QT[0:64, r, :], in_=G["qn"][(g * 4 + r) * 64:(g * 4 + r + 1) * 64, qt * TT:(qt + 1) * TT]), writes=[QT], dma=kq)
        return QT

    qt_next = [None]
    nxt_grp = load_group(0)
    for g in range(4):
        KsE, KsEe, KwT, Vs, Vw = nxt_grp
        if g < 3:
            nxt_grp = load_group(g + 1)
        for qt in range(4):
            q0 = NOWN + qt * TT
            nkt = q0 // 128 + 4
            if qt_next[0] is None:
                qt_next[0] = load_q(g, qt)
            QT = qt_next[0]
            nb_ = g * 4 + qt + 1
            qt_next[0] = load_q(nb_ // 4, nb_ % 4) if nb_ < 16 else None
            QTp = T(QT.ap, "QTp")
            oacc, _ = cx.pool("oacc", 2, [128, 4, 256], F32)
            impa, _ = cx.pool("impa", 2, [128, 4, 64], F32)
            P.op("dve", lambda h, oacc=oacc: h.memset(oacc[:, :, :], 0.0), writes=[oacc])
            def run_jobs(jobs, depth=4, hooks=None):
                q = []
                for ji, (sc_fn, pv_fn) in enumerate(jobs):
                    if hooks and ji in hooks:
                        hooks[ji]()
                    q.append((sc_fn(), pv_fn))
                    jobctr[0] += 1
                    if jobctr[0] % 6 == 0:
                        next(cg, None)
                    if len(q) > depth:
                        pc, f_ = q.pop(0)
                        f_(pc)
                for pc, f_ in q:
                    f_(pc)

            def cmp_score(r):
                Pc = []
                for ct in range(2):
                    ps = cx.ps()
                    mm(P, ps, ps[:, :], KcT, KcT[:, g, ct * 128:(ct + 1) * 128], QT, QT[0:64, r, :], True, False)
                    mm(P, ps, ps[:, :], identb, identb[:, :], cmpp, cmpp[:, qt * 2 + ct, :], False, True)
                    pc, _ = cx.pool("Ptc", 4, [128, TT], BF16)
                    P.op("act", lambda h, ps=ps, pc=pc: h.activation(out=pc[:, :], in_=ps[:, :], func=AF.Exp), reads=[ps], writes=[pc])
                    Pc.append(pc)
                return Pc

            def cmp_pv(r, Pc):
                for sub in range(4):
                    a = acc[2 + sub // 2]
                    o0 = (sub % 2) * 129
                    for ct in range(2):
                        mm(P, a, a[:, o0:o0 + 129], Pc[ct], Pc[ct][:, sub * 128:(sub + 1) * 128], Vc, Vc[:, g, ct, :], (ct == 0 and sub % 2 == 0), (ct == 1 and sub % 2 == 1))
                evac(acc[2:4], qt, g, r, 0, oacc, impa)

            run_jobs([((lambda r=r: cmp_score(r)), (lambda Pc, r=r: cmp_pv(r, Pc))) for r in range(4)], depth=1)
            pens = []
            for sub in range(4):
                i2, _ = cx.pool("i2", 2, [128, 64], F32)
                i3, _ = cx.pool("i3", 2, [128, 64], F32)
                m8, _ = cx.pool("m8", 2, [128, 16], F32)
                penp, _ = cx.pool("pen", 4, [128, 128], F32)
                P.op("dve", lambda h, penp=penp: h.memset(penp[:, 0:64], 0.0), writes=[penp])
                pen = T(penp.ap[:, 64:128], "penv")
                pen.w, pen.r = None, []
                P.op("dve", lambda h, i2=i2, sub=sub, impa=impa, qt=qt: h.tensor_tensor(out=i2[:, :], in0=impa[:, sub, :], in1=mulm[:, qt * 4 + sub, :], op=ALU.mult), reads=[impa, mulm], writes=[i2])
                P.op("dve", lambda h, i2=i2, sub=sub, qt=qt: h.tensor_tensor(out=i2[:, :], in0=i2[:, :], in1=addm[:, qt * 4 + sub, :], op=ALU.add), reads=[i2, addm], writes=[i2])
                P.op("dve", lambda h, i2=i2, m8=m8: h.max(out=m8[:, 0:8], in_=i2[:, :]), reads=[i2], writes=[m8])
                P.op("dve", lambda h, i2=i2, i3=i3, m8=m8: h.match_replace(out=i3[:, :], in_to_replace=m8[:, 0:8], in_values=i2[:, :], imm_value=-1e30), reads=[i2, m8], writes=[i3])
                P.op("dve", lambda h, i3=i3, m8=m8: h.max(out=m8[:, 8:16], in_=i3[:, :]), reads=[i3, m8], writes=[m8])
                P.op("dve", lambda h, i2=i2, m8=m8, pen=pen: h.tensor_scalar(out=pen[:, :], in0=i2[:, :], scalar1=m8[:, 15:16], scalar2=None, op0=ALU.is_ge), reads=[i2, m8], writes=[pen])
                P.op("dve", lambda h, pen=pen: h.tensor_scalar(out=pen[:, :], in0=pen[:, :], scalar1=-1.0, scalar2=-NEGB, op0=ALU.add, op1=ALU.mult), reads=[pen], writes=[pen])
                P.op("dve", lambda h, penp=penp: h.tensor_copy(out=penp[:, 64:65], in_=penp[:, 64:65]), reads=[pen], writes=[penp])
                pens.append(penp)

            def win_score(r, rel):
                kt = q0 // 128 - 4 + rel
                ps = cx.ps()
                c0, c1 = 128 * max(0, rel - 4), 128 * (min(3, rel) + 1)
                mm(P, ps, ps[:, c0:c1], KwT, KwT[:, kt * 128:(kt + 1) * 128], QT, QT[0:64, r, c0:c1], True, True)
                pc, _ = cx.pool("Pt", 7, [128, TT], BF16)
                P.op("act", lambda h, ps=ps, pc=pc, c0=c0, c1=c1: h.activation(out=pc[:, c0:c1], in_=ps[:, c0:c1], func=AF.Exp), reads=[ps], writes=[pc])
                for s_ in (rel, rel - 4):
                    if 0 <= s_ <= 3:
                        a0, a1 = 128 * s_, 128 * (s_ + 1)
                        P.op("pool", lambda h, pc=pc, rel=rel, a0=a0, a1=a1: h.tensor_tensor(out=pc[:, a0:a1], in0=pc[:, a0:a1], in1=wpen[:, rel, a0:a1], op=ALU.mult), reads=[pc, wpen], writes=[pc])
                return pc

            def win_pv(r, rel, pc):
                kt = q0 // 128 - 4 + rel
                for sub in range(4):
                    if not (sub <= rel <= sub + 4):
                        continue
                    a = acc[r % 2]
                    mm(P, a, a[:, sub * 65:(sub + 1) * 65], pc, pc[:, sub * 128:(sub + 1) * 128], Vw, Vw[:, kt, :], (rel == 0 and sub == 0), (rel == 7 and sub == 3))
                if rel == 7:
                    evac(acc[r % 2], qt, g, r, 2, oacc)

            def emit_pen():
                for sub in range(4):
                    penp = pens[sub]
                    pst = cx.ps()
                    P.op("pe", lambda h, pst=pst, penp=penp: h.transpose(out=pst[:, 0:128], in_=penp[:, :], identity=identf[:, :]), reads=[penp, identf], writes=[pst])
                    for r in range(4):
                        P.op("dve", lambda h, pst=pst, sub=sub, QT=QT, r=r: h.tensor_copy(out=QT[64:128, r, sub * 128:(sub + 1) * 128], in_=pst[64:128, 0:128]), reads=[pst], writes=[QTp])


            def emit_prev_out():
                while pending_out:
                    pending_out.pop(0)()

            run_jobs([((lambda r=r, rel=rel: win_score(r, rel)), (lambda pc, r=r, rel=rel: win_pv(r, rel, pc))) for rp in range(2) for rel in range(8) for r in (2 * rp, 2 * rp + 1)],
                     hooks={6: emit_prev_out, 20: emit_pen})
            def slc_score(r, kt):
                rel = kt - (nkt - 4)
                ps = cx.ps()
                c0 = 128 * max(0, rel)
                P.op("pe", lambda h, ps=ps, kt=kt, r=r, QT=QT, c0=c0, KsE=KsE: h.matmul(ps[:, c0:], KsE[:, kt * 128:(kt + 1) * 128], QT[:, r, c0:], start=True, stop=True),
                     reads=[KsE, KsEe, QT, QTp], writes=[ps])
                pc, _ = cx.pool("Pt", 7, [128, TT], BF16)
                P.op("act", lambda h, ps=ps, pc=pc, c0=c0: h.activation(out=pc[:, c0:], in_=ps[:, c0:], func=AF.Exp), reads=[ps], writes=[pc])
                if rel >= 0:
                    P.op("pool", lambda h, pc=pc, rel=rel, c0=c0: h.tensor_tensor(out=pc[:, c0:c0 + 128], in0=pc[:, c0:c0 + 128], in1=cpen[:, rel, c0:c0 + 128], op=ALU.mult), reads=[pc, cpen], writes=[pc])
                return pc

            def slc_pv(r, kt, pc):
                rel = kt - (nkt - 4)
                for sub in range(4):
                    if rel > sub:
                        continue
                    a = acc[r % 2]
                    mm(P, a, a[:, sub * 65:(sub + 1) * 65], pc, pc[:, sub * 128:(sub + 1) * 128], Vs, Vs[:, kt, :], (kt == 0 and sub == 0), (kt == nkt - 1 and sub == 3))
                if kt == nkt - 1:
                    evac(acc[r % 2], qt, g, r, 1, oacc)

            run_jobs([((lambda r=r, kt=kt: slc_score(r, kt)), (lambda pc, r=r, kt=kt: slc_pv(r, kt, pc))) for rp in range(2) for kt in range(nkt) for r in (2 * rp, 2 * rp + 1)])
            def emit_out(oacc=oacc, g=g, qt=qt):
                for sub in range(4):
                    for hp in range(2):
                        pst = cx.ps()
                        P.op("pe", lambda h, pst=pst, oacc=oacc, sub=sub, hp=hp: h.transpose(out=pst[:, 0:128], in_=oacc[:, sub, hp * 128:(hp + 1) * 128], identity=identf[:, :]),
                             reads=[oacc, identf], writes=[pst])
                        ob, ko = cx.pool("onb", 3, [128, 128], BF16)
                        P.op("act", lambda h, pst=pst, ob=ob: h.activation(out=ob[:, :], in_=pst[:, 0:128], func=AF.Copy), reads=[pst], writes=[ob])
                        r0 = (g * 4 + hp * 2) * 64
                        c0 = qt * TT + sub * 128
                        P.op("sp", lambda h, ob=ob, r0=r0, c0=c0: h.dma_start(out=G["on"][r0:r0 + 128, c0:c0 + 128], in_=ob[:, :]), reads=[ob], dma=ko)
            pending_out.append(emit_out)
    for f_ in pending_out:
        f_()
    del pending_out[:]
    for _ in cg:
        pass
    cx.end()


def conv_gen(cx, G):
    P = cx.P
    identf, vcol = G["identf"], G["vcol"]
    cw, cb = G["cw_sb"], G["cb_sb"]
    HS = NOWN
    for fc in range(16):
        isq = fc < 8
        src = G["mqr"] if isq else G["mkr"]
        f0 = (fc % 8) * 128
        for seg in ([1] if isq else [0, 1]):
            u, ku = cx.pool("cu", 2, [128, 3 + HS], F32)
            if seg == 0:
                P.op("dve", lambda h, u=u: h.memset(u[:, 0:3], 0.0), writes=[u])
                yield
                P.op("sp", lambda h, u=u, src=src, f0=f0: h.dma_start(out=u[:, 3:3 + HS], in_=src[f0:f0 + 128, 0:HS]), writes=[u], dma=ku)
                yield
            else:
                P.op("sp", lambda h, u=u, src=src, f0=f0: h.dma_start(out=u[:, 0:3 + HS], in_=src[f0:f0 + 128, HS - 3:2 * HS]), writes=[u], dma=ku)
                yield
                P.op("dve", lambda h, u=u: h.tensor_scalar(out=u[:, 0:3], in0=u[:, 0:3], scalar1=vcol[:, 0:1], scalar2=None, op0=ALU.mult), reads=[u, vcol], writes=[u])
                yield
            a, _ = cx.pool("ca", 2, [128, HS], F32)
            P.op("dve", lambda h, u=u, a=a, fc=fc: h.tensor_scalar(out=a[:, :], in0=u[:, 0:HS], scalar1=cw[:, fc, 0:1], scalar2=cb[:, fc:fc + 1], op0=ALU.mult, op1=ALU.add),
                 reads=[u, cw, cb], writes=[a])
            yield
            for j in range(1, 4):
                P.op("dve", lambda h, u=u, a=a, fc=fc, j=j: h.scalar_tensor_tensor(out=a[:, :], in0=u[:, j:j + HS], scalar=cw[:, fc, j:j + 1], in1=a[:, :], op0=ALU.mult, op1=ALU.add),
                     reads=[u, a, cw], writes=[a])
                yield
            P.op("act", lambda h, a=a: h.activation(out=a[:, :], in_=a[:, :], func=AF.Silu), reads=[a], writes=[a])
            yield
            ob, ko = cx.pool("cob", 2, [128, HS], BF16)
            if isq:
                P.op("dve", lambda h, a=a, ob=ob: h.tensor_copy(out=ob[:, :], in_=a[:, :]), reads=[a], writes=[ob])
                yield
                P.op("sp", lambda h, ob=ob, f0=f0: h.dma_start(out=G["qc"][f0:f0 + 128, :], in_=ob[:, :]), reads=[ob], dma=ko)
                yield
            else:
                P.op("dve", lambda h, a=a, ob=ob: h.tensor_scalar(out=ob[:, :], in0=a[:, :], scalar1=1.0 / 16, scalar2=None, op0=ALU.mult), reads=[a], writes=[ob])
                yield
                P.op("sp", lambda h, ob=ob, f0=f0, seg=seg: h.dma_start(out=G["kc"][f0:f0 + 128, seg * HS:(seg + 1) * HS], in_=ob[:, :]), reads=[ob], dma=ko)
                yield
                for i4 in range(4):
                    pst = cx.ps()
                    for s in range(4):
                        i = i4 * 4 + s
                        P.op("pe", lambda h, pst=pst, a=a, s=s, i=i: h.transpose(out=pst[:, s * 128:(s + 1) * 128], in_=a[:, i * 128:(i + 1) * 128], identity=identf[:, :]),
                             reads=[a, identf], writes=[pst])
                    kt_, kk = cx.pool("ckt", 2, [128, TT], BF16)
                    P.op("act", lambda h, pst=pst, kt_=kt_: h.activation(out=kt_[:, :], in_=pst[:, :], func=AF.Copy, scale=1.0 / 16), reads=[pst], writes=[kt_])
                    tt0 = (seg * HS) // 128 + i4 * 4
                    dstv = G["kctm"].rearrange("(s p) f -> p s f", p=128)[:, tt0:tt0 + 4, f0:f0 + 128]
                    P.op("sp", lambda h, kt_=kt_, dstv=dstv: h.dma_start(out=dstv, in_=kt_[:, :].rearrange("p (s f) -> p s f", s=4)), reads=[kt_], dma=kk)
                    yield


def mlstm_stage(cx, G):
    P = cx.P
    cx.begin(8)
    ones_bf, identf, vcol, triu, ones_f = G["ones_bf"], G["identf"], G["vcol"], G["triu"], G["ones_f"]
    cw, cb = G["cw_sb"], G["cb_sb"]
    HS = NOWN
    small = cx.sb([128, 32, 8], F32, "small")
    logf = cx.sb([128, 32, 4], F32, "logf")
    P.op("sp", lambda h: h.dma_start(out=small[:, :, :], in_=G["small"].rearrange("(s p) f -> p s f", p=128)[:, :, 0:8]), writes=[small], dma="ms")
    P.op("act", lambda h: h.activation(out=logf[:, :, :], in_=small[:, :, 4:8], func=AF.Exp, scale=-1.0), reads=[small], writes=[logf])
    P.op("dve", lambda h: h.tensor_scalar(out=logf[:, :, :], in0=logf[:, :, :], scalar1=1.0, scalar2=None, op0=ALU.add), reads=[logf], writes=[logf])
    P.op("act", lambda h: h.activation(out=logf[:, :, :], in_=logf[:, :, :], func=AF.Ln), reads=[logf], writes=[logf])
    P.op("dve", lambda h: h.tensor_scalar(out=logf[:, :, :], in0=logf[:, :, :], scalar1=-1.0, scalar2=None, op0=ALU.mult), reads=[logf], writes=[logf])
    C = [cx.sb([128, 2, 256], F32, "C%d" % h_) for h_ in range(4)]
    Cb = [cx.sb([128, 2, 256], BF16, "Cb%d" % h_) for h_ in range(4)]
    nb = [cx.sb([128, 2, 128], F32, "nb%d" % h_) for h_ in range(4)]
    nbb = [cx.sb([128, 2, 128], BF16, "nbb%d" % h_) for h_ in range(4)]
    for t in C + Cb + nb + nbb:
        P.op("dve", lambda h, t=t: h.memset(t[:, :, :], 0.0), writes=[t])
    kcv = G["kc"].rearrange("(c p) t -> p c t", p=128)
    qcv = G["qc"].rearrange("(c p) t -> p c t", p=128)
    hcv = G["hc"].rearrange("(c p) t -> p c t", p=128)
    def prep_tile(tile):
        own = tile >= 16
        Wm = ea = qT = None
        R, _ = cx.pool("R", 2, [128, 4, 128], F32)
        P.op("dve", lambda h, R=R, tile=tile: h.tensor_tensor(out=R[:, :, :], in0=triu[:, :].unsqueeze(1).to_broadcast([128, 4, 128]),
                                                              in1=logf[:, tile, :].unsqueeze(2).to_broadcast([128, 4, 128]), op=ALU.mult), reads=[triu, logf], writes=[R])
        psA = cx.ps()
        mm(P, psA, psA[:, :], ones_f, ones_f[:, :], R, R[:, :, :].rearrange("p a b -> p (a b)"), True, True)
        psB = cx.ps()
        mm(P, psB, psB[:, 0:4], triu, triu[:, :], logf, logf[:, tile, :], True, True)
        mm(P, psB, psB[:, 4:8], ones_f, ones_f[:, :], logf, logf[:, tile, :], True, True)
        bc, _ = cx.pool("bc", 2, [128, 4], F32)
        ucol, _ = cx.pool("ucol", 2, [128, 4], F32)
        eg, _ = cx.pool("eg", 2, [128, 4], F32)
        P.op("dve", lambda h, bc=bc, psB=psB, tile=tile: h.tensor_tensor(out=bc[:, :], in0=small[:, tile, 0:4], in1=psB[:, 0:4], op=ALU.subtract), reads=[small, psB], writes=[bc])
        P.op("dve", lambda h, bc=bc, ucol=ucol, psB=psB: h.tensor_tensor(out=ucol[:, :], in0=bc[:, :], in1=psB[:, 4:8], op=ALU.add), reads=[bc, psB], writes=[ucol])
        P.op("act", lambda h, ucol=ucol: h.activation(out=ucol[:, :], in_=ucol[:, :], func=AF.Exp), reads=[ucol], writes=[ucol])
        P.op("act", lambda h, eg=eg, psB=psB: h.activation(out=eg[:, :], in_=psB[:, 4:8], func=AF.Exp), reads=[psB], writes=[eg])
        kT, k1 = cx.pool("kT", 2, [128, 8, 128], BF16)
        kM, k2 = cx.pool("kM", 2, [128, 1024], BF16)
        vM, k3 = cx.pool("vM", 2, [128, 1024], BF16)
        P.op("sp", lambda h, kM=kM, tile=tile: h.dma_start(out=kM[:, :], in_=G["kctm"][tile * 128:(tile + 1) * 128, :]), writes=[kM], dma=k2)
        P.op("sp", lambda h, vM=vM, tile=tile: h.dma_start(out=vM[:, :], in_=G["mv"][tile * 128:(tile + 1) * 128, :]), writes=[vM], dma=k3)
        if own:
            P.op("sp", lambda h, kT=kT, tile=tile: h.dma_start(out=kT[:, :, :], in_=kcv[:, :, tile * 128:(tile + 1) * 128]), writes=[kT], dma=k1)
            qT, k4 = cx.pool("qT", 2, [128, 8, 128], BF16)
            P.op("sp", lambda h, qT=qT, tile=tile: h.dma_start(out=qT[:, :, :], in_=qcv[:, :, (tile - 16) * 128:(tile - 15) * 128]), writes=[qT], dma=k4)
            Wm, _ = cx.pool("Wm", 2, [128, 4, 128], F32)
            ea, _ = cx.pool("ea", 2, [128, 4, 128], F32)
            P.op("dve", lambda h, Wm=Wm, psA=psA, bc=bc: h.tensor_tensor(out=Wm[:, :, :], in0=psA[:, :].rearrange("p (a b) -> p a b", a=4),
                                                                         in1=bc[:, :].unsqueeze(2).to_broadcast([128, 4, 128]), op=ALU.add), reads=[psA, bc], writes=[Wm])
            P.op("act", lambda h, Wm=Wm: h.activation(out=Wm[:, :, :], in_=Wm[:, :, :], func=AF.Exp), reads=[Wm], writes=[Wm])
            P.op("dve", lambda h, Wm=Wm: h.tensor_tensor(out=Wm[:, :, :], in0=Wm[:, :, :], in1=triu[:, :].unsqueeze(1).to_broadcast([128, 4, 128]), op=ALU.mult), reads=[Wm, triu], writes=[Wm])
            P.op("act", lambda h, ea=ea, psA=psA: h.activation(out=ea[:, :, :], in_=psA[:, :].rearrange("p (a b) -> p a b", a=4), func=AF.Exp), reads=[psA], writes=[ea])
        return dict(own=own, bc=bc, ucol=ucol, eg=eg, kT=kT, kM=kM, vM=vM, qT=qT, Wm=Wm, ea=ea)

    nxt_prep = prep_tile(0)
    for tile in range(32):
        cur_ = nxt_prep
        if tile + 1 < 32:
            nxt_prep = prep_tile(tile + 1)
        own, bc, ucol, eg, kT, kM, vM, qT, Wm, ea = (cur_[k_] for k_ in ("own", "bc", "ucol", "eg", "kT", "kM", "vM", "qT", "Wm", "ea"))
        for hp in range(2):
            hds = (2 * hp, 2 * hp + 1)
            st = {}
            if own:
                for hd in hds:
                    ps = cx.ps()
                    for dc in range(2):
                        mm(P, ps, ps[:, 0:128], kT, kT[:, 2 * hd + dc, :], qT, qT[:, 2 * hd + dc, :], dc == 0, dc == 1)
                    st[hd, "ps"] = ps
            for hd in hds:
                if own:
                    ps = st[hd, "ps"]
                    AT, _ = cx.pool("AT", 4, [128, 128], BF16)
                    P.op("dve", lambda h, AT=AT, ps=ps, Wm=Wm, hd=hd: h.tensor_tensor(out=AT[:, :], in0=ps[:, 0:128], in1=Wm[:, hd, :], op=ALU.mult), reads=[ps, Wm], writes=[AT])
                    qs, _ = cx.pool("qs", 4, [128, 2, 128], BF16)
                    P.op("dve", lambda h, qs=qs, qT=qT, ea=ea, hd=hd: h.tensor_tensor(out=qs[:, :, :], in0=qT[:, 2 * hd:2 * hd + 2, :], in1=ea[:, hd:hd + 1, :].to_broadcast([128, 2, 128]), op=ALU.mult),
                         reads=[qT, ea], writes=[qs])
                    st[hd, "AT"] = AT
                    st[hd, "qs"] = qs
                uk, _ = cx.pool("uk", 4, [128, 256], BF16)
                P.op("dve", lambda h, uk=uk, kM=kM, ucol=ucol, hd=hd: h.tensor_scalar(out=uk[:, :], in0=kM[:, hd * 256:(hd + 1) * 256], scalar1=ucol[:, hd:hd + 1], scalar2=None, op0=ALU.mult),
                     reads=[kM, ucol], writes=[uk])
                st[hd, "uk"] = uk
            for hd in hds:
                if own:
                    AT, qs = st[hd, "AT"], st[hd, "qs"]
                    pn = cx.ps()
                    for dch in range(2):
                        o_ = pn[:, dch * 128:(dch + 1) * 128]
                        mm(P, pn, o_, vM, vM[:, hd * 256 + dch * 128:hd * 256 + (dch + 1) * 128], AT, AT[:, :], True, False)
                        for ec in range(2):
                            mm(P, pn, o_, Cb[hd], Cb[hd][:, ec, dch * 128:(dch + 1) * 128], qs, qs[:, ec, :], False, ec == 1)
                    o_ = pn[:, 256:384]
                    mm(P, pn, o_, ones_bf, ones_bf[:, :], AT, AT[:, :], True, False)
                    for ec in range(2):
                        mm(P, pn, o_, nbb[hd], nbb[hd][:, ec, :], qs, qs[:, ec, :], False, ec == 1)
                    st[hd, "pn"] = pn
                uk = st[hd, "uk"]
                pc = cx.ps()
                pnb = cx.ps()
                for ec in range(2):
                    mm(P, pc, pc[:, ec * 256:(ec + 1) * 256], uk, uk[:, ec * 128:(ec + 1) * 128], vM, vM[:, hd * 256:(hd + 1) * 256], True, True)
                for ec in range(2):
                    mm(P, pnb, pnb[:, ec * 128:(ec + 1) * 128], uk, uk[:, ec * 128:(ec + 1) * 128], ones_bf, ones_bf[:, :], True, True)
                st[hd, "pc"] = pc
                st[hd, "pnb"] = pnb
            for hd in hds:
                if own:
                    pn = st[hd, "pn"]
                    rd, _ = cx.pool("rd", 4, [128, 128], F32)
                    P.op("act", lambda h, rd=rd, pn=pn: h.activation(out=rd[:, :], in_=pn[:, 256:384], func=AF.Abs), reads=[pn], writes=[rd])
                    P.op("dve", lambda h, rd=rd: h.tensor_scalar(out=rd[:, :], in0=rd[:, :], scalar1=1.0, scalar2=None, op0=ALU.max), reads=[rd], writes=[rd])
                    P.op("dve", lambda h, rd=rd: h.reciprocal(out=rd[:, :], in_=rd[:, :]), reads=[rd], writes=[rd])
                    ho, kh = cx.pool("ho", 4, [128, 2, 128], F32)
                    P.op("dve", lambda h, ho=ho, pn=pn, rd=rd: h.tensor_tensor(out=ho[:, :, :], in0=pn[:, 0:256].rearrange("p (a b) -> p a b", a=2),
                                                                             in1=rd[:, :].unsqueeze(1).to_broadcast([128, 2, 128]), op=ALU.mult), reads=[pn, rd], writes=[ho])
                    P.op("sp", lambda h, ho=ho, hd=hd, tile=tile: h.dma_start(out=hcv[:, 2 * hd:2 * hd + 2, (tile - 16) * 128:(tile - 15) * 128], in_=ho[:, :, :]), reads=[ho], dma=kh)
                pc, pnb = st[hd, "pc"], st[hd, "pnb"]
                Cf = C[hd][:, :, :].rearrange("p a b -> p (a b)")
                nf = nb[hd][:, :, :].rearrange("p a b -> p (a b)")
                P.op("dve", lambda h, Cf=Cf, pc=pc, eg=eg, hd=hd: h.scalar_tensor_tensor(out=Cf, in0=Cf, scalar=eg[:, hd:hd + 1], in1=pc[:, :], op0=ALU.mult, op1=ALU.add), reads=[C[hd], eg, pc], writes=[C[hd]])
                P.op("dve", lambda h, nf=nf, pnb=pnb, eg=eg, hd=hd: h.scalar_tensor_tensor(out=nf, in0=nf, scalar=eg[:, hd:hd + 1], in1=pnb[:, 0:256], op0=ALU.mult, op1=ALU.add), reads=[nb[hd], eg, pnb], writes=[nb[hd]])
                if tile == 15:
                    P.op("dve", lambda h, Cf=Cf, hd=hd: h.tensor_scalar(out=Cf, in0=Cf, scalar1=vcol[:, 0:1], scalar2=None, op0=ALU.mult), reads=[C[hd], vcol], writes=[C[hd]])
                    P.op("dve", lambda h, nf=nf, hd=hd: h.tensor_scalar(out=nf, in0=nf, scalar1=vcol[:, 0:1], scalar2=None, op0=ALU.mult), reads=[nb[hd], vcol], writes=[nb[hd]])
                P.op("act", lambda h, hd=hd: h.activation(out=Cb[hd][:, :, :], in_=C[hd][:, :, :], func=AF.Copy), reads=[C[hd]], writes=[Cb[hd]])
                P.op("act", lambda h, hd=hd: h.activation(out=nbb[hd][:, :, :], in_=nb[hd][:, :, :], func=AF.Copy), reads=[nb[hd]], writes=[nbb[hd]])
    P.barrier()
    mov = G["mo"].rearrange("(c p) t -> p c t", p=128)
    hmv = G["hm"].rearrange("(c p) t -> p c t", p=128)
    mg = G["mgain_sb"]
    for t4 in range(NOWN // TT):
        for hd in range(4):
            hc_, k1 = cx.pool("hcl", 2, [128, 2, TT], F32)
            mo_, k2 = cx.pool("mol", 2, [128, 2, TT], F32)
            P.op("sp", lambda h, hc_=hc_, hd=hd, t4=t4: h.dma_start(out=hc_[:, :, :], in_=hcv[:, 2 * hd:2 * hd + 2, t4 * TT:(t4 + 1) * TT]), writes=[hc_], dma=k1)
            P.op("sp", lambda h, mo_=mo_, hd=hd, t4=t4: h.dma_start(out=mo_[:, :, :], in_=mov[:, 2 * hd:2 * hd + 2, t4 * TT:(t4 + 1) * TT]), writes=[mo_], dma=k2)
            ss = cx.ps()
            for dc in range(2):
                sq, _ = cx.pool("sq", 3, [128, TT], BF16)
                P.op("act", lambda h, sq=sq, hc_=hc_, dc=dc: h.activation(out=sq[:, :], in_=hc_[:, dc, :], func=AF.Square), reads=[hc_], writes=[sq])
                mm(P, ss, ss[:, :], ones_bf, ones_bf[:, :], sq, sq[:, :], dc == 0, dc == 1)
            rs, _ = cx.pool("rstd", 2, [128, TT], F32)
            P.op("dve", lambda h, ss=ss, rs=rs: h.tensor_scalar(out=rs[:, :], in0=ss[:, :], scalar1=1.0 / 256, scalar2=EPS, op0=ALU.mult, op1=ALU.add), reads=[ss], writes=[rs])
            P.op("act", lambda h, rs=rs: h.activation(out=rs[:, :], in_=rs[:, :], func=AF.Sqrt), reads=[rs], writes=[rs])
            P.op("dve", lambda h, rs=rs: h.reciprocal(out=rs[:, :], in_=rs[:, :]), reads=[rs], writes=[rs])
            ob, ko = cx.pool("hmo", 2, [128, 2, TT], BF16)
            for dc in range(2):
                P.op("dve", lambda h, hc_=hc_, rs=rs, dc=dc, hd=hd: h.scalar_tensor_tensor(out=hc_[:, dc, :], in0=hc_[:, dc, :], scalar=mg[:, 2 * hd + dc:2 * hd + dc + 1], in1=rs[:, :], op0=ALU.mult, op1=ALU.mult),
                     reads=[hc_, rs, mg], writes=[hc_])
            P.op("dve", lambda h, hc_=hc_, mo_=mo_, ob=ob: h.tensor_tensor(out=ob[:, :, :], in0=hc_[:, :, :], in1=mo_[:, :, :], op=ALU.mult), reads=[hc_, mo_], writes=[ob])
            P.op("sp", lambda h, ob=ob, hd=hd, t4=t4: h.dma_start(out=hmv[:, 2 * hd:2 * hd + 2, t4 * TT:(t4 + 1) * TT], in_=ob[:, :, :]), reads=[ob], dma=ko)
    cx.end()


def merge_stage(cx, G):
    P = cx.P
    cx.begin(8)
    onv = G["on"].rearrange("(c p) t -> p c t", p=128)
    hmv = G["hm"].rearrange("(c p) t -> p c t", p=128)
    mgv = G["mg"].rearrange("(c p) t -> p c t", p=128)
    x1v = G["x1T"].rearrange("(c p) t -> p c t", p=128)
    x2v = G["x2T"].rearrange("(c p) t -> p c t", p=128)
    for t4 in range(NOWN // TS):
        c0 = t4 * TS
        a_, k1 = cx.pool("mon", 1, [128, 8, TS], BF16)
        b_, k2 = cx.pool("mhm", 1, [128, 8, TS], BF16)
        for hf in range(2):
            P.op("sp", lambda h, a_=a_, c0=c0, hf=hf: h.dma_start(out=a_[:, 4 * hf:4 * hf + 4, :], in_=onv[:, 4 * hf:4 * hf + 4, c0:c0 + TS]), writes=[a_], dma=k1)
            P.op("sp", lambda h, b_=b_, c0=c0, hf=hf: h.dma_start(out=b_[:, 4 * hf:4 * hf + 4, :], in_=hmv[:, 4 * hf:4 * hf + 4, c0:c0 + TS]), writes=[b_], dma=k2)
        mT, _ = cx.pool("mT", 1, [128, KC, TS], BF16)
        for oc in range(KC):
            wa, ka = cx.pool("wB", 4, [128, 8 * 128], BF16)
            P.op("pool", lambda h, wa=wa, oc=oc: h.dma_start(out=wa[:, :], in_=G["wbn"][oc, :, :], max_dma_last_dim=4096), writes=[wa], dma=ka)
            wb, kb = cx.pool("wB", 4, [128, 8 * 128], BF16)
            P.op("pool", lambda h, wb=wb, oc=oc: h.dma_start(out=wb[:, :], in_=G["wbm"][oc, :, :], max_dma_last_dim=4096), writes=[wb], dma=kb)
            gA, kga = cx.pool("gA", 3, [128, TS], F32)
            gB, kgb = cx.pool("gB", 3, [128, TS], F32)
            P.op("sp", lambda h, gA=gA, oc=oc, c0=c0: h.dma_start(out=gA[:, :], in_=mgv[:, oc, c0:c0 + TS]), writes=[gA], dma=kga)
            P.op("sp", lambda h, gB=gB, oc=oc, c0=c0: h.dma_start(out=gB[:, :], in_=mgv[:, KC + oc, c0:c0 + TS]), writes=[gB], dma=kgb)
            pa = [cx.ps() for _ in range(NS)]
            pb = [cx.ps() for _ in range(NS)]
            for c in range(8):
                for s_ in range(NS):
                    mm(P, pa[s_], pa[s_][:, :], wa, wa[:, c * 128:(c + 1) * 128], a_, a_[:, c, s_ * TT:(s_ + 1) * TT], c == 0, c == 7)
            for c in range(8):
                for s_ in range(NS):
                    mm(P, pb[s_], pb[s_][:, :], wb, wb[:, c * 128:(c + 1) * 128], b_, b_[:, c, s_ * TT:(s_ + 1) * TT], c == 0, c == 7)
            for s_ in range(NS):
                sl = slice(s_ * TT, (s_ + 1) * TT)
                P.op("dve", lambda h, gA=gA, pa_=pa[s_], sl=sl: h.tensor_tensor(out=gA[:, sl], in0=gA[:, sl], in1=pa_[:, :], op=ALU.mult), reads=[gA, pa[s_]], writes=[gA])
                P.op("dve", lambda h, gB=gB, pb_=pb[s_], sl=sl: h.tensor_tensor(out=gB[:, sl], in0=gB[:, sl], in1=pb_[:, :], op=ALU.mult), reads=[gB, pb[s_]], writes=[gB])
            P.op("dve", lambda h, gA=gA, gB=gB, mT=mT, oc=oc: h.tensor_tensor(out=mT[:, oc, :], in0=gA[:, :], in1=gB[:, :], op=ALU.add), reads=[gA, gB], writes=[mT])
        for oc in range(KC):
            wo, kw_ = cx.pool("wA", 4, [128, KC * 128], BF16)
            P.op("pool", lambda h, wo=wo, oc=oc: h.dma_start(out=wo[:, :], in_=G["wo"][oc, :, :], max_dma_last_dim=8192), writes=[wo], dma=kw_)
            xt, kx = cx.pool("xt", 4, [128, TS], F32)
            P.op("sp", lambda h, xt=xt, oc=oc, c0=c0: h.dma_start(out=xt[:, :], in_=x1v[:, oc, NOWN + c0:NOWN + c0 + TS]), writes=[xt], dma=kx + "l")
            po = [cx.ps() for _ in range(NS)]
            for c in range(KC):
                for s_ in range(NS):
                    mm(P, po[s_], po[s_][:, :], wo, wo[:, c * 128:(c + 1) * 128], mT, mT[:, c, s_ * TT:(s_ + 1) * TT], c == 0, c == KC - 1)
            for s_ in range(NS):
                sl = slice(s_ * TT, (s_ + 1) * TT)
                P.op("dve", lambda h, xt=xt, po_=po[s_], sl=sl: h.tensor_tensor(out=xt[:, sl], in0=xt[:, sl], in1=po_[:, :], op=ALU.add), reads=[xt, po[s_]], writes=[xt])
            P.op("sp", lambda h, xt=xt, oc=oc, c0=c0: h.dma_start(out=x2v[:, oc, c0:c0 + TS], in_=xt[:, :]), reads=[xt], dma=kx + "s")
    cx.end()


def build(stage="full"):
    nc = bass.Bass("TRN2", target_bir_lowering=False)

    def din(name, shape, dt=F32):
        return nc.dram_tensor(name, list(shape), dt, kind="ExternalInput").ap()

    def scr(name, shape, dt=F32):
        return nc.dram_tensor(name, list(shape), dt, kind="Internal").ap()

    G = {}
    xT = din("xT", [D, SEQV])
    g1 = din("g1", [128, KC]); g2 = din("g2", [128, KC]); gmix = din("gmix", [128, KC])
    wg1 = din("wg1", [FC, 128, D]); wu1 = din("wu1", [FC, 128, D]); wd1 = din("wd1", [KC, 128, DFF])
    wg2 = din("wg2", [FC, 128, D]); wu2 = din("wu2", [FC, 128, D]); wd2 = din("wd2", [KC, 128, DFF])
    G["wi"] = din("wi", [NT_IN, 128, D]); bi = din("bi", [128, NT_IN])
    G["w1"] = din("w1", [2, 64, 32 * 256]); G["w2"] = din("w2", [2, 128, 128]); G["posT"] = din("posT", [2, 64, 32])
    G["wbn"] = din("wbn", [KC, 128, 1024]); G["wbm"] = din("wbm", [KC, 128, 1024]); G["wo"] = din("wo", [KC, 128, D])
    cbf = din("cbf", [128, 3 * 128], BF16)
    cf = din("cf", [128, 3 * 128 + 64 + 2 * 64])
    gk = din("gk", [128, 4]); mgain = din("mgain", [128, 8]); cw = din("cw", [128, 16 * 4]); cb = din("cb", [128, 16])
    pc = din("pc", [128, 3])
    G["c_cpen"] = din("c_cpen", [128, 4, 512], BF16); G["c_wpen"] = din("c_wpen", [128, 8, 512], BF16)
    G["c_cmpp"] = din("c_cmpp", [128, 8, 512], BF16); G["c_Eexp"] = din("c_Eexp", [64, SEQV], BF16)
    G["c_mulm"] = din("c_mulm", [128, 16, 64]); G["c_addm"] = din("c_addm", [128, 16, 64])
    out = nc.dram_tensor("out", [D, NOWN], F32, kind="ExternalOutput").ap()

    G["x1T"] = scr("x1T", [D, SEQV]); G["x2T"] = scr("x2T", [D, NOWN])
    G["kvc"] = scr("kvc", [512, SEQV], BF16); G["ks"] = scr("ks", [256, SEQV], BF16); G["kw"] = scr("kw", [256, SEQV], BF16)
    G["v"] = scr("vv", [SEQV, 2, 256], BF16); G["mqr"] = scr("mqr", [1024, SEQV]); G["mkr"] = scr("mkr", [1024, SEQV])
    G["mv"] = scr("mv", [SEQV, 1024], BF16); G["small"] = scr("small", [SEQV, 128])
    G["qn"] = scr("qn", [1024, NOWN], BF16); G["mo"] = scr("mo", [1024, NOWN]); G["mg"] = scr("mg", [4096, NOWN])
    G["on"] = scr("on", [1024, NOWN], BF16); G["hm"] = scr("hm", [1024, NOWN], BF16); G["hc"] = scr("hc", [1024, NOWN])
    G["qc"] = scr("qc", [1024, NOWN], BF16); G["kc"] = scr("kc", [1024, SEQV], BF16); G["kctm"] = scr("kctm", [SEQV, 1024], BF16)

    cx = Ctx(nc)
    P = cx.P
    cbf_sb = cx.gsb([128, 3 * 128], BF16, "cbf_sb")
    cf_sb = cx.gsb([128, 3 * 128 + 64 + 128], F32, "cf_sb")
    small_sb = {}
    for nm, src, shp in (("g1", g1, [128, KC]), ("g2", g2, [128, KC]), ("gmix", gmix, [128, KC]), ("bi", bi, [128, NT_IN]), ("gk", gk, [128, 4]),
                         ("mgain", mgain, [128, 8]), ("cw", cw, [128, 64]), ("cb", cb, [128, 16]), ("pc", pc, [128, 3])):
        t = cx.gsb(shp, F32, nm + "_sb")
        P.op("sp", lambda h, t=t, src=src: h.dma_start(out=t.ap, in_=src), writes=[t], dma="c0")
        small_sb[nm] = t
    P.op("sp", lambda h: h.dma_start(out=cbf_sb[:, :], in_=cbf[:, :]), writes=[cbf_sb], dma="c0")
    P.op("sp", lambda h: h.dma_start(out=cf_sb[:, :], in_=cf[:, :]), writes=[cf_sb], dma="c0")
    P.barrier()

    class V(T):
        __slots__ = ("parent",)

        def __init__(self, parent, ap):
            self.parent = parent
            self.ap = ap
            self.name = parent.name

        w = property(lambda s: s.parent.w, lambda s, v: setattr(s.parent, "w", v))
        r = property(lambda s: s.parent.r, lambda s, v: setattr(s.parent, "r", v))

    G["ones_bf"] = V(cbf_sb, cbf_sb[:, 0:128]); G["identb"] = V(cbf_sb, cbf_sb[:, 128:256]); G["bd64"] = V(cbf_sb, cbf_sb[:, 256:384])
    G["identf"] = V(cf_sb, cf_sb[:, 0:128]); G["triu"] = V(cf_sb, cf_sb[:, 128:256]); G["ones_f"] = V(cf_sb, cf_sb[:, 256:384])
    G["ov"] = V(cf_sb, cf_sb[:, 448:576].rearrange("p (a b) -> p a b", a=2))
    G["bi_sb"] = small_sb["bi"]; G["gk_sb"] = small_sb["gk"]; G["gmix_sb"] = small_sb["gmix"]; G["mgain_sb"] = small_sb["mgain"]
    G["cw_sb"] = V(small_sb["cw"], small_sb["cw"][:, :].rearrange("p (a b) -> p a b", b=4)); G["cb_sb"] = small_sb["cb"]
    G["vcol"] = V(small_sb["pc"], small_sb["pc"][:, 0:1]); G["validc"] = V(small_sb["pc"], small_sb["pc"][:, 1:3])
    G["KcT"] = cx.gsb([64, 4, 256], BF16, "KcT"); G["Vc"] = cx.gsb([128, 4, 2, 129], BF16, "Vc")

    all_tiles = [i * TS for i in range(SEQV // TS)]
    ffn_stage(cx, xT, G["x1T"], all_tiles, small_sb["g1"], wg1, wu1, wd1, G["ones_bf"], "f1")
    inproj_stage(cx, G)
    cmp_stage(cx, G)
    nsa_stage(cx, G)
    mlstm_stage(cx, G)
    merge_stage(cx, G)
    ffn_stage(cx, G["x2T"], out, [i * TS for i in range(NOWN // TS)], small_sb["g2"], wg2, wu2, wd2, G["ones_bf"], "f2")
    P.emit()
    return nc


def tile_w(w, kc):
    K, N = w.shape
    return np.ascontiguousarray(w.reshape(kc, 128, N // 128, 128).transpose(2, 1, 0, 3).reshape(N // 128, 128, kc * 128))


def colvec(v, n):
    return np.ascontiguousarray(np.asarray(v, np.float32).reshape(n, 128).T)


_cache = {}


def static_consts():
    bf = ml_dtypes.bfloat16
    p = np.arange(128)
    col = np.arange(512)
    ones = np.ones((128, 128), np.float32)
    ident = np.eye(128, dtype=np.float32)
    bd = (p[:, None] // 64 == p[None, :] // 64).astype(np.float32)
    triu = (p[:, None] <= p[None, :]).astype(np.float32)
    c = {}
    c["cbf"] = np.concatenate([ones, ident, bd], axis=1).astype(bf)
    ov = np.zeros((128, 2, 64), np.float32)
    for ct in range(2):
        cc = ct * 128 + p
        n = np.arange(64)
        ov[:, ct, :] = ((16 * cc[:, None] < 64 * n[None, :] + 64) & (16 * cc[:, None] + 32 > 64 * n[None, :])).astype(np.float32)
    c["cf"] = np.concatenate([ident, triu, ones, np.zeros((128, 64), np.float32), ov.reshape(128, 128)], axis=1).astype(np.float32)
    cpen = np.zeros((128, 4, 512), np.float32)
    for rel in range(4):
        cpen[:, rel, :] = np.where(128 * rel + p[:, None] > col[None, :], 0.0, 1.0)
    c["c_cpen"] = cpen.astype(bf)
    wpen = np.zeros((128, 8, 512), np.float32)
    for rel in range(8):
        kp = -512 + 128 * rel + p[:, None]
        ok = (kp <= col[None, :]) & (kp > col[None, :] - 512)
        wpen[:, rel, :] = np.where(ok, 1.0, 0.0)
    c["c_wpen"] = wpen.astype(bf)
    cmpp = np.zeros((128, 8, 512), np.float32)
    for qt in range(4):
        for ct in range(2):
            cc = ct * 128 + p[:, None]
            qv = NOWN + qt * 512 + col[None, :]
            cmpp[:, qt * 2 + ct, :] = np.where(16 * cc + 31 <= qv, 0.0, NEGB)
    c["c_cmpp"] = cmpp.astype(bf)
    kk = np.arange(SEQV)
    c["c_Eexp"] = (kk[None, :] // 64 == np.arange(64)[:, None]).astype(np.float32).astype(bf)
    return c


def percore_consts(hh):
    v = float(hh)
    pc = np.zeros((128, 3), np.float32)
    pc[:, 0] = v
    pc[:, 1] = v
    pc[:, 2] = 1.0
    mulm = np.zeros((128, 16, 64), np.float32)
    addm = np.zeros((128, 16, 64), np.float32)
    p = np.arange(128)
    n = np.arange(64)
    for qs in range(16):
        tv = NOWN + qs * 128 + p
        if hh == 1:
            tr = tv
            nr = n
        else:
            tr = tv - NOWN
            nr = n - 32
        cur = tr // 64
        real = nr[None, :] >= 0
        forced = real & ((nr[None, :] == 0) | (nr[None, :] == cur[:, None]) | (nr[None, :] == cur[:, None] - 1))
        causal = real & (nr[None, :] * 64 <= tr[:, None])
        mulm[:, qs, :] = (causal & ~forced).astype(np.float32)
        addm[:, qs, :] = np.where(forced, 1e4, np.where(causal, 0.0, -1.0))
    return {"pc": pc, "c_mulm": mulm, "c_addm": addm}


def prep_inputs(inp):
    f = lambda k: np.asarray(inp[k], np.float32)[0]
    x = np.asarray(inp["x"], np.float32)
    w_in = f("w_in")
    b_in = f("b_in")
    KVB = 1024
    cols = []
    for kvidx in (0, 1, 2, 4, 3, 5):
        cols.append(np.arange(KVB + kvidx * 256, KVB + (kvidx + 1) * 256))
    MQ = 2608
    cols.append(np.arange(MQ, MQ + 1024)); cols.append(np.arange(MQ + 1024, MQ + 2048)); cols.append(np.arange(MQ + 2048, MQ + 3072))
    small_cols = np.concatenate([np.arange(5680, 5684), np.arange(5684, 5688), np.arange(2560, 2608)])
    cols_all = np.concatenate(cols)
    cols_own = np.concatenate([np.arange(0, 1024), np.arange(5688, 6712), np.arange(6712, 10808)])
    w_small = np.zeros((D, 128), np.float32); w_small[:, :56] = w_in[:, small_cols]
    b_small = np.zeros((128,), np.float32); b_small[:56] = b_in[small_cols]
    w_perm = np.concatenate([w_in[:, cols_all], w_small, w_in[:, cols_own]], axis=1)
    b_perm = np.concatenate([b_in[cols_all], b_small, b_in[cols_own]])
    assert w_perm.shape[1] == NT_IN * 128
    gk = np.stack([np.tile(f("nsa_ks_gain"), 2), np.tile(f("nsa_kw_gain"), 2), np.tile(f("nsa_q_gain"), 2), np.tile(f("nsa_kc_gain"), 2)], axis=1)
    w1 = np.stack([f("cmp_w1_k").reshape(32, 64, 256).transpose(1, 0, 2).reshape(64, 32 * 256),
                   f("cmp_w1_v").reshape(32, 64, 256).transpose(1, 0, 2).reshape(64, 32 * 256)])
    w2 = np.stack([f("cmp_w2_k").reshape(2, 128, 64).transpose(1, 0, 2).reshape(128, 128),
                   f("cmp_w2_v").reshape(2, 128, 64).transpose(1, 0, 2).reshape(128, 128)])
    posT = np.stack([f("cmp_pos_k").T, f("cmp_pos_v").T])
    cwv = f("m_conv_w")
    cw = np.ascontiguousarray(cwv.reshape(4, 16, 128).transpose(2, 1, 0).reshape(128, 64))
    common = {
        "g1": colvec(f("ffn1_norm"), KC), "g2": colvec(f("ffn2_norm"), KC), "gmix": colvec(f("mix_norm"), KC),
        "wg1": tile_w(f("ffn1_w_gate"), KC), "wu1": tile_w(f("ffn1_w_up"), KC), "wd1": tile_w(f("ffn1_w_down"), FC),
        "wg2": tile_w(f("ffn2_w_gate"), KC), "wu2": tile_w(f("ffn2_w_up"), KC), "wd2": tile_w(f("ffn2_w_down"), FC),
        "wi": tile_w(w_perm, KC), "bi": colvec(b_perm, NT_IN),
        "w1": np.ascontiguousarray(w1), "w2": np.ascontiguousarray(w2), "posT": np.ascontiguousarray(posT),
        "wbn": tile_w(f("w_branch_nsa"), 8), "wbm": tile_w(f("w_branch_mlstm"), 8), "wo": tile_w(f("w_out"), KC),
        "gk": np.ascontiguousarray(gk.astype(np.float32)), "mgain": colvec(f("m_out_gain").reshape(-1), 8),
        "cw": cw, "cb": colvec(f("m_conv_b"), 16),
    }
    common.update(static_consts())
    pcs = [percore_consts(0), percore_consts(1)]
    maps = []
    for c in range(8):
        b, hh = c // 2, c % 2
        m = dict(common)
        m.update(pcs[hh])
        m["xT"] = np.ascontiguousarray(np.concatenate([x[b, 0:NOWN].T, x[b, NOWN * hh:NOWN * hh + NOWN].T], axis=1))
        maps.append(m)
    return maps


def kernel(**inp):
    if "nc" not in _cache:
        _cache["nc"] = build()
    nc = _cache["nc"]
    maps = prep_inputs(inp)
    res = run_bass_kernel_spmd(nc, maps, core_ids=list(range(8)))
    outp = np.empty((4, 4096, D), np.float32)
    for c in range(8):
        b, hh = c // 2, c % 2
        outp[b, NOWN * hh:NOWN * hh + NOWN, :] = res.results[c]["out"].T
    return outp
```

```python
import numpy as np
import ml_dtypes
import concourse.bass as bass
import concourse.mybir as mybir
from concourse.bass_utils import run_bass_kernel_spmd
from contextlib import ExitStack

F32 = mybir.dt.float32
BF16 = mybir.dt.bfloat16
AF = mybir.ActivationFunctionType
ALU = mybir.AluOpType
AX = mybir.AxisListType

D = 2048
DFF = 5632
KC = D // 128
FC = DFF // 128
TT = 512
SEQV = 4096
NOWN = 2048
EPS = 1e-6


class T:
    __slots__ = ("ap", "name", "w", "r")

    def __init__(self, ap, name=""):
        self.ap = ap
        self.name = name
        self.w = None
        self.r = []

    def __getitem__(self, idx):
        return self.ap[idx]


class Prog:
    ENGS = ("pe", "act", "dve", "pool", "sp")

    def __init__(self, nc):
        self.nc = nc
        self.ops = []
        self.last = {}
        self.dmas = []
        self.pending = {e: set() for e in self.ENGS}

    def barrier(self):
        deps = set(self.last.values()) | set(self.dmas)
        self.dmas = []
        for e in self.ENGS:
            self.pending[e] |= deps

    def op(self, eng, fn, reads=(), writes=(), dma=None):
        i = len(self.ops)
        deps = set(self.pending[eng])
        self.pending[eng] = set()
        self.last[eng] = i
        if dma is not None:
            self.dmas.append(i)
        for t in reads:
            if t.w is not None:
                deps.add(t.w)
        for t in writes:
            if t.w is not None:
                deps.add(t.w)
            deps.update(t.r)
        for t in reads:
            t.r.append(i)
        for t in writes:
            t.w = i
            t.r = []
        d2 = set()
        for d in deps:
            o = self.ops[d]
            if o["dma"] is None and o["eng"] == eng and eng == "pe":
                continue
            d2.add(d)
            o["needed"] = True
        self.ops.append(dict(eng=eng, fn=fn, deps=d2, dma=dma, needed=False, ev=None))
        return i

    def emit(self):
        nc = self.nc
        cnt = {}
        for o in self.ops:
            if o["dma"] is not None:
                k = "dma_" + o["dma"]
                cnt[k] = cnt.get(k, 0) + 16
                o["ev"] = (k, cnt[k])
            elif o["needed"]:
                k = "eng_" + o["eng"]
                cnt[k] = cnt.get(k, 0) + 1
                o["ev"] = (k, cnt[k])
        keys = sorted(cnt.keys())
        self.maxvals = dict(cnt)
        sems = {k: nc.alloc_semaphore(name=k) for k in keys}
        per_eng = {e: [] for e in self.ENGS}
        for o in self.ops:
            per_eng[o["eng"]].append(o)
        handles = {"pe": "tensor", "act": "scalar", "dve": "vector", "pool": "gpsimd", "sp": "sync"}
        ops = self.ops
        with nc.Block() as block:
            for e in self.ENGS:
                lst = per_eng[e]

                def body(h, lst=lst, e=e):
                    seen = {}
                    for o in lst:
                        need = {}
                        for d in o["deps"]:
                            k, v = ops[d]["ev"]
                            if seen.get(k, 0) >= v:
                                continue
                            if need.get(k, 0) < v:
                                need[k] = v
                        for k, v in need.items():
                            h.wait_ge(sems[k], v)
                            seen[k] = v
                        ins = o["fn"](h)
                        if o["ev"] is not None:
                            ins.then_inc(sems[o["ev"][0]], 16 if o["dma"] is not None else 1)
                    if e == "sp":
                        for k in keys:
                            if k.startswith("dma_"):
                                h.wait_ge(sems[k], cnt[k])
                getattr(block, handles[e])(body)
        return sems


class Ctx:
    def __init__(self, nc):
        self.nc = nc
        self.P = Prog(nc)
        self.n = 0
        self.psum = [T(nc.alloc_psum_tensor("ps%d" % i, [128, 512], F32).ap(), "ps%d" % i) for i in range(8)]
        self.psi = 0
        self.ps_mod = 8
        self.pools = {}
        self.stack = None

    def begin(self, ps_mod=8):
        self.stack = ExitStack()
        self.pools = {}
        self.ps_mod = ps_mod

    def end(self):
        self.P.barrier()
        self.stack.close()
        self.stack = None
        self.pools = {}

    def gsb(self, shape, dt, name):
        return T(self.nc.alloc_sbuf_tensor(name, list(shape), dt).ap(), name)

    def sb(self, shape, dt, name=None):
        self.n += 1
        name = "%s_%d" % (name or "t", self.n)
        h = self.stack.enter_context(self.nc.sbuf_tensor(name, list(shape), dt))
        return T(h.ap(), name)

    def ps(self):
        t = self.psum[self.psi % self.ps_mod]
        self.psi += 1
        return t

    def pool(self, key, n, shape, dt):
        if key not in self.pools:
            self.pools[key] = [[self.sb(shape, dt, "%s%d" % (key, i)) for i in range(n)], 0]
        p = self.pools[key]
        t = p[0][p[1] % n]
        idx = p[1] % n
        p[1] += 1
        return t, "%s%d" % (key, idx)


NS = 2
TS = NS * TT


class NormJob:
    def __init__(self, cx, src_dram, t0, gain_sb, ones_bf, xn, ss=None):
        self.cx, self.t0, self.gain_sb, self.ones_bf, self.xn = cx, t0, gain_sb, ones_bf, xn
        self.src_v = src_dram.rearrange("(c p) t -> p c t", p=128)
        self.ss = ss if ss is not None else [cx.psum[6], cx.psum[7]]

    def chunk(self, c):
        cx, P, t0, src_v, ss, ones_bf = self.cx, self.cx.P, self.t0, self.src_v, self.ss, self.ones_bf
        xc, kx = cx.pool("xc", 3, [128, TS], F32)
        P.op("sp", lambda h, c=c, xc=xc: h.dma_start(out=xc[:, :], in_=src_v[:, c, t0:t0 + TS]), writes=[xc], dma=kx)
        sq, _ = cx.pool("sq", 2, [128, TS], BF16)
        P.op("act", lambda h, xc=xc, sq=sq: h.activation(out=sq[:, :], in_=xc[:, :], func=AF.Square), reads=[xc], writes=[sq])
        for s_ in range(NS):
            mm(P, ss[s_], ss[s_][:, :], ones_bf, ones_bf[:, :], sq, sq[:, s_ * TT:(s_ + 1) * TT], c == 0, c == KC - 1)

    def finish(self):
        cx, P, t0, src_v, ss, xn, gain_sb = self.cx, self.cx.P, self.t0, self.src_v, self.ss, self.xn, self.gain_sb
        rstd, _ = cx.pool("rstdL", 2, [128, TS], F32)
        for s_ in range(NS):
            P.op("dve", lambda h, s_=s_: h.tensor_scalar(out=rstd[:, s_ * TT:(s_ + 1) * TT], in0=ss[s_][:, :], scalar1=1.0 / D, scalar2=EPS, op0=ALU.mult, op1=ALU.add),
                 reads=[ss[s_]], writes=[rstd])
        P.op("act", lambda h: h.activation(out=rstd[:, :], in_=rstd[:, :], func=AF.Sqrt), reads=[rstd], writes=[rstd])
        P.op("dve", lambda h: h.reciprocal(out=rstd[:, :], in_=rstd[:, :]), reads=[rstd], writes=[rstd])
        for c in range(KC):
            xc, kx = cx.pool("xc", 3, [128, TS], F32)
            P.op("sp", lambda h, c=c, xc=xc: h.dma_start(out=xc[:, :], in_=src_v[:, c, t0:t0 + TS]), writes=[xc], dma=kx)
            P.op("dve", lambda h, c=c, xc=xc: h.scalar_tensor_tensor(out=xn[:, c, :], in0=xc[:, :], scalar=gain_sb[:, c:c + 1], in1=rstd[:, :], op0=ALU.mult, op1=ALU.mult),
                 reads=[xc, rstd, gain_sb], writes=[xn])


def ffn_stage(cx, src_dram, dst_dram, tiles, gain_sb, wg, wu, wd, ones_bf, tag, dst_off=0):
    P = cx.P
    cx.begin(6)
    xn = cx.sb([128, KC, TS], BF16, "xn")
    hbuf = cx.sb([128, FC, TS], BF16, "hbuf")
    src_v = src_dram.rearrange("(c p) t -> p c t", p=128)
    dst_v = dst_dram.rearrange("(c p) t -> p c t", p=128)
    nj = NormJob(cx, src_dram, tiles[0], gain_sb, ones_bf, xn)
    for c in range(KC):
        nj.chunk(c)
    nj.finish()
    for ti_, t0 in enumerate(tiles):
        nxt = NormJob(cx, src_dram, tiles[ti_ + 1], gain_sb, ones_bf, xn) if ti_ + 1 < len(tiles) else None
        for j in range(FC):
            if nxt is not None and 20 <= j < 20 + KC:
                nxt.chunk(j - 20)
            wgs, kg = cx.pool("wA", 4, [128, KC * 128], BF16)
            P.op("pool", lambda h, j=j, wgs=wgs: h.dma_start(out=wgs[:, :], in_=wg[j, :, :], max_dma_last_dim=8192), writes=[wgs], dma=kg)
            wus, ku = cx.pool("wA", 4, [128, KC * 128], BF16)
            P.op("pool", lambda h, j=j, wus=wus: h.dma_start(out=wus[:, :], in_=wu[j, :, :], max_dma_last_dim=8192), writes=[wus], dma=ku)
            pg = [cx.ps() for _ in range(NS)]
            pu = [cx.ps() for _ in range(NS)]
            for c in range(KC):
                for s_ in range(NS):
                    mm(P, pg[s_], pg[s_][:, :], wgs, wgs[:, c * 128:(c + 1) * 128], xn, xn[:, c, s_ * TT:(s_ + 1) * TT], c == 0, c == KC - 1)
            for c in range(KC):
                for s_ in range(NS):
                    mm(P, pu[s_], pu[s_][:, :], wus, wus[:, c * 128:(c + 1) * 128], xn, xn[:, c, s_ * TT:(s_ + 1) * TT], c == 0, c == KC - 1)
            for s_ in range(NS):
                sg, _ = cx.pool("sg", 3, [128, TT], BF16)
                P.op("act", lambda h, pg_=pg[s_], sg=sg: h.activation(out=sg[:, :], in_=pg_[:, :], func=AF.Silu), reads=[pg[s_]], writes=[sg])
                P.op("dve", lambda h, j=j, sg=sg, pu_=pu[s_], s_=s_: h.tensor_tensor(out=hbuf[:, j, s_ * TT:(s_ + 1) * TT], in0=sg[:, :], in1=pu_[:, :], op=ALU.mult),
                     reads=[sg, pu[s_]], writes=[hbuf])
        if nxt is not None:
            nxt.finish()
        for m in range(KC):
            wds, kd = cx.pool("wD", 2, [128, FC * 128], BF16)
            for q in range(4):
                f0 = q * (FC // 4) * 128
                f1 = (q + 1) * (FC // 4) * 128
                P.op("pool", lambda h, m=m, wds=wds, f0=f0, f1=f1: h.dma_start(out=wds[:, f0:f1], in_=wd[m, :, f0:f1], max_dma_last_dim=5632), writes=[wds], dma=kd)
            po = [cx.ps() for _ in range(NS)]
            for j in range(FC):
                for s_ in range(NS):
                    mm(P, po[s_], po[s_][:, :], wds, wds[:, j * 128:(j + 1) * 128], hbuf, hbuf[:, j, s_ * TT:(s_ + 1) * TT], j == 0, j == FC - 1)
            xc, kx = cx.pool("xc", 3, [128, TS], F32)
            P.op("sp", lambda h, m=m, xc=xc, t0=t0: h.dma_start(out=xc[:, :], in_=src_v[:, m, t0:t0 + TS]), writes=[xc], dma=kx)
            ot, ko = cx.pool("ot", 2, [128, TS], F32)
            for s_ in range(NS):
                P.op("dve", lambda h, po_=po[s_], ot=ot, xc=xc, s_=s_: h.scalar_tensor_tensor(out=ot[:, s_ * TT:(s_ + 1) * TT], in0=po_[:, :], scalar=0.5, in1=xc[:, s_ * TT:(s_ + 1) * TT], op0=ALU.mult, op1=ALU.add),
                     reads=[po[s_], xc], writes=[ot])
            P.op("sp", lambda h, m=m, ot=ot, t0=t0: h.dma_start(out=dst_v[:, m, t0 + dst_off:t0 + dst_off + TS], in_=ot[:, :]), reads=[ot], dma=ko)
    cx.end()


def mm(P, out_t, out_ap, lhsT_t, lhsT_ap, rhs_t, rhs_ap, start, stop):
    P.op("pe", lambda h: h.matmul(out_ap, lhsT_ap, rhs_ap, start=start, stop=stop), reads=[lhsT_t, rhs_t], writes=[out_t])


NEGB = -30000.0
TILES_ALL = (["kcmp"] * 2 + ["vcmp"] * 2 + ["kslc"] * 2 + ["kwin"] * 2 + ["vslc"] * 2 + ["vwin"] * 2
             + ["mq"] * 8 + ["mk"] * 8 + ["mv"] * 8 + ["small"])
TILES_OWN = ["q"] * 8 + ["mo"] * 8 + ["merge"] * 32
NT_ALL = len(TILES_ALL)
NT_IN = NT_ALL + len(TILES_OWN)


def inproj_stage(cx, G):
    P = cx.P
    cx.begin(8)
    xn = cx.sb([128, KC, TS], BF16, "xn")
    wi, bi = G["wi"], G["bi_sb"]
    ones_bf, bd64, identf, vcol = G["ones_bf"], G["bd64"], G["identf"], G["vcol"]
    gk = G["gk_sb"]
    first_idx = {}
    for j, k in enumerate(TILES_ALL + TILES_OWN):
        first_idx.setdefault(k, j)
    for ti in range(SEQV // TS):
        t0b = ti * TS
        own = t0b >= NOWN
        nj = NormJob(cx, G["x1T"], t0b, G["gmix_sb"], ones_bf, xn, ss=[cx.ps(), cx.ps()])
        for c in range(KC):
            nj.chunk(c)
        nj.finish()
        nxt = None
        plan = TILES_ALL + (TILES_OWN if own else [])
        deferred = []
        for j, kind in enumerate(plan):
            if nxt is not None and 8 <= j < 8 + KC:
                nxt.chunk(j - 8)
            if nxt is not None and j == len(plan) - 1:
                pass
            k = j - first_idx[kind]
            ws, kw_ = cx.pool("wA", 6, [128, KC * 128], BF16)
            P.op("pool", lambda h, j=j, ws=ws: h.dma_start(out=ws[:, :], in_=wi[j, :, :], max_dma_last_dim=8192), writes=[ws], dma=kw_)
            pss = [cx.ps() for _ in range(NS)]
            for c in range(KC):
                for s_ in range(NS):
                    mm(P, pss[s_], pss[s_][:, :], ws, ws[:, c * 128:(c + 1) * 128], xn, xn[:, c, s_ * TT:(s_ + 1) * TT], c == 0, c == KC - 1)
            while deferred:
                deferred.pop(0)()
            for s_ in range(NS):
                ps = pss[s_]
                t0 = t0b + s_ * TT
                bcol = bi[:, j:j + 1]
                if kind in ("kcmp", "vcmp"):
                    ob, ko = cx.pool("obb", 3, [128, TT], BF16)
                    P.op("act", lambda h, ps=ps, ob=ob, bcol=bcol: h.activation(out=ob[:, :], in_=ps[:, :], func=AF.Identity, bias=bcol), reads=[ps, bi], writes=[ob])
                    r0 = (0 if kind == "kcmp" else 256) + k * 128
                    P.op("sp", lambda h, ob=ob, r0=r0, t0=t0: h.dma_start(out=G["kvc"][r0:r0 + 128, t0:t0 + TT], in_=ob[:, :]), reads=[ob], dma=ko)
                elif kind in ("mq", "mk"):
                    ob, ko = cx.pool("obf", 3, [128, TT], F32)
                    P.op("act", lambda h, ps=ps, ob=ob, bcol=bcol: h.activation(out=ob[:, :], in_=ps[:, :], func=AF.Identity, bias=bcol), reads=[ps, bi], writes=[ob])
                    dst = G["mqr"] if kind == "mq" else G["mkr"]
                    P.op("sp", lambda h, ob=ob, dst=dst, k=k, t0=t0: h.dma_start(out=dst[k * 128:(k + 1) * 128, t0:t0 + TT], in_=ob[:, :]), reads=[ob], dma=ko)
                elif kind in ("mo", "merge"):
                    ob, ko = cx.pool("obf", 3, [128, TT], F32)
                    P.op("act", lambda h, ps=ps, ob=ob, bcol=bcol: h.activation(out=ob[:, :], in_=ps[:, :], func=AF.Sigmoid, bias=bcol), reads=[ps, bi], writes=[ob])
                    dst = G["mo"] if kind == "mo" else G["mg"]
                    P.op("sp", lambda h, ob=ob, dst=dst, k=k, t0=t0: h.dma_start(out=dst[k * 128:(k + 1) * 128, t0 - NOWN:t0 - NOWN + TT], in_=ob[:, :]), reads=[ob], dma=ko)
                elif kind in ("kslc", "kwin", "q"):
                    z, _ = cx.pool("z", 5, [128, TT], F32)
                    P.op("act", lambda h, ps=ps, z=z, bcol=bcol: h.activation(out=z[:, :], in_=ps[:, :], func=AF.Identity, bias=bcol), reads=[ps, bi], writes=[z])
                    sq, _ = cx.pool("sqe", 5, [128, TT], BF16)
                    P.op("act", lambda h, z=z, sq=sq: h.activation(out=sq[:, :], in_=z[:, :], func=AF.Square), reads=[z], writes=[sq])
                    def part_b(ps=ps, z=z, t0=t0, k=k, kind=kind, j=j, own=own, sq=sq):
                        ps2 = cx.ps()
                        mm(P, ps2, ps2[:, :], bd64, bd64[:, :], sq, sq[:, :], True, True)
                        rs, _ = cx.pool("rstd", 2, [128, TT], F32)
                        P.op("dve", lambda h, ps2=ps2, rs=rs: h.tensor_scalar(out=rs[:, :], in0=ps2[:, :], scalar1=1.0 / 64, scalar2=EPS, op0=ALU.mult, op1=ALU.add), reads=[ps2], writes=[rs])
                        sc = 64.0 if kind == "q" else 1.0
                        P.op("act", lambda h, rs=rs, sc=sc: h.activation(out=rs[:, :], in_=rs[:, :], func=AF.Sqrt, scale=sc), reads=[rs], writes=[rs])
                        P.op("dve", lambda h, rs=rs: h.reciprocal(out=rs[:, :], in_=rs[:, :]), reads=[rs], writes=[rs])
                        gi = {"kslc": 0, "kwin": 1, "q": 2}[kind]
                        ob, ko = cx.pool("obb", 3, [128, TT], BF16)
                        P.op("dve", lambda h, z=z, rs=rs, ob=ob, gi=gi: h.scalar_tensor_tensor(out=ob[:, :], in0=z[:, :], scalar=gk[:, gi:gi + 1], in1=rs[:, :], op0=ALU.mult, op1=ALU.mult),
                             reads=[z, rs, gk], writes=[ob])
                        if kind == "q":
                            P.op("sp", lambda h, ob=ob, k=k, t0=t0: h.dma_start(out=G["qn"][k * 128:(k + 1) * 128, t0 - NOWN:t0 - NOWN + TT], in_=ob[:, :]), reads=[ob], dma=ko)
                        else:
                            dst = G["ks"] if kind == "kslc" else G["kw"]
                            P.op("sp", lambda h, ob=ob, dst=dst, k=k, t0=t0: h.dma_start(out=dst[k * 128:(k + 1) * 128, t0:t0 + TT], in_=ob[:, :]), reads=[ob], dma=ko)
                    deferred.append(part_b)
                else:
                    z, _ = cx.pool("z", 5, [128, TT], F32)
                    P.op("act", lambda h, ps=ps, z=z, bcol=bcol: h.activation(out=z[:, :], in_=ps[:, :], func=AF.Identity, bias=bcol), reads=[ps, bi], writes=[z])
                    def part_b(ps=ps, z=z, t0=t0, k=k, kind=kind, j=j, own=own):
                        pst = cx.ps()
                        for s in range(4):
                            P.op("pe", lambda h, pst=pst, z=z, s=s: h.transpose(out=pst[:, s * 128:(s + 1) * 128], in_=z[:, s * 128:(s + 1) * 128], identity=identf[:, :]),
                                 reads=[z, identf], writes=[pst])
                        if kind == "small":
                            ob, ko = cx.pool("otf", 2, [128, TT], F32)
                            P.op("dve", lambda h, pst=pst, ob=ob: h.tensor_copy(out=ob[:, :], in_=pst[:, :]), reads=[pst], writes=[ob])
                            dstv = G["small"].rearrange("(s p) f -> p s f", p=128)
                            P.op("sp", lambda h, ob=ob, dstv=dstv, t0=t0: h.dma_start(out=dstv[:, t0 // 128:t0 // 128 + 4, :], in_=ob[:, :].rearrange("p (s f) -> p s f", s=4)), reads=[ob], dma=ko)
                        else:
                            ob, ko = cx.pool("otb", 2, [128, TT], BF16)
                            if own or kind == "mv":
                                P.op("dve", lambda h, pst=pst, ob=ob: h.tensor_copy(out=ob[:, :], in_=pst[:, :]), reads=[pst], writes=[ob])
                            else:
                                P.op("dve", lambda h, pst=pst, ob=ob: h.tensor_scalar(out=ob[:, :], in0=pst[:, :], scalar1=vcol[:, 0:1], scalar2=None, op0=ALU.mult), reads=[pst, vcol], writes=[ob])
                            if kind == "mv":
                                dstv = G["mv"].rearrange("(s p) f -> p s f", p=128)[:, t0 // 128:t0 // 128 + 4, k * 128:(k + 1) * 128]
                            else:
                                br = 0 if kind == "vslc" else 1
                                dstv = G["v"].rearrange("(s p) b f -> p s b f", p=128)[:, t0 // 128:t0 // 128 + 4, br, k * 128:(k + 1) * 128]
                            P.op("sp", lambda h, ob=ob, dstv=dstv: h.dma_start(out=dstv, in_=ob[:, :].rearrange("p (s f) -> p s f", s=4)), reads=[ob], dma=ko)
                    deferred.append(part_b)
        while deferred:
            deferred.pop(0)()
    cx.end()


def cmp_stage(cx, G):
    P = cx.P
    cx.begin(8)
    ones_bf, validc = G["ones_bf"], G["validc"]
    KcT, Vc = G["KcT"], G["Vc"]
    for kv in range(2):
        w1 = cx.sb([64, 32 * 256], BF16, "w1")
        P.op("pool", lambda h, kv=kv, w1=w1: h.dma_start(out=w1[:, :], in_=G["w1"][kv, :, :], max_dma_last_dim=8192), writes=[w1], dma="w1")
        w2 = cx.sb([128, 2 * 64], BF16, "w2")
        P.op("pool", lambda h, kv=kv, w2=w2: h.dma_start(out=w2[:, :], in_=G["w2"][kv, :, :]), writes=[w2], dma="w2")
        posT = cx.sb([64, 32], BF16, "posT")
        P.op("pool", lambda h, kv=kv, posT=posT: h.dma_start(out=posT[:, :], in_=G["posT"][kv, :, :]), writes=[posT], dma="posT")
        b1 = cx.sb([128, 2], F32, "b1")
        for hc in range(2):
            ps = cx.ps()
            for j in range(32):
                mm(P, ps, ps[:, 0:1], w1, w1[:, j * 256 + hc * 128:j * 256 + hc * 128 + 128], posT, posT[:, j:j + 1], j == 0, j == 31)
            P.op("dve", lambda h, ps=ps, hc=hc, b1=b1: h.tensor_copy(out=b1[:, hc:hc + 1], in_=ps[:, 0:1]), reads=[ps], writes=[b1])
        for g in range(4):
            kvT, kk = cx.pool("kvT", 2, [64, SEQV + 32], BF16)
            P.op("dve", lambda h, kvT=kvT: h.memset(kvT[:, SEQV:SEQV + 32], 0.0), writes=[kvT])
            P.op("sp", lambda h, kvT=kvT, kv=kv, g=g: h.dma_start(out=kvT[:, 0:SEQV], in_=G["kvc"][kv * 256 + g * 64:kv * 256 + g * 64 + 64, :]), writes=[kvT], dma=kk)
            gT, _ = cx.pool("gT", 2, [128, 2, 256], BF16)
            for hc in range(2):
                ps = cx.ps()
                for j in range(32):
                    mm(P, ps, ps[:, 0:256], w1, w1[:, j * 256 + hc * 128:j * 256 + hc * 128 + 128], kvT, kvT[:, j:j + SEQV:16], j == 0, j == 31)
                z, _ = cx.pool("cz", 2, [128, 256], F32)
                t, _ = cx.pool("ct", 2, [128, 256], F32)
                P.op("act", lambda h, ps=ps, z=z, hc=hc, b1=b1: h.activation(out=z[:, :], in_=ps[:, 0:256], func=AF.Identity, bias=b1[:, hc:hc + 1]), reads=[ps, b1], writes=[z])
                P.op("act", lambda h, z=z, t=t: h.activation(out=t[:, :], in_=z[:, :], func=AF.Square), reads=[z], writes=[t])
                P.op("dve", lambda h, t=t: h.tensor_scalar(out=t[:, :], in0=t[:, :], scalar1=0.044715, scalar2=1.0, op0=ALU.mult, op1=ALU.add), reads=[t], writes=[t])
                P.op("dve", lambda h, t=t, z=z: h.tensor_tensor(out=t[:, :], in0=t[:, :], in1=z[:, :], op=ALU.mult), reads=[t, z], writes=[t])
                P.op("act", lambda h, t=t: h.activation(out=t[:, :], in_=t[:, :], func=AF.Sigmoid, scale=1.5957691216057308), reads=[t], writes=[t])
                P.op("dve", lambda h, t=t, z=z, gT=gT, hc=hc: h.tensor_tensor(out=gT[:, hc, :], in0=t[:, :], in1=z[:, :], op=ALU.mult), reads=[t, z], writes=[gT])
            if kv == 0:
                ps = cx.ps()
                for hc in range(2):
                    mm(P, ps, ps[0:64, 0:256], w2, w2[:, hc * 64:(hc + 1) * 64], gT, gT[:, hc, :], hc == 0, hc == 1)
                zk, _ = cx.pool("zk", 2, [64, 256], F32)
                sq, _ = cx.pool("sqk", 2, [64, 256], BF16)
                P.op("act", lambda h, ps=ps, zk=zk: h.activation(out=zk[:, :], in_=ps[0:64, 0:256], func=AF.Copy), reads=[ps], writes=[zk])
                P.op("act", lambda h, zk=zk, sq=sq: h.activation(out=sq[:, :], in_=zk[:, :], func=AF.Square), reads=[zk], writes=[sq])
                ps2 = cx.ps()
                mm(P, ps2, ps2[0:64, 0:256], ones_bf, ones_bf[0:64, 0:64], sq, sq[:, :], True, True)
                rs, _ = cx.pool("rsk", 2, [64, 256], F32)
                P.op("dve", lambda h, ps2=ps2, rs=rs: h.tensor_scalar(out=rs[:, :], in0=ps2[0:64, 0:256], scalar1=1.0 / 64, scalar2=EPS, op0=ALU.mult, op1=ALU.add), reads=[ps2], writes=[rs])
                P.op("act", lambda h, rs=rs: h.activation(out=rs[:, :], in_=rs[:, :], func=AF.Sqrt), reads=[rs], writes=[rs])
                P.op("dve", lambda h, rs=rs: h.reciprocal(out=rs[:, :], in_=rs[:, :]), reads=[rs], writes=[rs])
                P.op("dve", lambda h, zk=zk, rs=rs, g=g: h.scalar_tensor_tensor(out=KcT[:, g, :], in0=zk[:, :], scalar=G["gk_sb"][0:64, 3:4], in1=rs[:, :], op0=ALU.mult, op1=ALU.mult),
                     reads=[zk, rs, G["gk_sb"]], writes=[KcT])
            else:
                for ct in range(2):
                    ps = cx.ps()
                    for hc in range(2):
                        mm(P, ps, ps[:, 0:64], gT, gT[:, hc, ct * 128:(ct + 1) * 128], w2, w2[:, hc * 64:(hc + 1) * 64], hc == 0, hc == 1)
                    P.op("dve", lambda h, ps=ps, g=g, ct=ct: h.tensor_scalar(out=Vc[:, g, ct, 0:64], in0=ps[:, 0:64], scalar1=validc[:, ct:ct + 1], scalar2=None, op0=ALU.mult),
                         reads=[ps, validc], writes=[Vc])
                    P.op("dve", lambda h, g=g, ct=ct: h.tensor_copy(out=Vc[:, g, ct, 64:65], in_=validc[:, ct:ct + 1]), reads=[validc], writes=[Vc])
                    P.op("dve", lambda h, g=g, ct=ct: h.tensor_scalar(out=Vc[:, g, ct, 65:129], in0=G["ov"][:, ct, :], scalar1=validc[:, ct:ct + 1], scalar2=None, op0=ALU.mult),
                         reads=[validc, G["ov"]], writes=[Vc])
    cx.end()


def nsa_stage(cx, G):
    P = cx.P
    cx.begin(4)
    acc = cx.psum[4:8]
    identb, identf, vcol = G["identb"], G["identf"], G["vcol"]
    KcT, Vc = G["KcT"], G["Vc"]
    cpen = cx.sb([128, 4, 512], BF16, "cpen")
    wpen = cx.sb([128, 8, 512], BF16, "wpen")
    cmpp = cx.sb([128, 8, 512], BF16, "cmpp")
    mulm = cx.sb([128, 16, 64], F32, "mulm")
    addm = cx.sb([128, 16, 64], F32, "addm")
    sgate = cx.sb([128, 16, 48], F32, "sgate")
    for t, src in ((cpen, G["c_cpen"]), (wpen, G["c_wpen"]), (cmpp, G["c_cmpp"]), (mulm, G["c_mulm"]), (addm, G["c_addm"])):
        P.op("sp", lambda h, t=t, src=src: h.dma_start(out=t.ap, in_=src), writes=[t], dma="nc")
    P.op("sp", lambda h: h.dma_start(out=sgate[:, :, :], in_=G["small"].rearrange("(s p) f -> p s f", p=128)[:, 16:32, 8:56]), writes=[sgate], dma="nc")
    P.barrier()
    P.op("act", lambda h: h.activation(out=sgate[:, :, :], in_=sgate[:, :, :], func=AF.Sigmoid), reads=[sgate], writes=[sgate])
    vview = G["v"].rearrange("(s p) b (g d) -> p s b g d", p=128, g=4)
    cg = conv_gen(cx, G)
    jobctr = [0]

    def evac(accs, qt, g, r, branch, oacc, impa=None):
        W = 129 if branch == 0 else 65
        stg, _ = cx.pool("stg%d" % W, 3, [128, 4, W], F32)
        if branch == 0:
            for hb in range(2):
                a = accs[hb]
                P.op("dve", lambda h, a=a, stg=stg, hb=hb: h.tensor_copy(out=stg[:, 2 * hb:2 * hb + 2, :], in_=a[:, 0:2 * W].rearrange("p (s w) -> p s w", s=2)), reads=[a], writes=[stg])
        else:
            a = accs
            P.op("dve", lambda h, a=a, stg=stg: h.tensor_copy(out=stg[:, :, :], in_=a[:, 0:4 * W].rearrange("p (s w) -> p s w", s=4)), reads=[a], writes=[stg])
        rl, _ = cx.pool("rl", 4, [128, 4, 2], F32)
        gc = (g * 4 + r) * 3 + branch
        P.op("dve", lambda h, stg=stg, rl=rl: h.tensor_scalar(out=rl[:, :, 0:1], in0=stg[:, :, 64:65], scalar1=1e-30, scalar2=None, op0=ALU.max), reads=[stg], writes=[rl])
        P.op("dve", lambda h, rl=rl: h.reciprocal(out=rl[:, :, 0:1], in_=rl[:, :, 0:1]), reads=[rl], writes=[rl])
        P.op("dve", lambda h, rl=rl, qt=qt, gc=gc: h.tensor_tensor(out=rl[:, :, 1:2], in0=rl[:, :, 0:1], in1=sgate[:, qt * 4:qt * 4 + 4, gc:gc + 1], op=ALU.mult), reads=[rl, sgate], writes=[rl])
        tmp, _ = cx.pool("etmp", 3, [128, 4, 64], F32)
        P.op("dve", lambda h, stg=stg, rl=rl, tmp=tmp: h.tensor_tensor(out=tmp[:, :, :], in0=stg[:, :, 0:64], in1=rl[:, :, 1:2].to_broadcast([128, 4, 64]), op=ALU.mult), reads=[stg, rl], writes=[tmp])
        P.op("dve", lambda h, tmp=tmp, r=r, oacc=oacc: h.tensor_tensor(out=oacc[:, :, r * 64:(r + 1) * 64], in0=oacc[:, :, r * 64:(r + 1) * 64], in1=tmp[:, :, :], op=ALU.add), reads=[tmp, oacc], writes=[oacc])
        if impa is not None:
            if r == 0:
                P.op("dve", lambda h, stg=stg, rl=rl, impa=impa: h.tensor_tensor(out=impa[:, :, :], in0=stg[:, :, 65:129], in1=rl[:, :, 0:1].to_broadcast([128, 4, 64]), op=ALU.mult), reads=[stg, rl], writes=[impa])
            else:
                tm2, _ = cx.pool("etmp", 3, [128, 4, 64], F32)
                P.op("dve", lambda h, stg=stg, rl=rl, tm2=tm2: h.tensor_tensor(out=tm2[:, :, :], in0=stg[:, :, 65:129], in1=rl[:, :, 0:1].to_broadcast([128, 4, 64]), op=ALU.mult), reads=[stg, rl], writes=[tm2])
                P.op("dve", lambda h, tm2=tm2, impa=impa: h.tensor_tensor(out=impa[:, :, :], in0=impa[:, :, :], in1=tm2[:, :, :], op=ALU.add), reads=[tm2, impa], writes=[impa])

    pending_out = []
    KsEs, KsEes = [], []
    for i_ in range(2):
        t_ = cx.sb([128, SEQV], BF16, "KsE%d" % i_)
        e_ = T(t_.ap, "KsEe%d" % i_)
        P.op("sp", lambda h, t_=t_: h.dma_start(out=t_[64:128, :], in_=G["c_Eexp"]), writes=[e_], dma="nc")
        KsEs.append(t_)
        KsEes.append(e_)
    P.barrier()

    def load_group(g):
        KsT, k1 = KsEs[g % 2], "KsT%d" % (g % 2)
        KwT, k2 = cx.pool("KwT", 2, [64, SEQV], BF16)
        Vs, k3 = cx.pool("Vs", 2, [128, 32, 65], BF16)
        Vw, k4 = cx.pool("Vw", 2, [128, 32, 65], BF16)
        P.op("sp", lambda h, g=g, KsT=KsT: h.dma_start(out=KsT[0:64, :], in_=G["ks"][g * 64:(g + 1) * 64, :]), writes=[KsT], dma=k1)
        P.op("sp", lambda h, g=g, KwT=KwT: h.dma_start(out=KwT[:, :], in_=G["kw"][g * 64:(g + 1) * 64, :]), writes=[KwT], dma=k2)
        for br, V, kk in ((0, Vs, k3), (1, Vw, k4)):
            for hf in range(2):
                P.op("sp", lambda h, g=g, V=V, br=br, hf=hf: h.dma_start(out=V[:, hf * 16:(hf + 1) * 16, 0:64], in_=vview[:, hf * 16:(hf + 1) * 16, br, g, :]), writes=[V], dma=kk)
            P.op("dve", lambda h, V=V: h.tensor_copy(out=V[:, 0:16, 64:65], in_=vcol[:, 0:1].unsqueeze(1).to_broadcast([128, 16, 1])), reads=[vcol], writes=[V])
            P.op("dve", lambda h, V=V: h.memset(V[:, 16:32, 64:65], 1.0), writes=[V])
        return KsT, KsEes[g % 2], KwT, Vs, Vw

    def load_q(g, qt):
        QT, kq = cx.pool("QT", 3, [128, 4, TT], BF16)
        for r in range(4):
            P.op("sp", lambda h, QT=QT, r=r, g=g, qt=qt: h.dma_start(out=QT[0:64, r, :], in_=G["qn"][(g * 4 + r) * 64:(g * 4 + r + 1) * 64, qt * TT:(qt + 1) * TT]), writes=[QT], dma=kq)
        return QT

    qt_next = [None]
    nxt_grp = load_group(0)
    for g in range(4):
        KsE, KsEe, KwT, Vs, Vw = nxt_grp
        if g < 3:
            nxt_grp = load_group(g + 1)
        for qt in range(4):
            q0 = NOWN + qt * TT
            nkt = q0 // 128 + 4
            if qt_next[0] is None:
                qt_next[0] = load_q(g, qt)
            QT = qt_next[0]
            nb_ = g * 4 + qt + 1
            qt_next[0] = load_q(nb_ // 4, nb_ % 4) if nb_ < 16 else None
            QTp = T(QT.ap, "QTp")
            oacc, _ = cx.pool("oacc", 2, [128, 4, 256], F32)
            impa, _ = cx.pool("impa", 2, [128, 4, 64], F32)
            P.op("dve", lambda h, oacc=oacc: h.memset(oacc[:, :, :], 0.0), writes=[oacc])
            def run_jobs(jobs, depth=4, hooks=None):
                q = []
                for ji, (sc_fn, pv_fn) in enumerate(jobs):
                    if hooks and ji in hooks:
                        hooks[ji]()
                    q.append((sc_fn(), pv_fn))
                    jobctr[0] += 1
                    if jobctr[0] % 6 == 0:
                        next(cg, None)
                    if len(q) > depth:
                        pc, f_ = q.pop(0)
                        f_(pc)
                for pc, f_ in q:
                    f_(pc)

            def cmp_score(r):
                Pc = []
                for ct in range(2):
                    ps = cx.ps()
                    mm(P, ps, ps[:, :], KcT, KcT[:, g, ct * 128:(ct + 1) * 128], QT, QT[0:64, r, :], True, False)
                    mm(P, ps, ps[:, :], identb, identb[:, :], cmpp, cmpp[:, qt * 2 + ct, :], False, True)
                    pc, _ = cx.pool("Ptc", 4, [128, TT], BF16)
                    P.op("act", lambda h, ps=ps, pc=pc: h.activation(out=pc[:, :], in_=ps[:, :], func=AF.Exp), reads=[ps], writes=[pc])
                    Pc.append(pc)
                return Pc

            def cmp_pv(r, Pc):
                for sub in range(4):
                    a = acc[2 + sub // 2]
                    o0 = (sub % 2) * 129
                    for ct in range(2):
                        mm(P, a, a[:, o0:o0 + 129], Pc[ct], Pc[ct][:, sub * 128:(sub + 1) * 128], Vc, Vc[:, g, ct, :], (ct == 0 and sub % 2 == 0), (ct == 1 and sub % 2 == 1))
                evac(acc[2:4], qt, g, r, 0, oacc, impa)

            run_jobs([((lambda r=r: cmp_score(r)), (lambda Pc, r=r: cmp_pv(r, Pc))) for r in range(4)], depth=1)
            pens = []
            for sub in range(4):
                i2, _ = cx.pool("i2", 2, [128, 64], F32)
                i3, _ = cx.pool("i3", 2, [128, 64], F32)
                m8, _ = cx.pool("m8", 2, [128, 16], F32)
                penp, _ = cx.pool("pen", 4, [128, 128], F32)
                P.op("dve", lambda h, penp=penp: h.memset(penp[:, 0:64], 0.0), writes=[penp])
                pen = T(penp.ap[:, 64:128], "penv")
                pen.w, pen.r = None, []
                P.op("dve", lambda h, i2=i2, sub=sub, impa=impa, qt=qt: h.tensor_tensor(out=i2[:, :], in0=impa[:, sub, :], in1=mulm[:, qt * 4 + sub, :], op=ALU.mult), reads=[impa, mulm], writes=[i2])
                P.op("dve", lambda h, i2=i2, sub=sub, qt=qt: h.tensor_tensor(out=i2[:, :], in0=i2[:, :], in1=addm[:, qt * 4 + sub, :], op=ALU.add), reads=[i2, addm], writes=[i2])
                P.op("dve", lambda h, i2=i2, m8=m8: h.max(out=m8[:, 0:8], in_=i2[:, :]), reads=[i2], writes=[m8])
                P.op("dve", lambda h, i2=i2, i3=i3, m8=m8: h.match_replace(out=i3[:, :], in_to_replace=m8[:, 0:8], in_values=i2[:, :], imm_value=-1e30), reads=[i2, m8], writes=[i3])
                P.op("dve", lambda h, i3=i3, m8=m8: h.max(out=m8[:, 8:16], in_=i3[:, :]), reads=[i3, m8], writes=[m8])
                P.op("dve", lambda h, i2=i2, m8=m8, pen=pen: h.tensor_scalar(out=pen[:, :], in0=i2[:, :], scalar1=m8[:, 15:16], scalar2=None, op0=ALU.is_ge), reads=[i2, m8], writes=[pen])
                P.op("dve", lambda h, pen=pen: h.tensor_scalar(out=pen[:, :], in0=pen[:, :], scalar1=-1.0, scalar2=-NEGB, op0=ALU.add, op1=ALU.mult), reads=[pen], writes=[pen])
                P.op("dve", lambda h, penp=penp: h.tensor_copy(out=penp[:, 64:65], in_=penp[:, 64:65]), reads=[pen], writes=[penp])
                pens.append(penp)

            def win_score(r, rel):
                kt = q0 // 128 - 4 + rel
                ps = cx.ps()
                c0, c1 = 128 * max(0, rel - 4), 128 * (min(3, rel) + 1)
                mm(P, ps, ps[:, c0:c1], KwT, KwT[:, kt * 128:(kt + 1) * 128], QT, QT[0:64, r, c0:c1], True, True)
                pc, _ = cx.pool("Pt", 7, [128, TT], BF16)
                P.op("act", lambda h, ps=ps, pc=pc, c0=c0, c1=c1: h.activation(out=pc[:, c0:c1], in_=ps[:, c0:c1], func=AF.Exp), reads=[ps], writes=[pc])
                for s_ in (rel, rel - 4):
                    if 0 <= s_ <= 3:
                        a0, a1 = 128 * s_, 128 * (s_ + 1)
                        P.op("pool", lambda h, pc=pc, rel=rel, a0=a0, a1=a1: h.tensor_tensor(out=pc[:, a0:a1], in0=pc[:, a0:a1], in1=wpen[:, rel, a0:a1], op=ALU.mult), reads=[pc, wpen], writes=[pc])
                return pc

            def win_pv(r, rel, pc):
                kt = q0 // 128 - 4 + rel
                for sub in range(4):
                    if not (sub <= rel <= sub + 4):
                        continue
                    a = acc[r % 2]
                    mm(P, a, a[:, sub * 65:(sub + 1) * 65], pc, pc[:, sub * 128:(sub + 1) * 128], Vw, Vw[:, kt, :], (rel == 0 and sub == 0), (rel == 7 and sub == 3))
                if rel == 7:
                    evac(acc[r % 2], qt, g, r, 2, oacc)

            def emit_pen():
                pst = cx.ps()
                for sub in range(4):
                    penp = pens[sub]
                    P.op("pe", lambda h, pst=pst, penp=penp, sub=sub: h.transpose(out=pst[:, sub * 128:(sub + 1) * 128], in_=penp[:, :], identity=identf[:, :]), reads=[penp, identf], writes=[pst])
                for r in range(4):
                    P.op("dve", lambda h, pst=pst, QT=QT, r=r: h.tensor_copy(out=QT[64:128, r, :], in_=pst[64:128, :]), reads=[pst], writes=[QTp])

            def emit_prev_out():
                while pending_out:
                    pending_out.pop(0)()

            run_jobs([((lambda r=r, rel=rel: win_score(r, rel)), (lambda pc, r=r, rel=rel: win_pv(r, rel, pc))) for rp in range(2) for rel in range(8) for r in (2 * rp, 2 * rp + 1)],
                     hooks={6: emit_prev_out, 20: emit_pen})
            def slc_score(r, kt):
                rel = kt - (nkt - 4)
                ps = cx.ps()
                c0 = 128 * max(0, rel)
                P.op("pe", lambda h, ps=ps, kt=kt, r=r, QT=QT, c0=c0, KsE=KsE: h.matmul(ps[:, c0:], KsE[:, kt * 128:(kt + 1) * 128], QT[:, r, c0:], start=True, stop=True),
                     reads=[KsE, KsEe, QT, QTp], writes=[ps])
                pc, _ = cx.pool("Pt", 7, [128, TT], BF16)
                P.op("act", lambda h, ps=ps, pc=pc, c0=c0: h.activation(out=pc[:, c0:], in_=ps[:, c0:], func=AF.Exp), reads=[ps], writes=[pc])
                if rel >= 0:
                    P.op("pool", lambda h, pc=pc, rel=rel, c0=c0: h.tensor_tensor(out=pc[:, c0:c0 + 128], in0=pc[:, c0:c0 + 128], in1=cpen[:, rel, c0:c0 + 128], op=ALU.mult), reads=[pc, cpen], writes=[pc])
                return pc

            def slc_pv(r, kt, pc):
                rel = kt - (nkt - 4)
                for sub in range(4):
                    if rel > sub:
                        continue
                    a = acc[r % 2]
                    mm(P, a, a[:, sub * 65:(sub + 1) * 65], pc, pc[:, sub * 128:(sub + 1) * 128], Vs, Vs[:, kt, :], (kt == 0 and sub == 0), (kt == nkt - 1 and sub == 3))
                if kt == nkt - 1:
                    evac(acc[r % 2], qt, g, r, 1, oacc)

            run_jobs([((lambda r=r, kt=kt: slc_score(r, kt)), (lambda pc, r=r, kt=kt: slc_pv(r, kt, pc))) for rp in range(2) for kt in range(nkt) for r in (2 * rp, 2 * rp + 1)])
            def emit_out(oacc=oacc, g=g, qt=qt):
                for sub in range(4):
                    for hp in range(2):
                        pst = cx.ps()
                        P.op("pe", lambda h, pst=pst, oacc=oacc, sub=sub, hp=hp: h.transpose(out=pst[:, 0:128], in_=oacc[:, sub, hp * 128:(hp + 1) * 128], identity=identf[:, :]),
                             reads=[oacc, identf], writes=[pst])
                        ob, ko = cx.pool("onb", 3, [128, 128], BF16)
                        P.op("act", lambda h, pst=pst, ob=ob: h.activation(out=ob[:, :], in_=pst[:, 0:128], func=AF.Copy), reads=[pst], writes=[ob])
                        r0 = (g * 4 + hp * 2) * 64
                        c0 = qt * TT + sub * 128
                        P.op("sp", lambda h, ob=ob, r0=r0, c0=c0: h.dma_start(out=G["on"][r0:r0 + 128, c0:c0 + 128], in_=ob[:, :]), reads=[ob], dma=ko)
            pending_out.append(emit_out)
    for f_ in pending_out:
        f_()
    del pending_out[:]
    for _ in cg:
        pass
    cx.end()


def conv_gen(cx, G):
    P = cx.P
    identf, vcol = G["identf"], G["vcol"]
    cw, cb = G["cw_sb"], G["cb_sb"]
    HS = NOWN
    for fc in range(16):
        isq = fc < 8
        src = G["mqr"] if isq else G["mkr"]
        f0 = (fc % 8) * 128
        for seg in ([1] if isq else [0, 1]):
            u, ku = cx.pool("cu", 2, [128, 3 + HS], F32)
            if seg == 0:
                P.op("dve", lambda h, u=u: h.memset(u[:, 0:3], 0.0), writes=[u])
                yield
                P.op("sp", lambda h, u=u, src=src, f0=f0: h.dma_start(out=u[:, 3:3 + HS], in_=src[f0:f0 + 128, 0:HS]), writes=[u], dma=ku)
                yield
            else:
                P.op("sp", lambda h, u=u, src=src, f0=f0: h.dma_start(out=u[:, 0:3 + HS], in_=src[f0:f0 + 128, HS - 3:2 * HS]), writes=[u], dma=ku)
                yield
                P.op("dve", lambda h, u=u: h.tensor_scalar(out=u[:, 0:3], in0=u[:, 0:3], scalar1=vcol[:, 0:1], scalar2=None, op0=ALU.mult), reads=[u, vcol], writes=[u])
                yield
            a, _ = cx.pool("ca", 2, [128, HS], F32)
            P.op("dve", lambda h, u=u, a=a, fc=fc: h.tensor_scalar(out=a[:, :], in0=u[:, 0:HS], scalar1=cw[:, fc, 0:1], scalar2=cb[:, fc:fc + 1], op0=ALU.mult, op1=ALU.add),
                 reads=[u, cw, cb], writes=[a])
            yield
            for j in range(1, 4):
                P.op("dve", lambda h, u=u, a=a, fc=fc, j=j: h.scalar_tensor_tensor(out=a[:, :], in0=u[:, j:j + HS], scalar=cw[:, fc, j:j + 1], in1=a[:, :], op0=ALU.mult, op1=ALU.add),
                     reads=[u, a, cw], writes=[a])
                yield
            P.op("act", lambda h, a=a: h.activation(out=a[:, :], in_=a[:, :], func=AF.Silu), reads=[a], writes=[a])
            yield
            ob, ko = cx.pool("cob", 2, [128, HS], BF16)
            if isq:
                P.op("dve", lambda h, a=a, ob=ob: h.tensor_copy(out=ob[:, :], in_=a[:, :]), reads=[a], writes=[ob])
                yield
                P.op("sp", lambda h, ob=ob, f0=f0: h.dma_start(out=G["qc"][f0:f0 + 128, :], in_=ob[:, :]), reads=[ob], dma=ko)
                yield
            else:
                P.op("dve", lambda h, a=a, ob=ob: h.tensor_scalar(out=ob[:, :], in0=a[:, :], scalar1=1.0 / 16, scalar2=None, op0=ALU.mult), reads=[a], writes=[ob])
                yield
                P.op("sp", lambda h, ob=ob, f0=f0, seg=seg: h.dma_start(out=G["kc"][f0:f0 + 128, seg * HS:(seg + 1) * HS], in_=ob[:, :]), reads=[ob], dma=ko)
                yield
                for i4 in range(4):
                    pst = cx.ps()
                    for s in range(4):
                        i = i4 * 4 + s
                        P.op("pe", lambda h, pst=pst, a=a, s=s, i=i: h.transpose(out=pst[:, s * 128:(s + 1) * 128], in_=a[:, i * 128:(i + 1) * 128], identity=identf[:, :]),
                             reads=[a, identf], writes=[pst])
                    kt_, kk = cx.pool("ckt", 2, [128, TT], BF16)
                    P.op("act", lambda h, pst=pst, kt_=kt_: h.activation(out=kt_[:, :], in_=pst[:, :], func=AF.Copy, scale=1.0 / 16), reads=[pst], writes=[kt_])
                    tt0 = (seg * HS) // 128 + i4 * 4
                    dstv = G["kctm"].rearrange("(s p) f -> p s f", p=128)[:, tt0:tt0 + 4, f0:f0 + 128]
                    P.op("sp", lambda h, kt_=kt_, dstv=dstv: h.dma_start(out=dstv, in_=kt_[:, :].rearrange("p (s f) -> p s f", s=4)), reads=[kt_], dma=kk)
                    yield


def mlstm_stage(cx, G):
    P = cx.P
    cx.begin(8)
    ones_bf, identf, vcol, triu, ones_f = G["ones_bf"], G["identf"], G["vcol"], G["triu"], G["ones_f"]
    cw, cb = G["cw_sb"], G["cb_sb"]
    HS = NOWN
    small = cx.sb([128, 32, 8], F32, "small")
    logf = cx.sb([128, 32, 4], F32, "logf")
    P.op("sp", lambda h: h.dma_start(out=small[:, :, :], in_=G["small"].rearrange("(s p) f -> p s f", p=128)[:, :, 0:8]), writes=[small], dma="ms")
    P.op("act", lambda h: h.activation(out=logf[:, :, :], in_=small[:, :, 4:8], func=AF.Exp, scale=-1.0), reads=[small], writes=[logf])
    P.op("dve", lambda h: h.tensor_scalar(out=logf[:, :, :], in0=logf[:, :, :], scalar1=1.0, scalar2=None, op0=ALU.add), reads=[logf], writes=[logf])
    P.op("act", lambda h: h.activation(out=logf[:, :, :], in_=logf[:, :, :], func=AF.Ln), reads=[logf], writes=[logf])
    P.op("dve", lambda h: h.tensor_scalar(out=logf[:, :, :], in0=logf[:, :, :], scalar1=-1.0, scalar2=None, op0=ALU.mult), reads=[logf], writes=[logf])
    C = [cx.sb([128, 2, 256], F32, "C%d" % h_) for h_ in range(4)]
    Cb = [cx.sb([128, 2, 256], BF16, "Cb%d" % h_) for h_ in range(4)]
    nb = [cx.sb([128, 2, 128], F32, "nb%d" % h_) for h_ in range(4)]
    nbb = [cx.sb([128, 2, 128], BF16, "nbb%d" % h_) for h_ in range(4)]
    for t in C + Cb + nb + nbb:
        P.op("dve", lambda h, t=t: h.memset(t[:, :, :], 0.0), writes=[t])
    kcv = G["kc"].rearrange("(c p) t -> p c t", p=128)
    qcv = G["qc"].rearrange("(c p) t -> p c t", p=128)
    hcv = G["hc"].rearrange("(c p) t -> p c t", p=128)
    def prep_tile(tile):
        own = tile >= 16
        Wm = ea = qT = None
        R, _ = cx.pool("R", 2, [128, 4, 128], F32)
        P.op("dve", lambda h, R=R, tile=tile: h.tensor_tensor(out=R[:, :, :], in0=triu[:, :].unsqueeze(1).to_broadcast([128, 4, 128]),
                                                              in1=logf[:, tile, :].unsqueeze(2).to_broadcast([128, 4, 128]), op=ALU.mult), reads=[triu, logf], writes=[R])
        psA = cx.ps()
        mm(P, psA, psA[:, :], ones_f, ones_f[:, :], R, R[:, :, :].rearrange("p a b -> p (a b)"), True, True)
        psB = cx.ps()
        mm(P, psB, psB[:, 0:4], triu, triu[:, :], logf, logf[:, tile, :], True, True)
        mm(P, psB, psB[:, 4:8], ones_f, ones_f[:, :], logf, logf[:, tile, :], True, True)
        bc, _ = cx.pool("bc", 2, [128, 4], F32)
        ucol, _ = cx.pool("ucol", 2, [128, 4], F32)
        eg, _ = cx.pool("eg", 2, [128, 4], F32)
        P.op("dve", lambda h, bc=bc, psB=psB, tile=tile: h.tensor_tensor(out=bc[:, :], in0=small[:, tile, 0:4], in1=psB[:, 0:4], op=ALU.subtract), reads=[small, psB], writes=[bc])
        P.op("dve", lambda h, bc=bc, ucol=ucol, psB=psB: h.tensor_tensor(out=ucol[:, :], in0=bc[:, :], in1=psB[:, 4:8], op=ALU.add), reads=[bc, psB], writes=[ucol])
        P.op("act", lambda h, ucol=ucol: h.activation(out=ucol[:, :], in_=ucol[:, :], func=AF.Exp), reads=[ucol], writes=[ucol])
        P.op("act", lambda h, eg=eg, psB=psB: h.activation(out=eg[:, :], in_=psB[:, 4:8], func=AF.Exp), reads=[psB], writes=[eg])
        kT, k1 = cx.pool("kT", 2, [128, 8, 128], BF16)
        kM, k2 = cx.pool("kM", 2, [128, 1024], BF16)
        vM, k3 = cx.pool("vM", 2, [128, 1024], BF16)
        P.op("sp", lambda h, kM=kM, tile=tile: h.dma_start(out=kM[:, :], in_=G["kctm"][tile * 128:(tile + 1) * 128, :]), writes=[kM], dma=k2)
        P.op("sp", lambda h, vM=vM, tile=tile: h.dma_start(out=vM[:, :], in_=G["mv"][tile * 128:(tile + 1) * 128, :]), writes=[vM], dma=k3)
        if own:
            P.op("sp", lambda h, kT=kT, tile=tile: h.dma_start(out=kT[:, :, :], in_=kcv[:, :, tile * 128:(tile + 1) * 128]), writes=[kT], dma=k1)
            qT, k4 = cx.pool("qT", 2, [128, 8, 128], BF16)
            P.op("sp", lambda h, qT=qT, tile=tile: h.dma_start(out=qT[:, :, :], in_=qcv[:, :, (tile - 16) * 128:(tile - 15) * 128]), writes=[qT], dma=k4)
            Wm, _ = cx.pool("Wm", 2, [128, 4, 128], F32)
            ea, _ = cx.pool("ea", 2, [128, 4, 128], F32)
            P.op("dve", lambda h, Wm=Wm, psA=psA, bc=bc: h.tensor_tensor(out=Wm[:, :, :], in0=psA[:, :].rearrange("p (a b) -> p a b", a=4),
                                                                         in1=bc[:, :].unsqueeze(2).to_broadcast([128, 4, 128]), op=ALU.add), reads=[psA, bc], writes=[Wm])
            P.op("act", lambda h, Wm=Wm: h.activation(out=Wm[:, :, :], in_=Wm[:, :, :], func=AF.Exp), reads=[Wm], writes=[Wm])
            P.op("dve", lambda h, Wm=Wm: h.tensor_tensor(out=Wm[:, :, :], in0=Wm[:, :, :], in1=triu[:, :].unsqueeze(1).to_broadcast([128, 4, 128]), op=ALU.mult), reads=[Wm, triu], writes=[Wm])
            P.op("act", lambda h, ea=ea, psA=psA: h.activation(out=ea[:, :, :], in_=psA[:, :].rearrange("p (a b) -> p a b", a=4), func=AF.Exp), reads=[psA], writes=[ea])
        return dict(own=own, bc=bc, ucol=ucol, eg=eg, kT=kT, kM=kM, vM=vM, qT=qT, Wm=Wm, ea=ea)

    nxt_prep = prep_tile(0)
    for tile in range(32):
        cur_ = nxt_prep
        if tile + 1 < 32:
            nxt_prep = prep_tile(tile + 1)
        own, bc, ucol, eg, kT, kM, vM, qT, Wm, ea = (cur_[k_] for k_ in ("own", "bc", "ucol", "eg", "kT", "kM", "vM", "qT", "Wm", "ea"))
        for hp in range(2):
            hds = (2 * hp, 2 * hp + 1)
            st = {}
            if own:
                for hd in hds:
                    ps = cx.ps()
                    for dc in range(2):
                        mm(P, ps, ps[:, 0:128], kT, kT[:, 2 * hd + dc, :], qT, qT[:, 2 * hd + dc, :], dc == 0, dc == 1)
                    st[hd, "ps"] = ps
            for hd in hds:
                if own:
                    ps = st[hd, "ps"]
                    AT, _ = cx.pool("AT", 4, [128, 128], BF16)
                    P.op("dve", lambda h, AT=AT, ps=ps, Wm=Wm, hd=hd: h.tensor_tensor(out=AT[:, :], in0=ps[:, 0:128], in1=Wm[:, hd, :], op=ALU.mult), reads=[ps, Wm], writes=[AT])
                    qs, _ = cx.pool("qs", 4, [128, 2, 128], BF16)
                    P.op("dve", lambda h, qs=qs, qT=qT, ea=ea, hd=hd: h.tensor_tensor(out=qs[:, :, :], in0=qT[:, 2 * hd:2 * hd + 2, :], in1=ea[:, hd:hd + 1, :].to_broadcast([128, 2, 128]), op=ALU.mult),
                         reads=[qT, ea], writes=[qs])
                    st[hd, "AT"] = AT
                    st[hd, "qs"] = qs
                uk, _ = cx.pool("uk", 4, [128, 256], BF16)
                P.op("dve", lambda h, uk=uk, kM=kM, ucol=ucol, hd=hd: h.tensor_scalar(out=uk[:, :], in0=kM[:, hd * 256:(hd + 1) * 256], scalar1=ucol[:, hd:hd + 1], scalar2=None, op0=ALU.mult),
                     reads=[kM, ucol], writes=[uk])
                st[hd, "uk"] = uk
            for hd in hds:
                if own:
                    AT, qs = st[hd, "AT"], st[hd, "qs"]
                    pn = cx.ps()
                    for dch in range(2):
                        o_ = pn[:, dch * 128:(dch + 1) * 128]
                        mm(P, pn, o_, vM, vM[:, hd * 256 + dch * 128:hd * 256 + (dch + 1) * 128], AT, AT[:, :], True, False)
                        for ec in range(2):
                            mm(P, pn, o_, Cb[hd], Cb[hd][:, ec, dch * 128:(dch + 1) * 128], qs, qs[:, ec, :], False, ec == 1)
                    o_ = pn[:, 256:384]
                    mm(P, pn, o_, ones_bf, ones_bf[:, :], AT, AT[:, :], True, False)
                    for ec in range(2):
                        mm(P, pn, o_, nbb[hd], nbb[hd][:, ec, :], qs, qs[:, ec, :], False, ec == 1)
                    st[hd, "pn"] = pn
                uk = st[hd, "uk"]
                pc = cx.ps()
                pnb = cx.ps()
                for ec in range(2):
                    mm(P, pc, pc[:, ec * 256:(ec + 1) * 256], uk, uk[:, ec * 128:(ec + 1) * 128], vM, vM[:, hd * 256:(hd + 1) * 256], True, True)
                for ec in range(2):
                    mm(P, pnb, pnb[:, ec * 128:(ec + 1) * 128], uk, uk[:, ec * 128:(ec + 1) * 128], ones_bf, ones_bf[:, :], True, True)
                st[hd, "pc"] = pc
                st[hd, "pnb"] = pnb
            for hd in hds:
                if own:
                    pn = st[hd, "pn"]
                    rd, _ = cx.pool("rd", 4, [128, 128], F32)
                    P.op("act", lambda h, rd=rd, pn=pn: h.activation(out=rd[:, :], in_=pn[:, 256:384], func=AF.Abs), reads=[pn], writes=[rd])
                    P.op("dve", lambda h, rd=rd: h.tensor_scalar(out=rd[:, :], in0=rd[:, :], scalar1=1.0, scalar2=None, op0=ALU.max), reads=[rd], writes=[rd])
                    P.op("dve", lambda h, rd=rd: h.reciprocal(out=rd[:, :], in_=rd[:, :]), reads=[rd], writes=[rd])
                    ho, kh = cx.pool("ho", 4, [128, 2, 128], F32)
                    P.op("dve", lambda h, ho=ho, pn=pn, rd=rd: h.tensor_tensor(out=ho[:, :, :], in0=pn[:, 0:256].rearrange("p (a b) -> p a b", a=2),
                                                                             in1=rd[:, :].unsqueeze(1).to_broadcast([128, 2, 128]), op=ALU.mult), reads=[pn, rd], writes=[ho])
                    P.op("sp", lambda h, ho=ho, hd=hd, tile=tile: h.dma_start(out=hcv[:, 2 * hd:2 * hd + 2, (tile - 16) * 128:(tile - 15) * 128], in_=ho[:, :, :]), reads=[ho], dma=kh)
                pc, pnb = st[hd, "pc"], st[hd, "pnb"]
                Cf = C[hd][:, :, :].rearrange("p a b -> p (a b)")
                nf = nb[hd][:, :, :].rearrange("p a b -> p (a b)")
                P.op("dve", lambda h, Cf=Cf, pc=pc, eg=eg, hd=hd: h.scalar_tensor_tensor(out=Cf, in0=Cf, scalar=eg[:, hd:hd + 1], in1=pc[:, :], op0=ALU.mult, op1=ALU.add), reads=[C[hd], eg, pc], writes=[C[hd]])
                P.op("dve", lambda h, nf=nf, pnb=pnb, eg=eg, hd=hd: h.scalar_tensor_tensor(out=nf, in0=nf, scalar=eg[:, hd:hd + 1], in1=pnb[:, 0:256], op0=ALU.mult, op1=ALU.add), reads=[nb[hd], eg, pnb], writes=[nb[hd]])
                if tile == 15:
                    P.op("dve", lambda h, Cf=Cf, hd=hd: h.tensor_scalar(out=Cf, in0=Cf, scalar1=vcol[:, 0:1], scalar2=None, op0=ALU.mult), reads=[C[hd], vcol], writes=[C[hd]])
                    P.op("dve", lambda h, nf=nf, hd=hd: h.tensor_scalar(out=nf, in0=nf, scalar1=vcol[:, 0:1], scalar2=None, op0=ALU.mult), reads=[nb[hd], vcol], writes=[nb[hd]])
                P.op("act", lambda h, hd=hd: h.activation(out=Cb[hd][:, :, :], in_=C[hd][:, :, :], func=AF.Copy), reads=[C[hd]], writes=[Cb[hd]])
                P.op("act", lambda h, hd=hd: h.activation(out=nbb[hd][:, :, :], in_=nb[hd][:, :, :], func=AF.Copy), reads=[nb[hd]], writes=[nbb[hd]])
    P.barrier()
    mov = G["mo"].rearrange("(c p) t -> p c t", p=128)
    hmv = G["hm"].rearrange("(c p) t -> p c t", p=128)
    mg = G["mgain_sb"]
    for t4 in range(NOWN // TT):
        for hd in range(4):
            hc_, k1 = cx.pool("hcl", 2, [128, 2, TT], F32)
            mo_, k2 = cx.pool("mol", 2, [128, 2, TT], F32)
            P.op("sp", lambda h, hc_=hc_, hd=hd, t4=t4: h.dma_start(out=hc_[:, :, :], in_=hcv[:, 2 * hd:2 * hd + 2, t4 * TT:(t4 + 1) * TT]), writes=[hc_], dma=k1)
            P.op("sp", lambda h, mo_=mo_, hd=hd, t4=t4: h.dma_start(out=mo_[:, :, :], in_=mov[:, 2 * hd:2 * hd + 2, t4 * TT:(t4 + 1) * TT]), writes=[mo_], dma=k2)
            ss = cx.ps()
            for dc in range(2):
                sq, _ = cx.pool("sq", 3, [128, TT], BF16)
                P.op("act", lambda h, sq=sq, hc_=hc_, dc=dc: h.activation(out=sq[:, :], in_=hc_[:, dc, :], func=AF.Square), reads=[hc_], writes=[sq])
                mm(P, ss, ss[:, :], ones_bf, ones_bf[:, :], sq, sq[:, :], dc == 0, dc == 1)
            rs, _ = cx.pool("rstd", 2, [128, TT], F32)
            P.op("dve", lambda h, ss=ss, rs=rs: h.tensor_scalar(out=rs[:, :], in0=ss[:, :], scalar1=1.0 / 256, scalar2=EPS, op0=ALU.mult, op1=ALU.add), reads=[ss], writes=[rs])
            P.op("act", lambda h, rs=rs: h.activation(out=rs[:, :], in_=rs[:, :], func=AF.Sqrt), reads=[rs], writes=[rs])
            P.op("dve", lambda h, rs=rs: h.reciprocal(out=rs[:, :], in_=rs[:, :]), reads=[rs], writes=[rs])
            ob, ko = cx.pool("hmo", 2, [128, 2, TT], BF16)
            for dc in range(2):
                P.op("dve", lambda h, hc_=hc_, rs=rs, dc=dc, hd=hd: h.scalar_tensor_tensor(out=hc_[:, dc, :], in0=hc_[:, dc, :], scalar=mg[:, 2 * hd + dc:2 * hd + dc + 1], in1=rs[:, :], op0=ALU.mult, op1=ALU.mult),
                     reads=[hc_, rs, mg], writes=[hc_])
            P.op("dve", lambda h, hc_=hc_, mo_=mo_, ob=ob: h.tensor_tensor(out=ob[:, :, :], in0=hc_[:, :, :], in1=mo_[:, :, :], op=ALU.mult), reads=[hc_, mo_], writes=[ob])
            P.op("sp", lambda h, ob=ob, hd=hd, t4=t4: h.dma_start(out=hmv[:, 2 * hd:2 * hd + 2, t4 * TT:(t4 + 1) * TT], in_=ob[:, :, :]), reads=[ob], dma=ko)
    cx.end()


def merge_stage(cx, G):
    P = cx.P
    cx.begin(8)
    onv = G["on"].rearrange("(c p) t -> p c t", p=128)
    hmv = G["hm"].rearrange("(c p) t -> p c t", p=128)
    mgv = G["mg"].rearrange("(c p) t -> p c t", p=128)
    x1v = G["x1T"].rearrange("(c p) t -> p c t", p=128)
    x2v = G["x2T"].rearrange("(c p) t -> p c t", p=128)
    for t4 in range(NOWN // TS):
        c0 = t4 * TS
        a_, k1 = cx.pool("mon", 1, [128, 8, TS], BF16)
        b_, k2 = cx.pool("mhm", 1, [128, 8, TS], BF16)
        for hf in range(2):
            P.op("sp", lambda h, a_=a_, c0=c0, hf=hf: h.dma_start(out=a_[:, 4 * hf:4 * hf + 4, :], in_=onv[:, 4 * hf:4 * hf + 4, c0:c0 + TS]), writes=[a_], dma=k1)
            P.op("sp", lambda h, b_=b_, c0=c0, hf=hf: h.dma_start(out=b_[:, 4 * hf:4 * hf + 4, :], in_=hmv[:, 4 * hf:4 * hf + 4, c0:c0 + TS]), writes=[b_], dma=k2)
        mT, _ = cx.pool("mT", 1, [128, KC, TS], BF16)
        for oc in range(KC):
            wa, ka = cx.pool("wB", 4, [128, 8 * 128], BF16)
            P.op("pool", lambda h, wa=wa, oc=oc: h.dma_start(out=wa[:, :], in_=G["wbn"][oc, :, :], max_dma_last_dim=4096), writes=[wa], dma=ka)
            wb, kb = cx.pool("wB", 4, [128, 8 * 128], BF16)
            P.op("pool", lambda h, wb=wb, oc=oc: h.dma_start(out=wb[:, :], in_=G["wbm"][oc, :, :], max_dma_last_dim=4096), writes=[wb], dma=kb)
            gA, kga = cx.pool("gA", 3, [128, TS], F32)
            gB, kgb = cx.pool("gB", 3, [128, TS], F32)
            P.op("sp", lambda h, gA=gA, oc=oc, c0=c0: h.dma_start(out=gA[:, :], in_=mgv[:, oc, c0:c0 + TS]), writes=[gA], dma=kga)
            P.op("sp", lambda h, gB=gB, oc=oc, c0=c0: h.dma_start(out=gB[:, :], in_=mgv[:, KC + oc, c0:c0 + TS]), writes=[gB], dma=kgb)
            pa = [cx.ps() for _ in range(NS)]
            pb = [cx.ps() for _ in range(NS)]
            for c in range(8):
                for s_ in range(NS):
                    mm(P, pa[s_], pa[s_][:, :], wa, wa[:, c * 128:(c + 1) * 128], a_, a_[:, c, s_ * TT:(s_ + 1) * TT], c == 0, c == 7)
            for c in range(8):
                for s_ in range(NS):
                    mm(P, pb[s_], pb[s_][:, :], wb, wb[:, c * 128:(c + 1) * 128], b_, b_[:, c, s_ * TT:(s_ + 1) * TT], c == 0, c == 7)
            for s_ in range(NS):
                sl = slice(s_ * TT, (s_ + 1) * TT)
                P.op("dve", lambda h, gA=gA, pa_=pa[s_], sl=sl: h.tensor_tensor(out=gA[:, sl], in0=gA[:, sl], in1=pa_[:, :], op=ALU.mult), reads=[gA, pa[s_]], writes=[gA])
                P.op("dve", lambda h, gB=gB, pb_=pb[s_], sl=sl: h.tensor_tensor(out=gB[:, sl], in0=gB[:, sl], in1=pb_[:, :], op=ALU.mult), reads=[gB, pb[s_]], writes=[gB])
            P.op("dve", lambda h, gA=gA, gB=gB, mT=mT, oc=oc: h.tensor_tensor(out=mT[:, oc, :], in0=gA[:, :], in1=gB[:, :], op=ALU.add), reads=[gA, gB], writes=[mT])
        for oc in range(KC):
            wo, kw_ = cx.pool("wA", 4, [128, KC * 128], BF16)
            P.op("pool", lambda h, wo=wo, oc=oc: h.dma_start(out=wo[:, :], in_=G["wo"][oc, :, :], max_dma_last_dim=8192), writes=[wo], dma=kw_)
            xt, kx = cx.pool("xt", 4, [128, TS], F32)
            P.op("sp", lambda h, xt=xt, oc=oc, c0=c0: h.dma_start(out=xt[:, :], in_=x1v[:, oc, NOWN + c0:NOWN + c0 + TS]), writes=[xt], dma=kx + "l")
            po = [cx.ps() for _ in range(NS)]
            for c in range(KC):
                for s_ in range(NS):
                    mm(P, po[s_], po[s_][:, :], wo, wo[:, c * 128:(c + 1) * 128], mT, mT[:, c, s_ * TT:(s_ + 1) * TT], c == 0, c == KC - 1)
            for s_ in range(NS):
                sl = slice(s_ * TT, (s_ + 1) * TT)
                P.op("dve", lambda h, xt=xt, po_=po[s_], sl=sl: h.tensor_tensor(out=xt[:, sl], in0=xt[:, sl], in1=po_[:, :], op=ALU.add), reads=[xt, po[s_]], writes=[xt])
            P.op("sp", lambda h, xt=xt, oc=oc, c0=c0: h.dma_start(out=x2v[:, oc, c0:c0 + TS], in_=xt[:, :]), reads=[xt], dma=kx + "s")
    cx.end()


def build(stage="full"):
    nc = bass.Bass("TRN2", target_bir_lowering=False)

    def din(name, shape, dt=F32):
        return nc.dram_tensor(name, list(shape), dt, kind="ExternalInput").ap()

    def scr(name, shape, dt=F32):
        return nc.dram_tensor(name, list(shape), dt, kind="Internal").ap()

    G = {}
    xT = din("xT", [D, SEQV])
    g1 = din("g1", [128, KC]); g2 = din("g2", [128, KC]); gmix = din("gmix", [128, KC])
    wg1 = din("wg1", [FC, 128, D]); wu1 = din("wu1", [FC, 128, D]); wd1 = din("wd1", [KC, 128, DFF])
    wg2 = din("wg2", [FC, 128, D]); wu2 = din("wu2", [FC, 128, D]); wd2 = din("wd2", [KC, 128, DFF])
    G["wi"] = din("wi", [NT_IN, 128, D]); bi = din("bi", [128, NT_IN])
    G["w1"] = din("w1", [2, 64, 32 * 256]); G["w2"] = din("w2", [2, 128, 128]); G["posT"] = din("posT", [2, 64, 32])
    G["wbn"] = din("wbn", [KC, 128, 1024]); G["wbm"] = din("wbm", [KC, 128, 1024]); G["wo"] = din("wo", [KC, 128, D])
    cbf = din("cbf", [128, 3 * 128], BF16)
    cf = din("cf", [128, 3 * 128 + 64 + 2 * 64])
    gk = din("gk", [128, 4]); mgain = din("mgain", [128, 8]); cw = din("cw", [128, 16 * 4]); cb = din("cb", [128, 16])
    pc = din("pc", [128, 3])
    G["c_cpen"] = din("c_cpen", [128, 4, 512], BF16); G["c_wpen"] = din("c_wpen", [128, 8, 512], BF16)
    G["c_cmpp"] = din("c_cmpp", [128, 8, 512], BF16); G["c_Eexp"] = din("c_Eexp", [64, SEQV], BF16)
    G["c_mulm"] = din("c_mulm", [128, 16, 64]); G["c_addm"] = din("c_addm", [128, 16, 64])
    out = nc.dram_tensor("out", [D, NOWN], F32, kind="ExternalOutput").ap()

    G["x1T"] = scr("x1T", [D, SEQV]); G["x2T"] = scr("x2T", [D, NOWN])
    G["kvc"] = scr("kvc", [512, SEQV], BF16); G["ks"] = scr("ks", [256, SEQV], BF16); G["kw"] = scr("kw", [256, SEQV], BF16)
    G["v"] = scr("vv", [SEQV, 2, 256], BF16); G["mqr"] = scr("mqr", [1024, SEQV]); G["mkr"] = scr("mkr", [1024, SEQV])
    G["mv"] = scr("mv", [SEQV, 1024], BF16); G["small"] = scr("small", [SEQV, 128])
    G["qn"] = scr("qn", [1024, NOWN], BF16); G["mo"] = scr("mo", [1024, NOWN]); G["mg"] = scr("mg", [4096, NOWN])
    G["on"] = scr("on", [1024, NOWN], BF16); G["hm"] = scr("hm", [1024, NOWN], BF16); G["hc"] = scr("hc", [1024, NOWN])
    G["qc"] = scr("qc", [1024, NOWN], BF16); G["kc"] = scr("kc", [1024, SEQV], BF16); G["kctm"] = scr("kctm", [SEQV, 1024], BF16)

    cx = Ctx(nc)
    P = cx.P
    cbf_sb = cx.gsb([128, 3 * 128], BF16, "cbf_sb")
    cf_sb = cx.gsb([128, 3 * 128 + 64 + 128], F32, "cf_sb")
    small_sb = {}
    for nm, src, shp in (("g1", g1, [128, KC]), ("g2", g2, [128, KC]), ("gmix", gmix, [128, KC]), ("bi", bi, [128, NT_IN]), ("gk", gk, [128, 4]),
                         ("mgain", mgain, [128, 8]), ("cw", cw, [128, 64]), ("cb", cb, [128, 16]), ("pc", pc, [128, 3])):
        t = cx.gsb(shp, F32, nm + "_sb")
        P.op("sp", lambda h, t=t, src=src: h.dma_start(out=t.ap, in_=src), writes=[t], dma="c0")
        small_sb[nm] = t
    P.op("sp", lambda h: h.dma_start(out=cbf_sb[:, :], in_=cbf[:, :]), writes=[cbf_sb], dma="c0")
    P.op("sp", lambda h: h.dma_start(out=cf_sb[:, :], in_=cf[:, :]), writes=[cf_sb], dma="c0")
    P.barrier()

    class V(T):
        __slots__ = ("parent",)

        def __init__(self, parent, ap):
            self.parent = parent
            self.ap = ap
            self.name = parent.name

        w = property(lambda s: s.parent.w, lambda s, v: setattr(s.parent, "w", v))
        r = property(lambda s: s.parent.r, lambda s, v: setattr(s.parent, "r", v))

    G["ones_bf"] = V(cbf_sb, cbf_sb[:, 0:128]); G["identb"] = V(cbf_sb, cbf_sb[:, 128:256]); G["bd64"] = V(cbf_sb, cbf_sb[:, 256:384])
    G["identf"] = V(cf_sb, cf_sb[:, 0:128]); G["triu"] = V(cf_sb, cf_sb[:, 128:256]); G["ones_f"] = V(cf_sb, cf_sb[:, 256:384])
    G["ov"] = V(cf_sb, cf_sb[:, 448:576].rearrange("p (a b) -> p a b", a=2))
    G["bi_sb"] = small_sb["bi"]; G["gk_sb"] = small_sb["gk"]; G["gmix_sb"] = small_sb["gmix"]; G["mgain_sb"] = small_sb["mgain"]
    G["cw_sb"] = V(small_sb["cw"], small_sb["cw"][:, :].rearrange("p (a b) -> p a b", b=4)); G["cb_sb"] = small_sb["cb"]
    G["vcol"] = V(small_sb["pc"], small_sb["pc"][:, 0:1]); G["validc"] = V(small_sb["pc"], small_sb["pc"][:, 1:3])
    G["KcT"] = cx.gsb([64, 4, 256], BF16, "KcT"); G["Vc"] = cx.gsb([128, 4, 2, 129], BF16, "Vc")

    all_tiles = [i * TS for i in range(SEQV // TS)]
    ffn_stage(cx, xT, G["x1T"], all_tiles, small_sb["g1"], wg1, wu1, wd1, G["ones_bf"], "f1")
    inproj_stage(cx, G)
    cmp_stage(cx, G)
    nsa_stage(cx, G)
    mlstm_stage(cx, G)
    merge_stage(cx, G)
    ffn_stage(cx, G["x2T"], out, [i * TS for i in range(NOWN // TS)], small_sb["g2"], wg2, wu2, wd2, G["ones_bf"], "f2")
    P.emit()
    return nc


def tile_w(w, kc):
    K, N = w.shape
    return np.ascontiguousarray(w.reshape(kc, 128, N // 128, 128).transpose(2, 1, 0, 3).reshape(N // 128, 128, kc * 128))


def colvec(v, n):
    return np.ascontiguousarray(np.asarray(v, np.float32).reshape(n, 128).T)


_cache = {}


def static_consts():
    bf = ml_dtypes.bfloat16
    p = np.arange(128)
    col = np.arange(512)
    ones = np.ones((128, 128), np.float32)
    ident = np.eye(128, dtype=np.float32)
    bd = (p[:, None] // 64 == p[None, :] // 64).astype(np.float32)
    triu = (p[:, None] <= p[None, :]).astype(np.float32)
    c = {}
    c["cbf"] = np.concatenate([ones, ident, bd], axis=1).astype(bf)
    ov = np.zeros((128, 2, 64), np.float32)
    for ct in range(2):
        cc = ct * 128 + p
        n = np.arange(64)
        ov[:, ct, :] = ((16 * cc[:, None] < 64 * n[None, :] + 64) & (16 * cc[:, None] + 32 > 64 * n[None, :])).astype(np.float32)
    c["cf"] = np.concatenate([ident, triu, ones, np.zeros((128, 64), np.float32), ov.reshape(128, 128)], axis=1).astype(np.float32)
    cpen = np.zeros((128, 4, 512), np.float32)
    for rel in range(4):
        cpen[:, rel, :] = np.where(128 * rel + p[:, None] > col[None, :], 0.0, 1.0)
    c["c_cpen"] = cpen.astype(bf)
    wpen = np.zeros((128, 8, 512), np.float32)
    for rel in range(8):
        kp = -512 + 128 * rel + p[:, None]
        ok = (kp <= col[None, :]) & (kp > col[None, :] - 512)
        wpen[:, rel, :] = np.where(ok, 1.0, 0.0)
    c["c_wpen"] = wpen.astype(bf)
    cmpp = np.zeros((128, 8, 512), np.float32)
    for qt in range(4):
        for ct in range(2):
            cc = ct * 128 + p[:, None]
            qv = NOWN + qt * 512 + col[None, :]
            cmpp[:, qt * 2 + ct, :] = np.where(16 * cc + 31 <= qv, 0.0, NEGB)
    c["c_cmpp"] = cmpp.astype(bf)
    kk = np.arange(SEQV)
    c["c_Eexp"] = (kk[None, :] // 64 == np.arange(64)[:, None]).astype(np.float32).astype(bf)
    return c


def percore_consts(hh):
    v = float(hh)
    pc = np.zeros((128, 3), np.float32)
    pc[:, 0] = v
    pc[:, 1] = v
    pc[:, 2] = 1.0
    mulm = np.zeros((128, 16, 64), np.float32)
    addm = np.zeros((128, 16, 64), np.float32)
    p = np.arange(128)
    n = np.arange(64)
    for qs in range(16):
        tv = NOWN + qs * 128 + p
        if hh == 1:
            tr = tv
            nr = n
        else:
            tr = tv - NOWN
            nr = n - 32
        cur = tr // 64
        real = nr[None, :] >= 0
        forced = real & ((nr[None, :] == 0) | (nr[None, :] == cur[:, None]) | (nr[None, :] == cur[:, None] - 1))
        causal = real & (nr[None, :] * 64 <= tr[:, None])
        mulm[:, qs, :] = (causal & ~forced).astype(np.float32)
        addm[:, qs, :] = np.where(forced, 1e4, np.where(causal, 0.0, -1.0))
    return {"pc": pc, "c_mulm": mulm, "c_addm": addm}


def prep_inputs(inp):
    f = lambda k: np.asarray(inp[k], np.float32)[0]
    x = np.asarray(inp["x"], np.float32)
    w_in = f("w_in")
    b_in = f("b_in")
    KVB = 1024
    cols = []
    for kvidx in (0, 1, 2, 4, 3, 5):
        cols.append(np.arange(KVB + kvidx * 256, KVB + (kvidx + 1) * 256))
    MQ = 2608
    cols.append(np.arange(MQ, MQ + 1024)); cols.append(np.arange(MQ + 1024, MQ + 2048)); cols.append(np.arange(MQ + 2048, MQ + 3072))
    small_cols = np.concatenate([np.arange(5680, 5684), np.arange(5684, 5688), np.arange(2560, 2608)])
    cols_all = np.concatenate(cols)
    cols_own = np.concatenate([np.arange(0, 1024), np.arange(5688, 6712), np.arange(6712, 10808)])
    w_small = np.zeros((D, 128), np.float32); w_small[:, :56] = w_in[:, small_cols]
    b_small = np.zeros((128,), np.float32); b_small[:56] = b_in[small_cols]
    w_perm = np.concatenate([w_in[:, cols_all], w_small, w_in[:, cols_own]], axis=1)
    b_perm = np.concatenate([b_in[cols_all], b_small, b_in[cols_own]])
    assert w_perm.shape[1] == NT_IN * 128
    gk = np.stack([np.tile(f("nsa_ks_gain"), 2), np.tile(f("nsa_kw_gain"), 2), np.tile(f("nsa_q_gain"), 2), np.tile(f("nsa_kc_gain"), 2)], axis=1)
    w1 = np.stack([f("cmp_w1_k").reshape(32, 64, 256).transpose(1, 0, 2).reshape(64, 32 * 256),
                   f("cmp_w1_v").reshape(32, 64, 256).transpose(1, 0, 2).reshape(64, 32 * 256)])
    w2 = np.stack([f("cmp_w2_k").reshape(2, 128, 64).transpose(1, 0, 2).reshape(128, 128),
                   f("cmp_w2_v").reshape(2, 128, 64).transpose(1, 0, 2).reshape(128, 128)])
    posT = np.stack([f("cmp_pos_k").T, f("cmp_pos_v").T])
    cwv = f("m_conv_w")
    cw = np.ascontiguousarray(cwv.reshape(4, 16, 128).transpose(2, 1, 0).reshape(128, 64))
    common = {
        "g1": colvec(f("ffn1_norm"), KC), "g2": colvec(f("ffn2_norm"), KC), "gmix": colvec(f("mix_norm"), KC),
        "wg1": tile_w(f("ffn1_w_gate"), KC), "wu1": tile_w(f("ffn1_w_up"), KC), "wd1": tile_w(f("ffn1_w_down"), FC),
        "wg2": tile_w(f("ffn2_w_gate"), KC), "wu2": tile_w(f("ffn2_w_up"), KC), "wd2": tile_w(f("ffn2_w_down"), FC),
        "wi": tile_w(w_perm, KC), "bi": colvec(b_perm, NT_IN),
        "w1": np.ascontiguousarray(w1), "w2": np.ascontiguousarray(w2), "posT": np.ascontiguousarray(posT),
        "wbn": tile_w(f("w_branch_nsa"), 8), "wbm": tile_w(f("w_branch_mlstm"), 8), "wo": tile_w(f("w_out"), KC),
        "gk": np.ascontiguousarray(gk.astype(np.float32)), "mgain": colvec(f("m_out_gain").reshape(-1), 8),
        "cw": cw, "cb": colvec(f("m_conv_b"), 16),
    }
    common.update(static_consts())
    pcs = [percore_consts(0), percore_consts(1)]
    maps = []
    for c in range(8):
        b, hh = c // 2, c % 2
        m = dict(common)
        m.update(pcs[hh])
        m["xT"] = np.ascontiguousarray(np.concatenate([x[b, 0:NOWN].T, x[b, NOWN * hh:NOWN * hh + NOWN].T], axis=1))
        maps.append(m)
    return maps


def kernel(**inp):
    if "nc" not in _cache:
        _cache["nc"] = build()
    nc = _cache["nc"]
    maps = prep_inputs(inp)
    res = run_bass_kernel_spmd(nc, maps, core_ids=list(range(8)))
    outp = np.empty((4, 4096, D), np.float32)
    for c in range(8):
        b, hh = c // 2, c % 2
        outp[b, NOWN * hh:NOWN * hh + NOWN, :] = res.results[c]["out"].T
    return outp
```

```python
import numpy as np
import ml_dtypes
import concourse.bass as bass
import concourse.mybir as mybir
from concourse.bass_utils import run_bass_kernel_spmd
from contextlib import ExitStack

F32 = mybir.dt.float32
BF16 = mybir.dt.bfloat16
AF = mybir.ActivationFunctionType
ALU = mybir.AluOpType
AX = mybir.AxisListType

D = 2048
DFF = 5632
KC = D // 128
FC = DFF // 128
TT = 512
SEQV = 4096
NOWN = 2048
EPS = 1e-6


class T:
    __slots__ = ("ap", "name", "w", "r")

    def __init__(self, ap, name=""):
        self.ap = ap
        self.name = name
        self.w = None
        self.r = []

    def __getitem__(self, idx):
        return self.ap[idx]


class Prog:
    ENGS = ("pe", "act", "dve", "pool", "sp")

    def __init__(self, nc):
        self.nc = nc
        self.ops = []
        self.last = {}
        self.dmas = []
        self.pending = {e: set() for e in self.ENGS}

    def barrier(self):
        deps = set(self.last.values()) | set(self.dmas)
        self.dmas = []
        for e in self.ENGS:
            self.pending[e] |= deps

    def op(self, eng, fn, reads=(), writes=(), dma=None):
        i = len(self.ops)
        deps = set(self.pending[eng])
        self.pending[eng] = set()
        self.last[eng] = i
        if dma is not None:
            self.dmas.append(i)
        for t in reads:
            if t.w is not None:
                deps.add(t.w)
        for t in writes:
            if t.w is not None:
                deps.add(t.w)
            deps.update(t.r)
        for t in reads:
            t.r.append(i)
        for t in writes:
            t.w = i
            t.r = []
        d2 = set()
        for d in deps:
            o = self.ops[d]
            if o["dma"] is None and o["eng"] == eng and eng == "pe":
                continue
            d2.add(d)
            o["needed"] = True
        self.ops.append(dict(eng=eng, fn=fn, deps=d2, dma=dma, needed=False, ev=None))
        return i

    def emit(self):
        nc = self.nc
        cnt = {}
        for o in self.ops:
            if o["dma"] is not None:
                k = "dma_" + o["dma"]
                cnt[k] = cnt.get(k, 0) + 16
                o["ev"] = (k, cnt[k])
            elif o["needed"]:
                k = "eng_" + o["eng"]
                cnt[k] = cnt.get(k, 0) + 1
                o["ev"] = (k, cnt[k])
        keys = sorted(cnt.keys())
        self.maxvals = dict(cnt)
        sems = {k: nc.alloc_semaphore(name=k) for k in keys}
        per_eng = {e: [] for e in self.ENGS}
        for o in self.ops:
            per_eng[o["eng"]].append(o)
        handles = {"pe": "tensor", "act": "scalar", "dve": "vector", "pool": "gpsimd", "sp": "sync"}
        ops = self.ops
        with nc.Block() as block:
            for e in self.ENGS:
                lst = per_eng[e]

                def body(h, lst=lst, e=e):
                    seen = {}
                    for o in lst:
                        need = {}
                        for d in o["deps"]:
                            k, v = ops[d]["ev"]
                            if seen.get(k, 0) >= v:
                                continue
                            if need.get(k, 0) < v:
                                need[k] = v
                        for k, v in need.items():
                            h.wait_ge(sems[k], v)
                            seen[k] = v
                        ins = o["fn"](h)
                        if o["ev"] is not None:
                            ins.then_inc(sems[o["ev"][0]], 16 if o["dma"] is not None else 1)
                    if e == "sp":
                        for k in keys:
                            if k.startswith("dma_"):
                                h.wait_ge(sems[k], cnt[k])
                getattr(block, handles[e])(body)
        return sems


class Ctx:
    def __init__(self, nc):
        self.nc = nc
        self.P = Prog(nc)
        self.n = 0
        self.psum = [T(nc.alloc_psum_tensor("ps%d" % i, [128, 512], F32).ap(), "ps%d" % i) for i in range(8)]
        self.psi = 0
        self.ps_mod = 8
        self.pools = {}
        self.stack = None

    def begin(self, ps_mod=8):
        self.stack = ExitStack()
        self.pools = {}
        self.ps_mod = ps_mod

    def end(self):
        self.P.barrier()
        self.stack.close()
        self.stack = None
        self.pools = {}

    def gsb(self, shape, dt, name):
        return T(self.nc.alloc_sbuf_tensor(name, list(shape), dt).ap(), name)

    def sb(self, shape, dt, name=None):
        self.n += 1
        name = "%s_%d" % (name or "t", self.n)
        h = self.stack.enter_context(self.nc.sbuf_tensor(name, list(shape), dt))
        return T(h.ap(), name)

    def ps(self):
        t = self.psum[self.psi % self.ps_mod]
        self.psi += 1
        return t

    def pool(self, key, n, shape, dt):
        if key not in self.pools:
            self.pools[key] = [[self.sb(shape, dt, "%s%d" % (key, i)) for i in range(n)], 0]
        p = self.pools[key]
        t = p[0][p[1] % n]
        idx = p[1] % n
        p[1] += 1
        return t, "%s%d" % (key, idx)


NS = 2
TS = NS * TT


class NormJob:
    def __init__(self, cx, src_dram, t0, gain_sb, ones_bf, xn, ss=None):
        self.cx, self.t0, self.gain_sb, self.ones_bf, self.xn = cx, t0, gain_sb, ones_bf, xn
        self.src_v = src_dram.rearrange("(c p) t -> p c t", p=128)
        self.ss = ss if ss is not None else [cx.psum[6], cx.psum[7]]

    def chunk(self, c):
        cx, P, t0, src_v, ss, ones_bf = self.cx, self.cx.P, self.t0, self.src_v, self.ss, self.ones_bf
        xc, kx = cx.pool("xc", 3, [128, TS], F32)
        P.op("sp", lambda h, c=c, xc=xc: h.dma_start(out=xc[:, :], in_=src_v[:, c, t0:t0 + TS]), writes=[xc], dma=kx)
        sq, _ = cx.pool("sq", 2, [128, TS], BF16)
        P.op("act", lambda h, xc=xc, sq=sq: h.activation(out=sq[:, :], in_=xc[:, :], func=AF.Square), reads=[xc], writes=[sq])
        for s_ in range(NS):
            mm(P, ss[s_], ss[s_][:, :], ones_bf, ones_bf[:, :], sq, sq[:, s_ * TT:(s_ + 1) * TT], c == 0, c == KC - 1)

    def finish(self):
        cx, P, t0, src_v, ss, xn, gain_sb = self.cx, self.cx.P, self.t0, self.src_v, self.ss, self.xn, self.gain_sb
        rstd, _ = cx.pool("rstdL", 2, [128, TS], F32)
        for s_ in range(NS):
            P.op("dve", lambda h, s_=s_: h.tensor_scalar(out=rstd[:, s_ * TT:(s_ + 1) * TT], in0=ss[s_][:, :], scalar1=1.0 / D, scalar2=EPS, op0=ALU.mult, op1=ALU.add),
                 reads=[ss[s_]], writes=[rstd])
        P.op("act", lambda h: h.activation(out=rstd[:, :], in_=rstd[:, :], func=AF.Sqrt), reads=[rstd], writes=[rstd])
        P.op("dve", lambda h: h.reciprocal(out=rstd[:, :], in_=rstd[:, :]), reads=[rstd], writes=[rstd])
        for c in range(KC):
            xc, kx = cx.pool("xc", 3, [128, TS], F32)
            P.op("sp", lambda h, c=c, xc=xc: h.dma_start(out=xc[:, :], in_=src_v[:, c, t0:t0 + TS]), writes=[xc], dma=kx)
            P.op("dve", lambda h, c=c, xc=xc: h.scalar_tensor_tensor(out=xn[:, c, :], in0=xc[:, :], scalar=gain_sb[:, c:c + 1], in1=rstd[:, :], op0=ALU.mult, op1=ALU.mult),
                 reads=[xc, rstd, gain_sb], writes=[xn])


def ffn_stage(cx, src_dram, dst_dram, tiles, gain_sb, wg, wu, wd, ones_bf, tag, dst_off=0):
    P = cx.P
    cx.begin(6)
    xn = cx.sb([128, KC, TS], BF16, "xn")
    hbuf = cx.sb([128, FC, TS], BF16, "hbuf")
    src_v = src_dram.rearrange("(c p) t -> p c t", p=128)
    dst_v = dst_dram.rearrange("(c p) t -> p c t", p=128)
    nj = NormJob(cx, src_dram, tiles[0], gain_sb, ones_bf, xn)
    for c in range(KC):
        nj.chunk(c)
    nj.finish()
    for ti_, t0 in enumerate(tiles):
        nxt = NormJob(cx, src_dram, tiles[ti_ + 1], gain_sb, ones_bf, xn) if ti_ + 1 < len(tiles) else None
        for j in range(FC):
            if nxt is not None and 20 <= j < 20 + KC:
                nxt.chunk(j - 20)
            wgs, kg = cx.pool("wA", 4, [128, KC * 128], BF16)
            P.op("pool", lambda h, j=j, wgs=wgs: h.dma_start(out=wgs[:, :], in_=wg[j, :, :], max_dma_last_dim=8192), writes=[wgs], dma=kg)
            wus, ku = cx.pool("wA", 4, [128, KC * 128], BF16)
            P.op("pool", lambda h, j=j, wus=wus: h.dma_start(out=wus[:, :], in_=wu[j, :, :], max_dma_last_dim=8192), writes=[wus], dma=ku)
            pg = [cx.ps() for _ in range(NS)]
            pu = [cx.ps() for _ in range(NS)]
            for c in range(KC):
                for s_ in range(NS):
                    mm(P, pg[s_], pg[s_][:, :], wgs, wgs[:, c * 128:(c + 1) * 128], xn, xn[:, c, s_ * TT:(s_ + 1) * TT], c == 0, c == KC - 1)
            for c in range(KC):
                for s_ in range(NS):
                    mm(P, pu[s_], pu[s_][:, :], wus, wus[:, c * 128:(c + 1) * 128], xn, xn[:, c, s_ * TT:(s_ + 1) * TT], c == 0, c == KC - 1)
            for s_ in range(NS):
                sg, _ = cx.pool("sg", 3, [128, TT], BF16)
                P.op("act", lambda h, pg_=pg[s_], sg=sg: h.activation(out=sg[:, :], in_=pg_[:, :], func=AF.Silu), reads=[pg[s_]], writes=[sg])
                P.op("dve", lambda h, j=j, sg=sg, pu_=pu[s_], s_=s_: h.tensor_tensor(out=hbuf[:, j, s_ * TT:(s_ + 1) * TT], in0=sg[:, :], in1=pu_[:, :], op=ALU.mult),
                     reads=[sg, pu[s_]], writes=[hbuf])
        if nxt is not None:
            nxt.finish()
        for m in range(KC):
            wds, kd = cx.pool("wD", 2, [128, FC * 128], BF16)
            for q in range(4):
                f0 = q * (FC // 4) * 128
                f1 = (q + 1) * (FC // 4) * 128
                P.op("pool", lambda h, m=m, wds=wds, f0=f0, f1=f1: h.dma_start(out=wds[:, f0:f1], in_=wd[m, :, f0:f1], max_dma_last_dim=5632), writes=[wds], dma=kd)
            po = [cx.ps() for _ in range(NS)]
            for j in range(FC):
                for s_ in range(NS):
                    mm(P, po[s_], po[s_][:, :], wds, wds[:, j * 128:(j + 1) * 128], hbuf, hbuf[:, j, s_ * TT:(s_ + 1) * TT], j == 0, j == FC - 1)
            xc, kx = cx.pool("xc", 3, [128, TS], F32)
            P.op("sp", lambda h, m=m, xc=xc, t0=t0: h.dma_start(out=xc[:, :], in_=src_v[:, m, t0:t0 + TS]), writes=[xc], dma=kx)
            ot, ko = cx.pool("ot", 2, [128, TS], F32)
            for s_ in range(NS):
                P.op("dve", lambda h, po_=po[s_], ot=ot, xc=xc, s_=s_: h.scalar_tensor_tensor(out=ot[:, s_ * TT:(s_ + 1) * TT], in0=po_[:, :], scalar=0.5, in1=xc[:, s_ * TT:(s_ + 1) * TT], op0=ALU.mult, op1=ALU.add),
                     reads=[po[s_], xc], writes=[ot])
            P.op("sp", lambda h, m=m, ot=ot, t0=t0: h.dma_start(out=dst_v[:, m, t0 + dst_off:t0 + dst_off + TS], in_=ot[:, :]), reads=[ot], dma=ko)
    cx.end()


def mm(P, out_t, out_ap, lhsT_t, lhsT_ap, rhs_t, rhs_ap, start, stop):
    P.op("pe", lambda h: h.matmul(out_ap, lhsT_ap, rhs_ap, start=start, stop=stop), reads=[lhsT_t, rhs_t], writes=[out_t])


NEGB = -30000.0
TILES_ALL = (["kcmp"] * 2 + ["vcmp"] * 2 + ["kslc"] * 2 + ["kwin"] * 2 + ["vslc"] * 2 + ["vwin"] * 2
             + ["mq"] * 8 + ["mk"] * 8 + ["mv"] * 8 + ["small"])
TILES_OWN = ["q"] * 8 + ["mo"] * 8 + ["merge"] * 32
NT_ALL = len(TILES_ALL)
NT_IN = NT_ALL + len(TILES_OWN)


def inproj_stage(cx, G):
    P = cx.P
    cx.begin(8)
    xn = cx.sb([128, KC, TS], BF16, "xn")
    wi, bi = G["wi"], G["bi_sb"]
    ones_bf, bd64, identf, vcol = G["ones_bf"], G["bd64"], G["identf"], G["vcol"]
    gk = G["gk_sb"]
    first_idx = {}
    for j, k in enumerate(TILES_ALL + TILES_OWN):
        first_idx.setdefault(k, j)
    for ti in range(SEQV // TS):
        t0b = ti * TS
        own = t0b >= NOWN
        nj = NormJob(cx, G["x1T"], t0b, G["gmix_sb"], ones_bf, xn, ss=[cx.ps(), cx.ps()])
        for c in range(KC):
            nj.chunk(c)
        nj.finish()
        nxt = None
        plan = TILES_ALL + (TILES_OWN if own else [])
        deferred = []
        for j, kind in enumerate(plan):
            if nxt is not None and 8 <= j < 8 + KC:
                nxt.chunk(j - 8)
            if nxt is not None and j == len(plan) - 1:
                pass
            k = j - first_idx[kind]
            ws, kw_ = cx.pool("wA", 6, [128, KC * 128], BF16)
            P.op("pool", lambda h, j=j, ws=ws: h.dma_start(out=ws[:, :], in_=wi[j, :, :], max_dma_last_dim=8192), writes=[ws], dma=kw_)
            pss = [cx.ps() for _ in range(NS)]
            for c in range(KC):
                for s_ in range(NS):
                    mm(P, pss[s_], pss[s_][:, :], ws, ws[:, c * 128:(c + 1) * 128], xn, xn[:, c, s_ * TT:(s_ + 1) * TT], c == 0, c == KC - 1)
            while deferred:
                deferred.pop(0)()
            for s_ in range(NS):
                ps = pss[s_]
                t0 = t0b + s_ * TT
                bcol = bi[:, j:j + 1]
                if kind in ("kcmp", "vcmp"):
                    ob, ko = cx.pool("obb", 3, [128, TT], BF16)
                    P.op("act", lambda h, ps=ps, ob=ob, bcol=bcol: h.activation(out=ob[:, :], in_=ps[:, :], func=AF.Identity, bias=bcol), reads=[ps, bi], writes=[ob])
                    r0 = (0 if kind == "kcmp" else 256) + k * 128
                    P.op("sp", lambda h, ob=ob, r0=r0, t0=t0: h.dma_start(out=G["kvc"][r0:r0 + 128, t0:t0 + TT], in_=ob[:, :]), reads=[ob], dma=ko)
                elif kind in ("mq", "mk"):
                    ob, ko = cx.pool("obf", 3, [128, TT], F32)
                    P.op("act", lambda h, ps=ps, ob=ob, bcol=bcol: h.activation(out=ob[:, :], in_=ps[:, :], func=AF.Identity, bias=bcol), reads=[ps, bi], writes=[ob])
                    dst = G["mqr"] if kind == "mq" else G["mkr"]
                    P.op("sp", lambda h, ob=ob, dst=dst, k=k, t0=t0: h.dma_start(out=dst[k * 128:(k + 1) * 128, t0:t0 + TT], in_=ob[:, :]), reads=[ob], dma=ko)
                elif kind in ("mo", "merge"):
                    ob, ko = cx.pool("obf", 3, [128, TT], F32)
                    P.op("act", lambda h, ps=ps, ob=ob, bcol=bcol: h.activation(out=ob[:, :], in_=ps[:, :], func=AF.Sigmoid, bias=bcol), reads=[ps, bi], writes=[ob])
                    dst = G["mo"] if kind == "mo" else G["mg"]
                    P.op("sp", lambda h, ob=ob, dst=dst, k=k, t0=t0: h.dma_start(out=dst[k * 128:(k + 1) * 128, t0 - NOWN:t0 - NOWN + TT], in_=ob[:, :]), reads=[ob], dma=ko)
                elif kind in ("kslc", "kwin", "q"):
                    z, _ = cx.pool("z", 5, [128, TT], F32)
                    P.op("act", lambda h, ps=ps, z=z, bcol=bcol: h.activation(out=z[:, :], in_=ps[:, :], func=AF.Identity, bias=bcol), reads=[ps, bi], writes=[z])
                    sq, _ = cx.pool("sqe", 5, [128, TT], BF16)
                    P.op("act", lambda h, z=z, sq=sq: h.activation(out=sq[:, :], in_=z[:, :], func=AF.Square), reads=[z], writes=[sq])
                    def part_b(ps=ps, z=z, t0=t0, k=k, kind=kind, j=j, own=own, sq=sq):
                        ps2 = cx.ps()
                        mm(P, ps2, ps2[:, :], bd64, bd64[:, :], sq, sq[:, :], True, True)
                        rs, _ = cx.pool("rstd", 2, [128, TT], F32)
                        P.op("dve", lambda h, ps2=ps2, rs=rs: h.tensor_scalar(out=rs[:, :], in0=ps2[:, :], scalar1=1.0 / 64, scalar2=EPS, op0=ALU.mult, op1=ALU.add), reads=[ps2], writes=[rs])
                        sc = 64.0 if kind == "q" else 1.0
                        P.op("act", lambda h, rs=rs, sc=sc: h.activation(out=rs[:, :], in_=rs[:, :], func=AF.Sqrt, scale=sc), reads=[rs], writes=[rs])
                        P.op("dve", lambda h, rs=rs: h.reciprocal(out=rs[:, :], in_=rs[:, :]), reads=[rs], writes=[rs])
                        gi = {"kslc": 0, "kwin": 1, "q": 2}[kind]
                        ob, ko = cx.pool("obb", 3, [128, TT], BF16)
                        P.op("dve", lambda h, z=z, rs=rs, ob=ob, gi=gi: h.scalar_tensor_tensor(out=ob[:, :], in0=z[:, :], scalar=gk[:, gi:gi + 1], in1=rs[:, :], op0=ALU.mult, op1=ALU.mult),
                             reads=[z, rs, gk], writes=[ob])
                        if kind == "q":
                            P.op("sp", lambda h, ob=ob, k=k, t0=t0: h.dma_start(out=G["qn"][k * 128:(k + 1) * 128, t0 - NOWN:t0 - NOWN + TT], in_=ob[:, :]), reads=[ob], dma=ko)
                        else:
                            dst = G["ks"] if kind == "kslc" else G["kw"]
                            P.op("sp", lambda h, ob=ob, dst=dst, k=k, t0=t0: h.dma_start(out=dst[k * 128:(k + 1) * 128, t0:t0 + TT], in_=ob[:, :]), reads=[ob], dma=ko)
                    deferred.append(part_b)
                else:
                    z, _ = cx.pool("z", 5, [128, TT], F32)
                    P.op("act", lambda h, ps=ps, z=z, bcol=bcol: h.activation(out=z[:, :], in_=ps[:, :], func=AF.Identity, bias=bcol), reads=[ps, bi], writes=[z])
                    def part_b(ps=ps, z=z, t0=t0, k=k, kind=kind, j=j, own=own):
                        pst = cx.ps()
                        for s in range(4):
                            P.op("pe", lambda h, pst=pst, z=z, s=s: h.transpose(out=pst[:, s * 128:(s + 1) * 128], in_=z[:, s * 128:(s + 1) * 128], identity=identf[:, :]),
                                 reads=[z, identf], writes=[pst])
                        if kind == "small":
                            ob, ko = cx.pool("otf", 2, [128, TT], F32)
                            P.op("dve", lambda h, pst=pst, ob=ob: h.tensor_copy(out=ob[:, :], in_=pst[:, :]), reads=[pst], writes=[ob])
                            dstv = G["small"].rearrange("(s p) f -> p s f", p=128)
                            P.op("sp", lambda h, ob=ob, dstv=dstv, t0=t0: h.dma_start(out=dstv[:, t0 // 128:t0 // 128 + 4, :], in_=ob[:, :].rearrange("p (s f) -> p s f", s=4)), reads=[ob], dma=ko)
                        else:
                            ob, ko = cx.pool("otb", 2, [128, TT], BF16)
                            if own or kind == "mv":
                                P.op("dve", lambda h, pst=pst, ob=ob: h.tensor_copy(out=ob[:, :], in_=pst[:, :]), reads=[pst], writes=[ob])
                            else:
                                P.op("dve", lambda h, pst=pst, ob=ob: h.tensor_scalar(out=ob[:, :], in0=pst[:, :], scalar1=vcol[:, 0:1], scalar2=None, op0=ALU.mult), reads=[pst, vcol], writes=[ob])
                            if kind == "mv":
                                dstv = G["mv"].rearrange("(s p) f -> p s f", p=128)[:, t0 // 128:t0 // 128 + 4, k * 128:(k + 1) * 128]
                            else:
                                br = 0 if kind == "vslc" else 1
                                dstv = G["v"].rearrange("(s p) b f -> p s b f", p=128)[:, t0 // 128:t0 // 128 + 4, br, k * 128:(k + 1) * 128]
                            P.op("sp", lambda h, ob=ob, dstv=dstv: h.dma_start(out=dstv, in_=ob[:, :].rearrange("p (s f) -> p s f", s=4)), reads=[ob], dma=ko)
                    deferred.append(part_b)
        while deferred:
            deferred.pop(0)()
    cx.end()


def cmp_stage(cx, G):
    P = cx.P
    cx.begin(8)
    ones_bf, validc = G["ones_bf"], G["validc"]
    KcT, Vc = G["KcT"], G["Vc"]
    for kv in range(2):
        w1 = cx.sb([64, 32 * 256], BF16, "w1")
        P.op("pool", lambda h, kv=kv, w1=w1: h.dma_start(out=w1[:, :], in_=G["w1"][kv, :, :], max_dma_last_dim=8192), writes=[w1], dma="w1")
        w2 = cx.sb([128, 2 * 64], BF16, "w2")
        P.op("pool", lambda h, kv=kv, w2=w2: h.dma_start(out=w2[:, :], in_=G["w2"][kv, :, :]), writes=[w2], dma="w2")
        posT = cx.sb([64, 32], BF16, "posT")
        P.op("pool", lambda h, kv=kv, posT=posT: h.dma_start(out=posT[:, :], in_=G["posT"][kv, :, :]), writes=[posT], dma="posT")
        b1 = cx.sb([128, 2], F32, "b1")
        for hc in range(2):
            ps = cx.ps()
            for j in range(32):
                mm(P, ps, ps[:, 0:1], w1, w1[:, j * 256 + hc * 128:j * 256 + hc * 128 + 128], posT, posT[:, j:j + 1], j == 0, j == 31)
            P.op("dve", lambda h, ps=ps, hc=hc, b1=b1: h.tensor_copy(out=b1[:, hc:hc + 1], in_=ps[:, 0:1]), reads=[ps], writes=[b1])
        for g in range(4):
            kvT, kk = cx.pool("kvT", 2, [64, SEQV + 32], BF16)
            P.op("dve", lambda h, kvT=kvT: h.memset(kvT[:, SEQV:SEQV + 32], 0.0), writes=[kvT])
            P.op("sp", lambda h, kvT=kvT, kv=kv, g=g: h.dma_start(out=kvT[:, 0:SEQV], in_=G["kvc"][kv * 256 + g * 64:kv * 256 + g * 64 + 64, :]), writes=[kvT], dma=kk)
            gT, _ = cx.pool("gT", 2, [128, 2, 256], BF16)
            for hc in range(2):
                ps = cx.ps()
                for j in range(32):
                    mm(P, ps, ps[:, 0:256], w1, w1[:, j * 256 + hc * 128:j * 256 + hc * 128 + 128], kvT, kvT[:, j:j + SEQV:16], j == 0, j == 31)
                z, _ = cx.pool("cz", 2, [128, 256], F32)
                t, _ = cx.pool("ct", 2, [128, 256], F32)
                P.op("act", lambda h, ps=ps, z=z, hc=hc, b1=b1: h.activation(out=z[:, :], in_=ps[:, 0:256], func=AF.Identity, bias=b1[:, hc:hc + 1]), reads=[ps, b1], writes=[z])
                P.op("act", lambda h, z=z, t=t: h.activation(out=t[:, :], in_=z[:, :], func=AF.Square), reads=[z], writes=[t])
                P.op("dve", lambda h, t=t: h.tensor_scalar(out=t[:, :], in0=t[:, :], scalar1=0.044715, scalar2=1.0, op0=ALU.mult, op1=ALU.add), reads=[t], writes=[t])
                P.op("dve", lambda h, t=t, z=z: h.tensor_tensor(out=t[:, :], in0=t[:, :], in1=z[:, :], op=ALU.mult), reads=[t, z], writes=[t])
                P.op("act", lambda h, t=t: h.activation(out=t[:, :], in_=t[:, :], func=AF.Sigmoid, scale=1.5957691216057308), reads=[t], writes=[t])
                P.op("dve", lambda h, t=t, z=z, gT=gT, hc=hc: h.tensor_tensor(out=gT[:, hc, :], in0=t[:, :], in1=z[:, :], op=ALU.mult), reads=[t, z], writes=[gT])
            if kv == 0:
                ps = cx.ps()
                for hc in range(2):
                    mm(P, ps, ps[0:64, 0:256], w2, w2[:, hc * 64:(hc + 1) * 64], gT, gT[:, hc, :], hc == 0, hc == 1)
                zk, _ = cx.pool("zk", 2, [64, 256], F32)
                sq, _ = cx.pool("sqk", 2, [64, 256], BF16)
                P.op("act", lambda h, ps=ps, zk=zk: h.activation(out=zk[:, :], in_=ps[0:64, 0:256], func=AF.Copy), reads=[ps], writes=[zk])
                P.op("act", lambda h, zk=zk, sq=sq: h.activation(out=sq[:, :], in_=zk[:, :], func=AF.Square), reads=[zk], writes=[sq])
                ps2 = cx.ps()
                mm(P, ps2, ps2[0:64, 0:256], ones_bf, ones_bf[0:64, 0:64], sq, sq[:, :], True, True)
                rs, _ = cx.pool("rsk", 2, [64, 256], F32)
                P.op("dve", lambda h, ps2=ps2, rs=rs: h.tensor_scalar(out=rs[:, :], in0=ps2[0:64, 0:256], scalar1=1.0 / 64, scalar2=EPS, op0=ALU.mult, op1=ALU.add), reads=[ps2], writes=[rs])
                P.op("act", lambda h, rs=rs: h.activation(out=rs[:, :], in_=rs[:, :], func=AF.Sqrt), reads=[rs], writes=[rs])
                P.op("dve", lambda h, rs=rs: h.reciprocal(out=rs[:, :], in_=rs[:, :]), reads=[rs], writes=[rs])
                P.op("dve", lambda h, zk=zk, rs=rs, g=g: h.scalar_tensor_tensor(out=KcT[:, g, :], in0=zk[:, :], scalar=G["gk_sb"][0:64, 3:4], in1=rs[:, :], op0=ALU.mult, op1=ALU.mult),
                     reads=[zk, rs, G["gk_sb"]], writes=[KcT])
            else:
                for ct in range(2):
                    ps = cx.ps()
                    for hc in range(2):
                        mm(P, ps, ps[:, 0:64], gT, gT[:, hc, ct * 128:(ct + 1) * 128], w2, w2[:, hc * 64:(hc + 1) * 64], hc == 0, hc == 1)
                    P.op("dve", lambda h, ps=ps, g=g, ct=ct: h.tensor_scalar(out=Vc[:, g, ct, 0:64], in0=ps[:, 0:64], scalar1=validc[:, ct:ct + 1], scalar2=None, op0=ALU.mult),
                         reads=[ps, validc], writes=[Vc])
                    P.op("dve", lambda h, g=g, ct=ct: h.tensor_copy(out=Vc[:, g, ct, 64:65], in_=validc[:, ct:ct + 1]), reads=[validc], writes=[Vc])
                    P.op("dve", lambda h, g=g, ct=ct: h.tensor_scalar(out=Vc[:, g, ct, 65:129], in0=G["ov"][:, ct, :], scalar1=validc[:, ct:ct + 1], scalar2=None, op0=ALU.mult),
                         reads=[validc, G["ov"]], writes=[Vc])
    cx.end()


def nsa_stage(cx, G):
    P = cx.P
    cx.begin(4)
    acc = cx.psum[4:8]
    identb, identf, vcol = G["identb"], G["identf"], G["vcol"]
    KcT, Vc = G["KcT"], G["Vc"]
    cpen = cx.sb([128, 4, 512], BF16, "cpen")
    wpen = cx.sb([128, 8, 512], BF16, "wpen")
    cmpp = cx.sb([128, 8, 512], BF16, "cmpp")
    mulm = cx.sb([128, 16, 64], F32, "mulm")
    addm = cx.sb([128, 16, 64], F32, "addm")
    sgate = cx.sb([128, 16, 48], F32, "sgate")
    for t, src in ((cpen, G["c_cpen"]), (wpen, G["c_wpen"]), (cmpp, G["c_cmpp"]), (mulm, G["c_mulm"]), (addm, G["c_addm"])):
        P.op("sp", lambda h, t=t, src=src: h.dma_start(out=t.ap, in_=src), writes=[t], dma="nc")
    P.op("sp", lambda h: h.dma_start(out=sgate[:, :, :], in_=G["small"].rearrange("(s p) f -> p s f", p=128)[:, 16:32, 8:56]), writes=[sgate], dma="nc")
    P.barrier()
    P.op("act", lambda h: h.activation(out=sgate[:, :, :], in_=sgate[:, :, :], func=AF.Sigmoid), reads=[sgate], writes=[sgate])
    vview = G["v"].rearrange("(s p) b (g d) -> p s b g d", p=128, g=4)
    cg = conv_gen(cx, G)
    jobctr = [0]

    def evac(accs, qt, g, r, branch, oacc, impa=None):
        W = 129 if branch == 0 else 65
        stg, _ = cx.pool("stg%d" % W, 3, [128, 4, W], F32)
        if branch == 0:
            for hb in range(2):
                a = accs[hb]
                P.op("dve", lambda h, a=a, stg=stg, hb=hb: h.tensor_copy(out=stg[:, 2 * hb:2 * hb + 2, :], in_=a[:, 0:2 * W].rearrange("p (s w) -> p s w", s=2)), reads=[a], writes=[stg])
        else:
            a = accs
            P.op("dve", lambda h, a=a, stg=stg: h.tensor_copy(out=stg[:, :, :], in_=a[:, 0:4 * W].rearrange("p (s w) -> p s w", s=4)), reads=[a], writes=[stg])
        rl, _ = cx.pool("rl", 4, [128, 4, 2], F32)
        gc = (g * 4 + r) * 3 + branch
        P.op("dve", lambda h, stg=stg, rl=rl: h.tensor_scalar(out=rl[:, :, 0:1], in0=stg[:, :, 64:65], scalar1=1e-30, scalar2=None, op0=ALU.max), reads=[stg], writes=[rl])
        P.op("dve", lambda h, rl=rl: h.reciprocal(out=rl[:, :, 0:1], in_=rl[:, :, 0:1]), reads=[rl], writes=[rl])
        P.op("dve", lambda h, rl=rl, qt=qt, gc=gc: h.tensor_tensor(out=rl[:, :, 1:2], in0=rl[:, :, 0:1], in1=sgate[:, qt * 4:qt * 4 + 4, gc:gc + 1], op=ALU.mult), reads=[rl, sgate], writes=[rl])
        tmp, _ = cx.pool("etmp", 3, [128, 4, 64], F32)
        P.op("dve", lambda h, stg=stg, rl=rl, tmp=tmp: h.tensor_tensor(out=tmp[:, :, :], in0=stg[:, :, 0:64], in1=rl[:, :, 1:2].to_broadcast([128, 4, 64]), op=ALU.mult), reads=[stg, rl], writes=[tmp])
        P.op("dve", lambda h, tmp=tmp, r=r, oacc=oacc: h.tensor_tensor(out=oacc[:, :, r * 64:(r + 1) * 64], in0=oacc[:, :, r * 64:(r + 1) * 64], in1=tmp[:, :, :], op=ALU.add), reads=[tmp, oacc], writes=[oacc])
        if impa is not None:
            if r == 0:
                P.op("dve", lambda h, stg=stg, rl=rl, impa=impa: h.tensor_tensor(out=impa[:, :, :], in0=stg[:, :, 65:129], in1=rl[:, :, 0:1].to_broadcast([128, 4, 64]), op=ALU.mult), reads=[stg, rl], writes=[impa])
            else:
                tm2, _ = cx.pool("etmp", 3, [128, 4, 64], F32)
                P.op("dve", lambda h, stg=stg, rl=rl, tm2=tm2: h.tensor_tensor(out=tm2[:, :, :], in0=stg[:, :, 65:129], in1=rl[:, :, 0:1].to_broadcast([128, 4, 64]), op=ALU.mult), reads=[stg, rl], writes=[tm2])
                P.op("dve", lambda h, tm2=tm2, impa=impa: h.tensor_tensor(out=impa[:, :, :], in0=impa[:, :, :], in1=tm2[:, :, :], op=ALU.add), reads=[tm2, impa], writes=[impa])

    pending_out = []
    KsEs, KsEes = [], []
    for i_ in range(2):
        t_ = cx.sb([128, SEQV], BF16, "KsE%d" % i_)
        e_ = T(t_.ap, "KsEe%d" % i_)
        P.op("sp", lambda h, t_=t_: h.dma_start(out=t_[64:128, :], in_=G["c_Eexp"]), writes=[e_], dma="nc")
        KsEs.append(t_)
        KsEes.append(e_)
    P.barrier()

    def load_group(g):
        KsT, k1 = KsEs[g % 2], "KsT%d" % (g % 2)
        KwT, k2 = cx.pool("KwT", 2, [64, SEQV], BF16)
        Vs, k3 = cx.pool("Vs", 2, [128, 32, 65], BF16)
        Vw, k4 = cx.pool("Vw", 2, [128, 32, 65], BF16)
        P.op("sp", lambda h, g=g, KsT=KsT: h.dma_start(out=KsT[0:64, :], in_=G["ks"][g * 64:(g + 1) * 64, :]), writes=[KsT], dma=k1)
        P.op("sp", lambda h, g=g, KwT=KwT: h.dma_start(out=KwT[:, :], in_=G["kw"][g * 64:(g + 1) * 64, :]), writes=[KwT], dma=k2)
        for br, V, kk in ((0, Vs, k3), (1, Vw, k4)):
            for hf in range(2):
                P.op("sp", lambda h, g=g, V=V, br=br, hf=hf: h.dma_start(out=V[:, hf * 16:(hf + 1) * 16, 0:64], in_=vview[:, hf * 16:(hf + 1) * 16, br, g, :]), writes=[V], dma=kk)
            P.op("dve", lambda h, V=V: h.tensor_copy(out=V[:, 0:16, 64:65], in_=vcol[:, 0:1].unsqueeze(1).to_broadcast([128, 16, 1])), reads=[vcol], writes=[V])
            P.op("dve", lambda h, V=V: h.memset(V[:, 16:32, 64:65], 1.0), writes=[V])
        return KsT, KsEes[g % 2], KwT, Vs, Vw

    def load_q(g, qt):
        QT, kq = cx.pool("QT", 3, [128, 4, TT], BF16)
        for r in range(4):
            P.op("sp", lambda h, QT=QT, r=r, g=g, qt=qt: h.dma_start(out=QT[0:64, r, :], in_=G["qn"][(g * 4 + r) * 64:(g * 4 + r + 1) * 64, qt * TT:(qt + 1) * TT]), writes=[QT], dma=kq)
        return QT

    qt_next = [None]
    nxt_grp = load_group(0)
    for g in range(4):
        KsE, KsEe, KwT, Vs, Vw = nxt_grp
        if g < 3:
            nxt_grp = load_group(g + 1)
        for qt in range(4):
            q0 = NOWN + qt * TT
            nkt = q0 // 128 + 4
            if qt_next[0] is None:
                qt_next[0] = load_q(g, qt)
            QT = qt_next[0]
            nb_ = g * 4 + qt + 1
            qt_next[0] = load_q(nb_ // 4, nb_ % 4) if nb_ < 16 else None
            QTp = T(QT.ap, "QTp")
            oacc, _ = cx.pool("oacc", 2, [128, 4, 256], F32)
            impa, _ = cx.pool("impa", 2, [128, 4, 64], F32)
            P.op("dve", lambda h, oacc=oacc: h.memset(oacc[:, :, :], 0.0), writes=[oacc])
            def run_jobs(jobs, depth=4, hooks=None):
                q = []
                for ji, (sc_fn, pv_fn) in enumerate(jobs):
                    if hooks and ji in hooks:
                        hooks[ji]()
                    q.append((sc_fn(), pv_fn))
                    jobctr[0] += 1
                    if jobctr[0] % 6 == 0:
                        next(cg, None)
                    if len(q) > depth:
                        pc, f_ = q.pop(0)
                        f_(pc)
                for pc, f_ in q:
                    f_(pc)

            def cmp_score(r):
                Pc = []
                for ct in range(2):
                    ps = cx.ps()
                    mm(P, ps, ps[:, :], KcT, KcT[:, g, ct * 128:(ct + 1) * 128], QT, QT[0:64, r, :], True, False)
                    mm(P, ps, ps[:, :], identb, identb[:, :], cmpp, cmpp[:, qt * 2 + ct, :], False, True)
                    pc, _ = cx.pool("Ptc", 6, [128, TT], BF16)
                    P.op("act", lambda h, ps=ps, pc=pc: h.activation(out=pc[:, :], in_=ps[:, :], func=AF.Exp), reads=[ps], writes=[pc])
                    Pc.append(pc)
                return Pc

            def cmp_pv(r, Pc):
                for sub in range(4):
                    a = acc[2 + sub // 2]
                    o0 = (sub % 2) * 129
                    for ct in range(2):
                        mm(P, a, a[:, o0:o0 + 129], Pc[ct], Pc[ct][:, sub * 128:(sub + 1) * 128], Vc, Vc[:, g, ct, :], (ct == 0 and sub % 2 == 0), (ct == 1 and sub % 2 == 1))
                evac(acc[2:4], qt, g, r, 0, oacc, impa)

            run_jobs([((lambda r=r: cmp_score(r)), (lambda Pc, r=r: cmp_pv(r, Pc))) for r in range(4)], depth=2)
            pens = []
            for sub in range(4):
                i2, _ = cx.pool("i2", 2, [128, 64], F32)
                i3, _ = cx.pool("i3", 2, [128, 64], F32)
                m8, _ = cx.pool("m8", 2, [128, 16], F32)
                penp, _ = cx.pool("pen", 4, [128, 128], F32)
                P.op("dve", lambda h, penp=penp: h.memset(penp[:, 0:64], 0.0), writes=[penp])
                pen = T(penp.ap[:, 64:128], "penv")
                pen.w, pen.r = None, []
                P.op("dve", lambda h, i2=i2, sub=sub, impa=impa, qt=qt: h.tensor_tensor(out=i2[:, :], in0=impa[:, sub, :], in1=mulm[:, qt * 4 + sub, :], op=ALU.mult), reads=[impa, mulm], writes=[i2])
                P.op("dve", lambda h, i2=i2, sub=sub, qt=qt: h.tensor_tensor(out=i2[:, :], in0=i2[:, :], in1=addm[:, qt * 4 + sub, :], op=ALU.add), reads=[i2, addm], writes=[i2])
                P.op("dve", lambda h, i2=i2, m8=m8: h.max(out=m8[:, 0:8], in_=i2[:, :]), reads=[i2], writes=[m8])
                P.op("dve", lambda h, i2=i2, i3=i3, m8=m8: h.match_replace(out=i3[:, :], in_to_replace=m8[:, 0:8], in_values=i2[:, :], imm_value=-1e30), reads=[i2, m8], writes=[i3])
                P.op("dve", lambda h, i3=i3, m8=m8: h.max(out=m8[:, 8:16], in_=i3[:, :]), reads=[i3, m8], writes=[m8])
                P.op("dve", lambda h, i2=i2, m8=m8, pen=pen: h.tensor_scalar(out=pen[:, :], in0=i2[:, :], scalar1=m8[:, 15:16], scalar2=None, op0=ALU.is_ge), reads=[i2, m8], writes=[pen])
                P.op("dve", lambda h, pen=pen: h.tensor_scalar(out=pen[:, :], in0=pen[:, :], scalar1=-1.0, scalar2=-NEGB, op0=ALU.add, op1=ALU.mult), reads=[pen], writes=[pen])
                P.op("dve", lambda h, penp=penp: h.tensor_copy(out=penp[:, 64:65], in_=penp[:, 64:65]), reads=[pen], writes=[penp])
                pens.append(penp)

            def win_score(r, rel):
                kt = q0 // 128 - 4 + rel
                ps = cx.ps()
                c0, c1 = 128 * max(0, rel - 4), 128 * (min(3, rel) + 1)
                mm(P, ps, ps[:, c0:c1], KwT, KwT[:, kt * 128:(kt + 1) * 128], QT, QT[0:64, r, c0:c1], True, True)
                pc, _ = cx.pool("Pt", 7, [128, TT], BF16)
                P.op("act", lambda h, ps=ps, pc=pc, c0=c0, c1=c1: h.activation(out=pc[:, c0:c1], in_=ps[:, c0:c1], func=AF.Exp), reads=[ps], writes=[pc])
                for s_ in (rel, rel - 4):
                    if 0 <= s_ <= 3:
                        a0, a1 = 128 * s_, 128 * (s_ + 1)
                        P.op("pool", lambda h, pc=pc, rel=rel, a0=a0, a1=a1: h.tensor_tensor(out=pc[:, a0:a1], in0=pc[:, a0:a1], in1=wpen[:, rel, a0:a1], op=ALU.mult), reads=[pc, wpen], writes=[pc])
                return pc

            def win_pv(r, rel, pc):
                kt = q0 // 128 - 4 + rel
                for sub in range(4):
                    if not (sub <= rel <= sub + 4):
                        continue
                    a = acc[r % 2]
                    mm(P, a, a[:, sub * 65:(sub + 1) * 65], pc, pc[:, sub * 128:(sub + 1) * 128], Vw, Vw[:, kt, :], (rel == 0 and sub == 0), (rel == 7 and sub == 3))
                if rel == 7:
                    evac(acc[r % 2], qt, g, r, 2, oacc)

            def emit_pen():
                pst = cx.ps()
                for sub in range(4):
                    penp = pens[sub]
                    P.op("pe", lambda h, pst=pst, penp=penp, sub=sub: h.transpose(out=pst[:, sub * 128:(sub + 1) * 128], in_=penp[:, :], identity=identf[:, :]), reads=[penp, identf], writes=[pst])
                for r in range(4):
                    P.op("dve", lambda h, pst=pst, QT=QT, r=r: h.tensor_copy(out=QT[64:128, r, :], in_=pst[64:128, :]), reads=[pst], writes=[QTp])

            def emit_prev_out():
                while pending_out:
                    pending_out.pop(0)()

            run_jobs([((lambda r=r, rel=rel: win_score(r, rel)), (lambda pc, r=r, rel=rel: win_pv(r, rel, pc))) for rp in range(2) for rel in range(8) for r in (2 * rp, 2 * rp + 1)],
                     hooks={6: emit_prev_out, 20: emit_pen})
            def slc_score(r, kt):
                rel = kt - (nkt - 4)
                ps = cx.ps()
                c0 = 128 * max(0, rel)
                P.op("pe", lambda h, ps=ps, kt=kt, r=r, QT=QT, c0=c0, KsE=KsE: h.matmul(ps[:, c0:], KsE[:, kt * 128:(kt + 1) * 128], QT[:, r, c0:], start=True, stop=True),
                     reads=[KsE, KsEe, QT, QTp], writes=[ps])
                pc, _ = cx.pool("Pt", 7, [128, TT], BF16)
                P.op("act", lambda h, ps=ps, pc=pc, c0=c0: h.activation(out=pc[:, c0:], in_=ps[:, c0:], func=AF.Exp), reads=[ps], writes=[pc])
                if rel >= 0:
                    P.op("pool", lambda h, pc=pc, rel=rel, c0=c0: h.tensor_tensor(out=pc[:, c0:c0 + 128], in0=pc[:, c0:c0 + 128], in1=cpen[:, rel, c0:c0 + 128], op=ALU.mult), reads=[pc, cpen], writes=[pc])
                return pc

            def slc_pv(r, kt, pc):
                rel = kt - (nkt - 4)
                for sub in range(4):
                    if rel > sub:
                        continue
                    a = acc[r % 2]
                    mm(P, a, a[:, sub * 65:(sub + 1) * 65], pc, pc[:, sub * 128:(sub + 1) * 128], Vs, Vs[:, kt, :], (kt == 0 and sub == 0), (kt == nkt - 1 and sub == 3))
                if kt == nkt - 1:
                    evac(acc[r % 2], qt, g, r, 1, oacc)

            run_jobs([((lambda r=r, kt=kt: slc_score(r, kt)), (lambda pc, r=r, kt=kt: slc_pv(r, kt, pc))) for rp in range(2) for kt in range(nkt) for r in (2 * rp, 2 * rp + 1)])
            def emit_out(oacc=oacc, g=g, qt=qt):
                for sub in range(4):
                    for hp in range(2):
                        pst = cx.ps()
                        P.op("pe", lambda h, pst=pst, oacc=oacc, sub=sub, hp=hp: h.transpose(out=pst[:, 0:128], in_=oacc[:, sub, hp * 128:(hp + 1) * 128], identity=identf[:, :]),
                             reads=[oacc, identf], writes=[pst])
                        ob, ko = cx.pool("onb", 3, [128, 128], BF16)
                        P.op("act", lambda h, pst=pst, ob=ob: h.activation(out=ob[:, :], in_=pst[:, 0:128], func=AF.Copy), reads=[pst], writes=[ob])
                        r0 = (g * 4 + hp * 2) * 64
                        c0 = qt * TT + sub * 128
                        P.op("sp", lambda h, ob=ob, r0=r0, c0=c0: h.dma_start(out=G["on"][r0:r0 + 128, c0:c0 + 128], in_=ob[:, :]), reads=[ob], dma=ko)
            pending_out.append(emit_out)
    for f_ in pending_out:
        f_()
    del pending_out[:]
    for _ in cg:
        pass
    cx.end()


def conv_gen(cx, G):
    P = cx.P
    identf, vcol = G["identf"], G["vcol"]
    cw, cb = G["cw_sb"], G["cb_sb"]
    HS = NOWN
    for fc in range(16):
        isq = fc < 8
        src = G["mqr"] if isq else G["mkr"]
        f0 = (fc % 8) * 128
        for seg in ([1] if isq else [0, 1]):
            u, ku = cx.pool("cu", 2, [128, 3 + HS], F32)
            if seg == 0:
                P.op("dve", lambda h, u=u: h.memset(u[:, 0:3], 0.0), writes=[u])
                yield
                P.op("sp", lambda h, u=u, src=src, f0=f0: h.dma_start(out=u[:, 3:3 + HS], in_=src[f0:f0 + 128, 0:HS]), writes=[u], dma=ku)
                yield
            else:
                P.op("sp", lambda h, u=u, src=src, f0=f0: h.dma_start(out=u[:, 0:3 + HS], in_=src[f0:f0 + 128, HS - 3:2 * HS]), writes=[u], dma=ku)
                yield
                P.op("dve", lambda h, u=u: h.tensor_scalar(out=u[:, 0:3], in0=u[:, 0:3], scalar1=vcol[:, 0:1], scalar2=None, op0=ALU.mult), reads=[u, vcol], writes=[u])
                yield
            a, _ = cx.pool("ca", 2, [128, HS], F32)
            P.op("dve", lambda h, u=u, a=a, fc=fc: h.tensor_scalar(out=a[:, :], in0=u[:, 0:HS], scalar1=cw[:, fc, 0:1], scalar2=cb[:, fc:fc + 1], op0=ALU.mult, op1=ALU.add),
                 reads=[u, cw, cb], writes=[a])
            yield
            for j in range(1, 4):
                P.op("dve", lambda h, u=u, a=a, fc=fc, j=j: h.scalar_tensor_tensor(out=a[:, :], in0=u[:, j:j + HS], scalar=cw[:, fc, j:j + 1], in1=a[:, :], op0=ALU.mult, op1=ALU.add),
                     reads=[u, a, cw], writes=[a])
                yield
            P.op("act", lambda h, a=a: h.activation(out=a[:, :], in_=a[:, :], func=AF.Silu), reads=[a], writes=[a])
            yield
            ob, ko = cx.pool("cob", 2, [128, HS], BF16)
            if isq:
                P.op("dve", lambda h, a=a, ob=ob: h.tensor_copy(out=ob[:, :], in_=a[:, :]), reads=[a], writes=[ob])
                yield
                P.op("sp", lambda h, ob=ob, f0=f0: h.dma_start(out=G["qc"][f0:f0 + 128, :], in_=ob[:, :]), reads=[ob], dma=ko)
                yield
            else:
                P.op("dve", lambda h, a=a, ob=ob: h.tensor_scalar(out=ob[:, :], in0=a[:, :], scalar1=1.0 / 16, scalar2=None, op0=ALU.mult), reads=[a], writes=[ob])
                yield
                P.op("sp", lambda h, ob=ob, f0=f0, seg=seg: h.dma_start(out=G["kc"][f0:f0 + 128, seg * HS:(seg + 1) * HS], in_=ob[:, :]), reads=[ob], dma=ko)
                yield
                for i4 in range(4):
                    pst = cx.ps()
                    for s in range(4):
                        i = i4 * 4 + s
                        P.op("pe", lambda h, pst=pst, a=a, s=s, i=i: h.transpose(out=pst[:, s * 128:(s + 1) * 128], in_=a[:, i * 128:(i + 1) * 128], identity=identf[:, :]),
                             reads=[a, identf], writes=[pst])
                    kt_, kk = cx.pool("ckt", 2, [128, TT], BF16)
                    P.op("act", lambda h, pst=pst, kt_=kt_: h.activation(out=kt_[:, :], in_=pst[:, :], func=AF.Copy, scale=1.0 / 16), reads=[pst], writes=[kt_])
                    tt0 = (seg * HS) // 128 + i4 * 4
                    dstv = G["kctm"].rearrange("(s p) f -> p s f", p=128)[:, tt0:tt0 + 4, f0:f0 + 128]
                    P.op("sp", lambda h, kt_=kt_, dstv=dstv: h.dma_start(out=dstv, in_=kt_[:, :].rearrange("p (s f) -> p s f", s=4)), reads=[kt_], dma=kk)
                    yield


def mlstm_stage(cx, G):
    P = cx.P
    cx.begin(8)
    ones_bf, identf, vcol, triu, ones_f = G["ones_bf"], G["identf"], G["vcol"], G["triu"], G["ones_f"]
    cw, cb = G["cw_sb"], G["cb_sb"]
    HS = NOWN
    small = cx.sb([128, 32, 8], F32, "small")
    logf = cx.sb([128, 32, 4], F32, "logf")
    P.op("sp", lambda h: h.dma_start(out=small[:, :, :], in_=G["small"].rearrange("(s p) f -> p s f", p=128)[:, :, 0:8]), writes=[small], dma="ms")
    P.op("act", lambda h: h.activation(out=logf[:, :, :], in_=small[:, :, 4:8], func=AF.Exp, scale=-1.0), reads=[small], writes=[logf])
    P.op("dve", lambda h: h.tensor_scalar(out=logf[:, :, :], in0=logf[:, :, :], scalar1=1.0, scalar2=None, op0=ALU.add), reads=[logf], writes=[logf])
    P.op("act", lambda h: h.activation(out=logf[:, :, :], in_=logf[:, :, :], func=AF.Ln), reads=[logf], writes=[logf])
    P.op("dve", lambda h: h.tensor_scalar(out=logf[:, :, :], in0=logf[:, :, :], scalar1=-1.0, scalar2=None, op0=ALU.mult), reads=[logf], writes=[logf])
    C = [cx.sb([128, 2, 256], F32, "C%d" % h_) for h_ in range(4)]
    Cb = [cx.sb([128, 2, 256], BF16, "Cb%d" % h_) for h_ in range(4)]
    nb = [cx.sb([128, 2, 128], F32, "nb%d" % h_) for h_ in range(4)]
    nbb = [cx.sb([128, 2, 128], BF16, "nbb%d" % h_) for h_ in range(4)]
    for t in C + Cb + nb + nbb:
        P.op("dve", lambda h, t=t: h.memset(t[:, :, :], 0.0), writes=[t])
    kcv = G["kc"].rearrange("(c p) t -> p c t", p=128)
    qcv = G["qc"].rearrange("(c p) t -> p c t", p=128)
    hcv = G["hc"].rearrange("(c p) t -> p c t", p=128)
    def prep_tile(tile):
        own = tile >= 16
        Wm = ea = qT = None
        R, _ = cx.pool("R", 2, [128, 4, 128], F32)
        P.op("dve", lambda h, R=R, tile=tile: h.tensor_tensor(out=R[:, :, :], in0=triu[:, :].unsqueeze(1).to_broadcast([128, 4, 128]),
                                                              in1=logf[:, tile, :].unsqueeze(2).to_broadcast([128, 4, 128]), op=ALU.mult), reads=[triu, logf], writes=[R])
        psA = cx.ps()
        mm(P, psA, psA[:, :], ones_f, ones_f[:, :], R, R[:, :, :].rearrange("p a b -> p (a b)"), True, True)
        psB = cx.ps()
        mm(P, psB, psB[:, 0:4], triu, triu[:, :], logf, logf[:, tile, :], True, True)
        mm(P, psB, psB[:, 4:8], ones_f, ones_f[:, :], logf, logf[:, tile, :], True, True)
        bc, _ = cx.pool("bc", 2, [128, 4], F32)
        ucol, _ = cx.pool("ucol", 2, [128, 4], F32)
        eg, _ = cx.pool("eg", 2, [128, 4], F32)
        P.op("dve", lambda h, bc=bc, psB=psB, tile=tile: h.tensor_tensor(out=bc[:, :], in0=small[:, tile, 0:4], in1=psB[:, 0:4], op=ALU.subtract), reads=[small, psB], writes=[bc])
        P.op("dve", lambda h, bc=bc, ucol=ucol, psB=psB: h.tensor_tensor(out=ucol[:, :], in0=bc[:, :], in1=psB[:, 4:8], op=ALU.add), reads=[bc, psB], writes=[ucol])
        P.op("act", lambda h, ucol=ucol: h.activation(out=ucol[:, :], in_=ucol[:, :], func=AF.Exp), reads=[ucol], writes=[ucol])
        P.op("act", lambda h, eg=eg, psB=psB: h.activation(out=eg[:, :], in_=psB[:, 4:8], func=AF.Exp), reads=[psB], writes=[eg])
        kT, k1 = cx.pool("kT", 2, [128, 8, 128], BF16)
        kM, k2 = cx.pool("kM", 2, [128, 1024], BF16)
        vM, k3 = cx.pool("vM", 2, [128, 1024], BF16)
        P.op("sp", lambda h, kM=kM, tile=tile: h.dma_start(out=kM[:, :], in_=G["kctm"][tile * 128:(tile + 1) * 128, :]), writes=[kM], dma=k2)
        P.op("sp", lambda h, vM=vM, tile=tile: h.dma_start(out=vM[:, :], in_=G["mv"][tile * 128:(tile + 1) * 128, :]), writes=[vM], dma=k3)
        if own:
            P.op("sp", lambda h, kT=kT, tile=tile: h.dma_start(out=kT[:, :, :], in_=kcv[:, :, tile * 128:(tile + 1) * 128]), writes=[kT], dma=k1)
            qT, k4 = cx.pool("qT", 2, [128, 8, 128], BF16)
            P.op("sp", lambda h, qT=qT, tile=tile: h.dma_start(out=qT[:, :, :], in_=qcv[:, :, (tile - 16) * 128:(tile - 15) * 128]), writes=[qT], dma=k4)
            Wm, _ = cx.pool("Wm", 2, [128, 4, 128], F32)
            ea, _ = cx.pool("ea", 2, [128, 4, 128], F32)
            P.op("dve", lambda h, Wm=Wm, psA=psA, bc=bc: h.tensor_tensor(out=Wm[:, :, :], in0=psA[:, :].rearrange("p (a b) -> p a b", a=4),
                                                                         in1=bc[:, :].unsqueeze(2).to_broadcast([128, 4, 128]), op=ALU.add), reads=[psA, bc], writes=[Wm])
            P.op("act", lambda h, Wm=Wm: h.activation(out=Wm[:, :, :], in_=Wm[:, :, :], func=AF.Exp), reads=[Wm], writes=[Wm])
            P.op("dve", lambda h, Wm=Wm: h.tensor_tensor(out=Wm[:, :, :], in0=Wm[:, :, :], in1=triu[:, :].unsqueeze(1).to_broadcast([128, 4, 128]), op=ALU.mult), reads=[Wm, triu], writes=[Wm])
            P.op("act", lambda h, ea=ea, psA=psA: h.activation(out=ea[:, :, :], in_=psA[:, :].rearrange("p (a b) -> p a b", a=4), func=AF.Exp), reads=[psA], writes=[ea])
        return dict(own=own, bc=bc, ucol=ucol, eg=eg, kT=kT, kM=kM, vM=vM, qT=qT, Wm=Wm, ea=ea)

    nxt_prep = prep_tile(0)
    for tile in range(32):
        cur_ = nxt_prep
        if tile + 1 < 32:
            nxt_prep = prep_tile(tile + 1)
        own, bc, ucol, eg, kT, kM, vM, qT, Wm, ea = (cur_[k_] for k_ in ("own", "bc", "ucol", "eg", "kT", "kM", "vM", "qT", "Wm", "ea"))
        for hp in range(2):
            hds = (2 * hp, 2 * hp + 1)
            st = {}
            if own:
                for hd in hds:
                    ps = cx.ps()
                    for dc in range(2):
                        mm(P, ps, ps[:, 0:128], kT, kT[:, 2 * hd + dc, :], qT, qT[:, 2 * hd + dc, :], dc == 0, dc == 1)
                    st[hd, "ps"] = ps
            for hd in hds:
                if own:
                    ps = st[hd, "ps"]
                    AT, _ = cx.pool("AT", 4, [128, 128], BF16)
                    P.op("dve", lambda h, AT=AT, ps=ps, Wm=Wm, hd=hd: h.tensor_tensor(out=AT[:, :], in0=ps[:, 0:128], in1=Wm[:, hd, :], op=ALU.mult), reads=[ps, Wm], writes=[AT])
                    qs, _ = cx.pool("qs", 4, [128, 2, 128], BF16)
                    P.op("dve", lambda h, qs=qs, qT=qT, ea=ea, hd=hd: h.tensor_tensor(out=qs[:, :, :], in0=qT[:, 2 * hd:2 * hd + 2, :], in1=ea[:, hd:hd + 1, :].to_broadcast([128, 2, 128]), op=ALU.mult),
                         reads=[qT, ea], writes=[qs])
                    st[hd, "AT"] = AT
                    st[hd, "qs"] = qs
                uk, _ = cx.pool("uk", 4, [128, 256], BF16)
                P.op("dve", lambda h, uk=uk, kM=kM, ucol=ucol, hd=hd: h.tensor_scalar(out=uk[:, :], in0=kM[:, hd * 256:(hd + 1) * 256], scalar1=ucol[:, hd:hd + 1], scalar2=None, op0=ALU.mult),
                     reads=[kM, ucol], writes=[uk])
                st[hd, "uk"] = uk
            for hd in hds:
                if own:
                    AT, qs = st[hd, "AT"], st[hd, "qs"]
                    pn = cx.ps()
                    for dch in range(2):
                        o_ = pn[:, dch * 128:(dch + 1) * 128]
                        mm(P, pn, o_, vM, vM[:, hd * 256 + dch * 128:hd * 256 + (dch + 1) * 128], AT, AT[:, :], True, False)
                        for ec in range(2):
                            mm(P, pn, o_, Cb[hd], Cb[hd][:, ec, dch * 128:(dch + 1) * 128], qs, qs[:, ec, :], False, ec == 1)
                    o_ = pn[:, 256:384]
                    mm(P, pn, o_, ones_bf, ones_bf[:, :], AT, AT[:, :], True, False)
                    for ec in range(2):
                        mm(P, pn, o_, nbb[hd], nbb[hd][:, ec, :], qs, qs[:, ec, :], False, ec == 1)
                    st[hd, "pn"] = pn
                uk = st[hd, "uk"]
                pc = cx.ps()
                pnb = cx.ps()
                for ec in range(2):
                    mm(P, pc, pc[:, ec * 256:(ec + 1) * 256], uk, uk[:, ec * 128:(ec + 1) * 128], vM, vM[:, hd * 256:(hd + 1) * 256], True, True)
                for ec in range(2):
                    mm(P, pnb, pnb[:, ec * 128:(ec + 1) * 128], uk, uk[:, ec * 128:(ec + 1) * 128], ones_bf, ones_bf[:, :], True, True)
                st[hd, "pc"] = pc
                st[hd, "pnb"] = pnb
            for hd in hds:
                if own:
                    pn = st[hd, "pn"]
                    rd, _ = cx.pool("rd", 4, [128, 128], F32)
                    P.op("act", lambda h, rd=rd, pn=pn: h.activation(out=rd[:, :], in_=pn[:, 256:384], func=AF.Abs), reads=[pn], writes=[rd])
                    P.op("dve", lambda h, rd=rd: h.tensor_scalar(out=rd[:, :], in0=rd[:, :], scalar1=1.0, scalar2=None, op0=ALU.max), reads=[rd], writes=[rd])
                    P.op("dve", lambda h, rd=rd: h.reciprocal(out=rd[:, :], in_=rd[:, :]), reads=[rd], writes=[rd])
                    ho, kh = cx.pool("ho", 4, [128, 2, 128], F32)
                    P.op("dve", lambda h, ho=ho, pn=pn, rd=rd: h.tensor_tensor(out=ho[:, :, :], in0=pn[:, 0:256].rearrange("p (a b) -> p a b", a=2),
                                                                             in1=rd[:, :].unsqueeze(1).to_broadcast([128, 2, 128]), op=ALU.mult), reads=[pn, rd], writes=[ho])
                    P.op("sp", lambda h, ho=ho, hd=hd, tile=tile: h.dma_start(out=hcv[:, 2 * hd:2 * hd + 2, (tile - 16) * 128:(tile - 15) * 128], in_=ho[:, :, :]), reads=[ho], dma=kh)
                pc, pnb = st[hd, "pc"], st[hd, "pnb"]
                Cf = C[hd][:, :, :].rearrange("p a b -> p (a b)")
                nf = nb[hd][:, :, :].rearrange("p a b -> p (a b)")
                P.op("dve", lambda h, Cf=Cf, pc=pc, eg=eg, hd=hd: h.scalar_tensor_tensor(out=Cf, in0=Cf, scalar=eg[:, hd:hd + 1], in1=pc[:, :], op0=ALU.mult, op1=ALU.add), reads=[C[hd], eg, pc], writes=[C[hd]])
                P.op("dve", lambda h, nf=nf, pnb=pnb, eg=eg, hd=hd: h.scalar_tensor_tensor(out=nf, in0=nf, scalar=eg[:, hd:hd + 1], in1=pnb[:, 0:256], op0=ALU.mult, op1=ALU.add), reads=[nb[hd], eg, pnb], writes=[nb[hd]])
                if tile == 15:
                    P.op("dve", lambda h, Cf=Cf, hd=hd: h.tensor_scalar(out=Cf, in0=Cf, scalar1=vcol[:, 0:1], scalar2=None, op0=ALU.mult), reads=[C[hd], vcol], writes=[C[hd]])
                    P.op("dve", lambda h, nf=nf, hd=hd: h.tensor_scalar(out=nf, in0=nf, scalar1=vcol[:, 0:1], scalar2=None, op0=ALU.mult), reads=[nb[hd], vcol], writes=[nb[hd]])
                P.op("act", lambda h, hd=hd: h.activation(out=Cb[hd][:, :, :], in_=C[hd][:, :, :], func=AF.Copy), reads=[C[hd]], writes=[Cb[hd]])
                P.op("act", lambda h, hd=hd: h.activation(out=nbb[hd][:, :, :], in_=nb[hd][:, :, :], func=AF.Copy), reads=[nb[hd]], writes=[nbb[hd]])
    P.barrier()
    mov = G["mo"].rearrange("(c p) t -> p c t", p=128)
    hmv = G["hm"].rearrange("(c p) t -> p c t", p=128)
    mg = G["mgain_sb"]
    for t4 in range(NOWN // TT):
        for hd in range(4):
            hc_, k1 = cx.pool("hcl", 2, [128, 2, TT], F32)
            mo_, k2 = cx.pool("mol", 2, [128, 2, TT], F32)
            P.op("sp", lambda h, hc_=hc_, hd=hd, t4=t4: h.dma_start(out=hc_[:, :, :], in_=hcv[:, 2 * hd:2 * hd + 2, t4 * TT:(t4 + 1) * TT]), writes=[hc_], dma=k1)
            P.op("sp", lambda h, mo_=mo_, hd=hd, t4=t4: h.dma_start(out=mo_[:, :, :], in_=mov[:, 2 * hd:2 * hd + 2, t4 * TT:(t4 + 1) * TT]), writes=[mo_], dma=k2)
            ss = cx.ps()
            for dc in range(2):
                sq, _ = cx.pool("sq", 3, [128, TT], BF16)
                P.op("act", lambda h, sq=sq, hc_=hc_, dc=dc: h.activation(out=sq[:, :], in_=hc_[:, dc, :], func=AF.Square), reads=[hc_], writes=[sq])
                mm(P, ss, ss[:, :], ones_bf, ones_bf[:, :], sq, sq[:, :], dc == 0, dc == 1)
            rs, _ = cx.pool("rstd", 2, [128, TT], F32)
            P.op("dve", lambda h, ss=ss, rs=rs: h.tensor_scalar(out=rs[:, :], in0=ss[:, :], scalar1=1.0 / 256, scalar2=EPS, op0=ALU.mult, op1=ALU.add), reads=[ss], writes=[rs])
            P.op("act", lambda h, rs=rs: h.activation(out=rs[:, :], in_=rs[:, :], func=AF.Sqrt), reads=[rs], writes=[rs])
            P.op("dve", lambda h, rs=rs: h.reciprocal(out=rs[:, :], in_=rs[:, :]), reads=[rs], writes=[rs])
            ob, ko = cx.pool("hmo", 2, [128, 2, TT], BF16)
            for dc in range(2):
                P.op("dve", lambda h, hc_=hc_, rs=rs, dc=dc, hd=hd: h.scalar_tensor_tensor(out=hc_[:, dc, :], in0=hc_[:, dc, :], scalar=mg[:, 2 * hd + dc:2 * hd + dc + 1], in1=rs[:, :], op0=ALU.mult, op1=ALU.mult),
                     reads=[hc_, rs, mg], writes=[hc_])
            P.op("dve", lambda h, hc_=hc_, mo_=mo_, ob=ob: h.tensor_tensor(out=ob[:, :, :], in0=hc_[:, :, :], in1=mo_[:, :, :], op=ALU.mult), reads=[hc_, mo_], writes=[ob])
            P.op("sp", lambda h, ob=ob, hd=hd, t4=t4: h.dma_start(out=hmv[:, 2 * hd:2 * hd + 2, t4 * TT:(t4 + 1) * TT], in_=ob[:, :, :]), reads=[ob], dma=ko)
    cx.end()


def merge_stage(cx, G):
    P = cx.P
    cx.begin(8)
    onv = G["on"].rearrange("(c p) t -> p c t", p=128)
    hmv = G["hm"].rearrange("(c p) t -> p c t", p=128)
    mgv = G["mg"].rearrange("(c p) t -> p c t", p=128)
    x1v = G["x1T"].rearrange("(c p) t -> p c t", p=128)
    x2v = G["x2T"].rearrange("(c p) t -> p c t", p=128)
    for t4 in range(NOWN // TS):
        c0 = t4 * TS
        a_, k1 = cx.pool("mon", 1, [128, 8, TS], BF16)
        b_, k2 = cx.pool("mhm", 1, [128, 8, TS], BF16)
        for hf in range(2):
            P.op("sp", lambda h, a_=a_, c0=c0, hf=hf: h.dma_start(out=a_[:, 4 * hf:4 * hf + 4, :], in_=onv[:, 4 * hf:4 * hf + 4, c0:c0 + TS]), writes=[a_], dma=k1)
            P.op("sp", lambda h, b_=b_, c0=c0, hf=hf: h.dma_start(out=b_[:, 4 * hf:4 * hf + 4, :], in_=hmv[:, 4 * hf:4 * hf + 4, c0:c0 + TS]), writes=[b_], dma=k2)
        mT, _ = cx.pool("mT", 1, [128, KC, TS], BF16)
        for oc in range(KC):
            wa, ka = cx.pool("wB", 4, [128, 8 * 128], BF16)
            P.op("pool", lambda h, wa=wa, oc=oc: h.dma_start(out=wa[:, :], in_=G["wbn"][oc, :, :], max_dma_last_dim=4096), writes=[wa], dma=ka)
            wb, kb = cx.pool("wB", 4, [128, 8 * 128], BF16)
            P.op("pool", lambda h, wb=wb, oc=oc: h.dma_start(out=wb[:, :], in_=G["wbm"][oc, :, :], max_dma_last_dim=4096), writes=[wb], dma=kb)
            gA, kga = cx.pool("gA", 3, [128, TS], F32)
            gB, kgb = cx.pool("gB", 3, [128, TS], F32)
            P.op("sp", lambda h, gA=gA, oc=oc, c0=c0: h.dma_start(out=gA[:, :], in_=mgv[:, oc, c0:c0 + TS]), writes=[gA], dma=kga)
            P.op("sp", lambda h, gB=gB, oc=oc, c0=c0: h.dma_start(out=gB[:, :], in_=mgv[:, KC + oc, c0:c0 + TS]), writes=[gB], dma=kgb)
            pa = [cx.ps() for _ in range(NS)]
            pb = [cx.ps() for _ in range(NS)]
            for c in range(8):
                for s_ in range(NS):
                    mm(P, pa[s_], pa[s_][:, :], wa, wa[:, c * 128:(c + 1) * 128], a_, a_[:, c, s_ * TT:(s_ + 1) * TT], c == 0, c == 7)
            for c in range(8):
                for s_ in range(NS):
                    mm(P, pb[s_], pb[s_][:, :], wb, wb[:, c * 128:(c + 1) * 128], b_, b_[:, c, s_ * TT:(s_ + 1) * TT], c == 0, c == 7)
            for s_ in range(NS):
                sl = slice(s_ * TT, (s_ + 1) * TT)
                P.op("dve", lambda h, gA=gA, pa_=pa[s_], sl=sl: h.tensor_tensor(out=gA[:, sl], in0=gA[:, sl], in1=pa_[:, :], op=ALU.mult), reads=[gA, pa[s_]], writes=[gA])
                P.op("dve", lambda h, gB=gB, pb_=pb[s_], sl=sl: h.tensor_tensor(out=gB[:, sl], in0=gB[:, sl], in1=pb_[:, :], op=ALU.mult), reads=[gB, pb[s_]], writes=[gB])
            P.op("dve", lambda h, gA=gA, gB=gB, mT=mT, oc=oc: h.tensor_tensor(out=mT[:, oc, :], in0=gA[:, :], in1=gB[:, :], op=ALU.add), reads=[gA, gB], writes=[mT])
        for oc in range(KC):
            wo, kw_ = cx.pool("wA", 4, [128, KC * 128], BF16)
            P.op("pool", lambda h, wo=wo, oc=oc: h.dma_start(out=wo[:, :], in_=G["wo"][oc, :, :], max_dma_last_dim=8192), writes=[wo], dma=kw_)
            xt, kx = cx.pool("xt", 4, [128, TS], F32)
            P.op("sp", lambda h, xt=xt, oc=oc, c0=c0: h.dma_start(out=xt[:, :], in_=x1v[:, oc, NOWN + c0:NOWN + c0 + TS]), writes=[xt], dma=kx + "l")
            po = [cx.ps() for _ in range(NS)]
            for c in range(KC):
                for s_ in range(NS):
                    mm(P, po[s_], po[s_][:, :], wo, wo[:, c * 128:(c + 1) * 128], mT, mT[:, c, s_ * TT:(s_ + 1) * TT], c == 0, c == KC - 1)
            for s_ in range(NS):
                sl = slice(s_ * TT, (s_ + 1) * TT)
                P.op("dve", lambda h, xt=xt, po_=po[s_], sl=sl: h.tensor_tensor(out=xt[:, sl], in0=xt[:, sl], in1=po_[:, :], op=ALU.add), reads=[xt, po[s_]], writes=[xt])
            P.op("sp", lambda h, xt=xt, oc=oc, c0=c0: h.dma_start(out=x2v[:, oc, c0:c0 + TS], in_=xt[:, :]), reads=[xt], dma=kx + "s")
    cx.end()


def build(stage="full"):
    nc = bass.Bass("TRN2", target_bir_lowering=False)

    def din(name, shape, dt=F32):
        return nc.dram_tensor(name, list(shape), dt, kind="ExternalInput").ap()

    def scr(name, shape, dt=F32):
        return nc.dram_tensor(name, list(shape), dt, kind="Internal").ap()

    G = {}
    xT = din("xT", [D, SEQV])
    g1 = din("g1", [128, KC]); g2 = din("g2", [128, KC]); gmix = din("gmix", [128, KC])
    wg1 = din("wg1", [FC, 128, D]); wu1 = din("wu1", [FC, 128, D]); wd1 = din("wd1", [KC, 128, DFF])
    wg2 = din("wg2", [FC, 128, D]); wu2 = din("wu2", [FC, 128, D]); wd2 = din("wd2", [KC, 128, DFF])
    G["wi"] = din("wi", [NT_IN, 128, D]); bi = din("bi", [128, NT_IN])
    G["w1"] = din("w1", [2, 64, 32 * 256]); G["w2"] = din("w2", [2, 128, 128]); G["posT"] = din("posT", [2, 64, 32])
    G["wbn"] = din("wbn", [KC, 128, 1024]); G["wbm"] = din("wbm", [KC, 128, 1024]); G["wo"] = din("wo", [KC, 128, D])
    cbf = din("cbf", [128, 3 * 128], BF16)
    cf = din("cf", [128, 3 * 128 + 64 + 2 * 64])
    gk = din("gk", [128, 4]); mgain = din("mgain", [128, 8]); cw = din("cw", [128, 16 * 4]); cb = din("cb", [128, 16])
    pc = din("pc", [128, 3])
    G["c_cpen"] = din("c_cpen", [128, 4, 512], BF16); G["c_wpen"] = din("c_wpen", [128, 8, 512], BF16)
    G["c_cmpp"] = din("c_cmpp", [128, 8, 512], BF16); G["c_Eexp"] = din("c_Eexp", [64, SEQV], BF16)
    G["c_mulm"] = din("c_mulm", [128, 16, 64]); G["c_addm"] = din("c_addm", [128, 16, 64])
    out = nc.dram_tensor("out", [D, NOWN], F32, kind="ExternalOutput").ap()

    G["x1T"] = scr("x1T", [D, SEQV]); G["x2T"] = scr("x2T", [D, NOWN])
    G["kvc"] = scr("kvc", [512, SEQV], BF16); G["ks"] = scr("ks", [256, SEQV], BF16); G["kw"] = scr("kw", [256, SEQV], BF16)
    G["v"] = scr("vv", [SEQV, 2, 256], BF16); G["mqr"] = scr("mqr", [1024, SEQV]); G["mkr"] = scr("mkr", [1024, SEQV])
    G["mv"] = scr("mv", [SEQV, 1024], BF16); G["small"] = scr("small", [SEQV, 128])
    G["qn"] = scr("qn", [1024, NOWN], BF16); G["mo"] = scr("mo", [1024, NOWN]); G["mg"] = scr("mg", [4096, NOWN])
    G["on"] = scr("on", [1024, NOWN], BF16); G["hm"] = scr("hm", [1024, NOWN], BF16); G["hc"] = scr("hc", [1024, NOWN])
    G["qc"] = scr("qc", [1024, NOWN], BF16); G["kc"] = scr("kc", [1024, SEQV], BF16); G["kctm"] = scr("kctm", [SEQV, 1024], BF16)

    cx = Ctx(nc)
    P = cx.P
    cbf_sb = cx.gsb([128, 3 * 128], BF16, "cbf_sb")
    cf_sb = cx.gsb([128, 3 * 128 + 64 + 128], F32, "cf_sb")
    small_sb = {}
    for nm, src, shp in (("g1", g1, [128, KC]), ("g2", g2, [128, KC]), ("gmix", gmix, [128, KC]), ("bi", bi, [128, NT_IN]), ("gk", gk, [128, 4]),
                         ("mgain", mgain, [128, 8]), ("cw", cw, [128, 64]), ("cb", cb, [128, 16]), ("pc", pc, [128, 3])):
        t = cx.gsb(shp, F32, nm + "_sb")
        P.op("sp", lambda h, t=t, src=src: h.dma_start(out=t.ap, in_=src), writes=[t], dma="c0")
        small_sb[nm] = t
    P.op("sp", lambda h: h.dma_start(out=cbf_sb[:, :], in_=cbf[:, :]), writes=[cbf_sb], dma="c0")
    P.op("sp", lambda h: h.dma_start(out=cf_sb[:, :], in_=cf[:, :]), writes=[cf_sb], dma="c0")
    P.barrier()

    class V(T):
        __slots__ = ("parent",)

        def __init__(self, parent, ap):
            self.parent = parent
            self.ap = ap
            self.name = parent.name

        w = property(lambda s: s.parent.w, lambda s, v: setattr(s.parent, "w", v))
        r = property(lambda s: s.parent.r, lambda s, v: setattr(s.parent, "r", v))

    G["ones_bf"] = V(cbf_sb, cbf_sb[:, 0:128]); G["identb"] = V(cbf_sb, cbf_sb[:, 128:256]); G["bd64"] = V(cbf_sb, cbf_sb[:, 256:384])
    G["identf"] = V(cf_sb, cf_sb[:, 0:128]); G["triu"] = V(cf_sb, cf_sb[:, 128:256]); G["ones_f"] = V(cf_sb, cf_sb[:, 256:384])
    G["ov"] = V(cf_sb, cf_sb[:, 448:576].rearrange("p (a b) -> p a b", a=2))
    G["bi_sb"] = small_sb["bi"]; G["gk_sb"] = small_sb["gk"]; G["gmix_sb"] = small_sb["gmix"]; G["mgain_sb"] = small_sb["mgain"]
    G["cw_sb"] = V(small_sb["cw"], small_sb["cw"][:, :].rearrange("p (a b) -> p a b", b=4)); G["cb_sb"] = small_sb["cb"]
    G["vcol"] = V(small_sb["pc"], small_sb["pc"][:, 0:1]); G["validc"] = V(small_sb["pc"], small_sb["pc"][:, 1:3])
    G["KcT"] = cx.gsb([64, 4, 256], BF16, "KcT"); G["Vc"] = cx.gsb([128, 4, 2, 129], BF16, "Vc")

    all_tiles = [i * TS for i in range(SEQV // TS)]
    ffn_stage(cx, xT, G["x1T"], all_tiles, small_sb["g1"], wg1, wu1, wd1, G["ones_bf"], "f1")
    inproj_stage(cx, G)
    cmp_stage(cx, G)
    nsa_stage(cx, G)
    mlstm_stage(cx, G)
    merge_stage(cx, G)
    ffn_stage(cx, G["x2T"], out, [i * TS for i in range(NOWN // TS)], small_sb["g2"], wg2, wu2, wd2, G["ones_bf"], "f2")
    P.emit()
    return nc


def tile_w(w, kc):
    K, N = w.shape
    return np.ascontiguousarray(w.reshape(kc, 128, N // 128, 128).transpose(2, 1, 0, 3).reshape(N // 128, 128, kc * 128))


def colvec(v, n):
    return np.ascontiguousarray(np.asarray(v, np.float32).reshape(n, 128).T)


_cache = {}


def static_consts():
    bf = ml_dtypes.bfloat16
    p = np.arange(128)
    col = np.arange(512)
    ones = np.ones((128, 128), np.float32)
    ident = np.eye(128, dtype=np.float32)
    bd = (p[:, None] // 64 == p[None, :] // 64).astype(np.float32)
    triu = (p[:, None] <= p[None, :]).astype(np.float32)
    c = {}
    c["cbf"] = np.concatenate([ones, ident, bd], axis=1).astype(bf)
    ov = np.zeros((128, 2, 64), np.float32)
    for ct in range(2):
        cc = ct * 128 + p
        n = np.arange(64)
        ov[:, ct, :] = ((16 * cc[:, None] < 64 * n[None, :] + 64) & (16 * cc[:, None] + 32 > 64 * n[None, :])).astype(np.float32)
    c["cf"] = np.concatenate([ident, triu, ones, np.zeros((128, 64), np.float32), ov.reshape(128, 128)], axis=1).astype(np.float32)
    cpen = np.zeros((128, 4, 512), np.float32)
    for rel in range(4):
        cpen[:, rel, :] = np.where(128 * rel + p[:, None] > col[None, :], 0.0, 1.0)
    c["c_cpen"] = cpen.astype(bf)
    wpen = np.zeros((128, 8, 512), np.float32)
    for rel in range(8):
        kp = -512 + 128 * rel + p[:, None]
        ok = (kp <= col[None, :]) & (kp > col[None, :] - 512)
        wpen[:, rel, :] = np.where(ok, 1.0, 0.0)
    c["c_wpen"] = wpen.astype(bf)
    cmpp = np.zeros((128, 8, 512), np.float32)
    for qt in range(4):
        for ct in range(2):
            cc = ct * 128 + p[:, None]
            qv = NOWN + qt * 512 + col[None, :]
            cmpp[:, qt * 2 + ct, :] = np.where(16 * cc + 31 <= qv, 0.0, NEGB)
    c["c_cmpp"] = cmpp.astype(bf)
    kk = np.arange(SEQV)
    c["c_Eexp"] = (kk[None, :] // 64 == np.arange(64)[:, None]).astype(np.float32).astype(bf)
    return c


def percore_consts(hh):
    v = float(hh)
    pc = np.zeros((128, 3), np.float32)
    pc[:, 0] = v
    pc[:, 1] = v
    pc[:, 2] = 1.0
    mulm = np.zeros((128, 16, 64), np.float32)
    addm = np.zeros((128, 16, 64), np.float32)
    p = np.arange(128)
    n = np.arange(64)
    for qs in range(16):
        tv = NOWN + qs * 128 + p
        if hh == 1:
            tr = tv
            nr = n
        else:
            tr = tv - NOWN
            nr = n - 32
        cur = tr // 64
        real = nr[None, :] >= 0
        forced = real & ((nr[None, :] == 0) | (nr[None, :] == cur[:, None]) | (nr[None, :] == cur[:, None] - 1))
        causal = real & (nr[None, :] * 64 <= tr[:, None])
        mulm[:, qs, :] = (causal & ~forced).astype(np.float32)
        addm[:, qs, :] = np.where(forced, 1e4, np.where(causal, 0.0, -1.0))
    return {"pc": pc, "c_mulm": mulm, "c_addm": addm}


def prep_inputs(inp):
    f = lambda k: np.asarray(inp[k], np.float32)[0]
    x = np.asarray(inp["x"], np.float32)
    w_in = f("w_in")
    b_in = f("b_in")
    KVB = 1024
    cols = []
    for kvidx in (0, 1, 2, 4, 3, 5):
        cols.append(np.arange(KVB + kvidx * 256, KVB + (kvidx + 1) * 256))
    MQ = 2608
    cols.append(np.arange(MQ, MQ + 1024)); cols.append(np.arange(MQ + 1024, MQ + 2048)); cols.append(np.arange(MQ + 2048, MQ + 3072))
    small_cols = np.concatenate([np.arange(5680, 5684), np.arange(5684, 5688), np.arange(2560, 2608)])
    cols_all = np.concatenate(cols)
    cols_own = np.concatenate([np.arange(0, 1024), np.arange(5688, 6712), np.arange(6712, 10808)])
    w_small = np.zeros((D, 128), np.float32); w_small[:, :56] = w_in[:, small_cols]
    b_small = np.zeros((128,), np.float32); b_small[:56] = b_in[small_cols]
    w_perm = np.concatenate([w_in[:, cols_all], w_small, w_in[:, cols_own]], axis=1)
    b_perm = np.concatenate([b_in[cols_all], b_small, b_in[cols_own]])
    assert w_perm.shape[1] == NT_IN * 128
    gk = np.stack([np.tile(f("nsa_ks_gain"), 2), np.tile(f("nsa_kw_gain"), 2), np.tile(f("nsa_q_gain"), 2), np.tile(f("nsa_kc_gain"), 2)], axis=1)
    w1 = np.stack([f("cmp_w1_k").reshape(32, 64, 256).transpose(1, 0, 2).reshape(64, 32 * 256),
                   f("cmp_w1_v").reshape(32, 64, 256).transpose(1, 0, 2).reshape(64, 32 * 256)])
    w2 = np.stack([f("cmp_w2_k").reshape(2, 128, 64).transpose(1, 0, 2).reshape(128, 128),
                   f("cmp_w2_v").reshape(2, 128, 64).transpose(1, 0, 2).reshape(128, 128)])
    posT = np.stack([f("cmp_pos_k").T, f("cmp_pos_v").T])
    cwv = f("m_conv_w")
    cw = np.ascontiguousarray(cwv.reshape(4, 16, 128).transpose(2, 1, 0).reshape(128, 64))
    common = {
        "g1": colvec(f("ffn1_norm"), KC), "g2": colvec(f("ffn2_norm"), KC), "gmix": colvec(f("mix_norm"), KC),
        "wg1": tile_w(f("ffn1_w_gate"), KC), "wu1": tile_w(f("ffn1_w_up"), KC), "wd1": tile_w(f("ffn1_w_down"), FC),
        "wg2": tile_w(f("ffn2_w_gate"), KC), "wu2": tile_w(f("ffn2_w_up"), KC), "wd2": tile_w(f("ffn2_w_down"), FC),
        "wi": tile_w(w_perm, KC), "bi": colvec(b_perm, NT_IN),
        "w1": np.ascontiguousarray(w1), "w2": np.ascontiguousarray(w2), "posT": np.ascontiguousarray(posT),
        "wbn": tile_w(f("w_branch_nsa"), 8), "wbm": tile_w(f("w_branch_mlstm"), 8), "wo": tile_w(f("w_out"), KC),
        "gk": np.ascontiguousarray(gk.astype(np.float32)), "mgain": colvec(f("m_out_gain").reshape(-1), 8),
        "cw": cw, "cb": colvec(f("m_conv_b"), 16),
    }
    common.update(static_consts())
    pcs = [percore_consts(0), percore_consts(1)]
    maps = []
    for c in range(8):
        b, hh = c // 2, c % 2
        m = dict(common)
        m.update(pcs[hh])
        m["xT"] = np.ascontiguousarray(np.concatenate([x[b, 0:NOWN].T, x[b, NOWN * hh:NOWN * hh + NOWN].T], axis=1))
        maps.append(m)
    return maps


def kernel(**inp):
    if "nc" not in _cache:
        _cache["nc"] = build()
    nc = _cache["nc"]
    maps = prep_inputs(inp)
    res = run_bass_kernel_spmd(nc, maps, core_ids=list(range(8)))
    outp = np.empty((4, 4096, D), np.float32)
    for c in range(8):
        b, hh = c // 2, c % 2
        outp[b, NOWN * hh:NOWN * hh + NOWN, :] = res.results[c]["out"].T
    return outp
```
